# Optimizing a Trainium2 kernel written in Bass

```python
import jax, jax.numpy as jnp
from jax import lax
import numpy as np

D_MODEL = 1024
BATCH = 8
SEQ = 4096
DEPTH = 2
DEC_BATCH = 32
DEC_SEQ = 32
PAST_LEN = 4096

CHUNK = 64
N_EVEN = (DEPTH + 1) // 2
N_ODD = DEPTH // 2
MIX_WIDTH = D_MODEL
SB_WIDTH = MIX_WIDTH // 2
SB_HEAD_DIM = 64
N_SB_HEADS = SB_WIDTH // SB_HEAD_DIM
SB_SCALE = SB_HEAD_DIM ** -0.5
Q_BLOCK = 128
D_RNN = MIX_WIDTH // 2
N_LRU_BLOCKS = 8
LRU_BLOCK = D_RNN // N_LRU_BLOCKS
CONV_W = 4
LRU_C = 8.0
IN_WIDTH = 3 * SB_WIDTH + 2 * D_RNN
POOL_WINDOWS = (2, 4, 8, 16)
N_POOL_GROUPS = len(POOL_WINDOWS)
POOL_GROUP = D_MODEL // N_POOL_GROUPS
POOL_BUF = max(POOL_WINDOWS) - 1
D_FF = -(-8 * D_MODEL // (3 * 256)) * 256
EPS = 1e-6

kernel_name = "stickbreak_rglru_pool_streaming_step"


def rmsnorm(x, g):
    xf = x.astype(jnp.float32)
    y = xf * lax.rsqrt(jnp.mean(xf * xf, axis=-1, keepdims=True) + EPS)
    return (y * g.astype(jnp.float32)).astype(x.dtype)


def swiglu(x, w_gate, w_up, w_down):
    return (jax.nn.silu(x @ w_gate) * (x @ w_up)) @ w_down


def sb_block(q, k, v, q_pos, k_pos):
    z = jnp.einsum('bqhd,bkhd->bhqk', q, k).astype(jnp.float32) * SB_SCALE
    before = (k_pos[None, :] < q_pos[:, None])[None, None]
    log_not = jnp.where(before, jax.nn.log_sigmoid(-z), 0.0)
    between = lax.cumsum(log_not, axis=3, reverse=True) - log_not
    w = jnp.where(before, jnp.exp(jax.nn.log_sigmoid(z) + between), 0.0)
    return jnp.einsum('bhqk,bkhd->bqhd', w.astype(v.dtype), v)


def sb_prompt(q, k, v):
    b, s = q.shape[0], q.shape[1]
    nb = s // Q_BLOCK
    qb = q.reshape(b, nb, Q_BLOCK, N_SB_HEADS, SB_HEAD_DIM).swapaxes(0, 1)
    pos_b = jnp.arange(s).reshape(nb, Q_BLOCK)
    k_pos = jnp.arange(s)
    out = lax.map(lambda a: sb_block(a[0], k, v, a[1], k_pos), (qb, pos_b))
    return out.swapaxes(0, 1).reshape(b, s, N_SB_HEADS, SB_HEAD_DIM)


def causal_conv(u, buf, w, b):
    t = u.shape[1]
    up = jnp.concatenate([buf.astype(u.dtype), u], axis=1)
    y = b + sum(up[:, i:i + t] * w[i] for i in range(CONV_W))
    return y, up[:, -(CONV_W - 1):]


def rg_lru(u, h0, rg_w, rg_b, ig_w, ig_b, lam):
    b, t, _ = u.shape
    ub = u.reshape(b, t, N_LRU_BLOCKS, LRU_BLOCK)
    r = jax.nn.sigmoid(jnp.einsum('btnc,ncd->btnd', ub, rg_w).reshape(b, t, D_RNN) + rg_b)
    i = jax.nn.sigmoid(jnp.einsum('btnc,ncd->btnd', ub, ig_w).reshape(b, t, D_RNN) + ig_b)
    log_a = (-LRU_C * r * jax.nn.softplus(-lam)).astype(jnp.float32)
    a = jnp.exp(log_a)
    x_in = jnp.sqrt(-jnp.expm1(2.0 * log_a)) * (i * u).astype(jnp.float32)

    def combine(left, right):
        a1, b1 = left
        a2, b2 = right
        return a1 * a2, a2 * b1 + b2

    a_cum, h_zero = lax.associative_scan(combine, (a, x_in), axis=1)
    h = h_zero + a_cum * h0[:, None].astype(jnp.float32)
    return h.astype(u.dtype), h[:, -1].astype(u.dtype)


def hybrid_mixer(xn, past_k, past_v, h0, conv_buf, w_in, conv_w, conv_b, rg_w, rg_b, ig_w, ig_b, lam, w_out):
    b, t, _ = xn.shape
    proj = xn @ w_in
    q, k, v, u, g = jnp.split(proj, [SB_WIDTH, 2 * SB_WIDTH, 3 * SB_WIDTH, 3 * SB_WIDTH + D_RNN], axis=-1)
    q = q.reshape(b, t, N_SB_HEADS, SB_HEAD_DIM)
    k = k.reshape(b, t, N_SB_HEADS, SB_HEAD_DIM)
    v = v.reshape(b, t, N_SB_HEADS, SB_HEAD_DIM)
    if past_k is None:
        attn = sb_prompt(q, k, v)
    else:
        p = past_k.shape[1]
        k_all = jnp.concatenate([past_k.astype(k.dtype), k], axis=1)
        v_all = jnp.concatenate([past_v.astype(v.dtype), v], axis=1)
        attn = sb_block(q, k_all, v_all, p + jnp.arange(t), jnp.arange(p + t))
    uc, conv_new = causal_conv(u, conv_buf, conv_w, conv_b)
    h, h_last = rg_lru(uc, h0, rg_w, rg_b, ig_w, ig_b, lam)
    lru_out = h * jax.nn.gelu(g)
    y = jnp.concatenate([attn.reshape(b, t, SB_WIDTH), lru_out], axis=-1) @ w_out
    return y.astype(xn.dtype), k, v, h_last, conv_new


def pool_mixer(xn, buf, start_pos, pool_w, pool_scale):
    b, t, _ = xn.shape
    xcat = jnp.concatenate([buf.astype(xn.dtype), xn], axis=1)
    xf = xcat.astype(jnp.float32)
    cs = jnp.concatenate([jnp.zeros((b, 1, D_MODEL), jnp.float32), jnp.cumsum(xf, axis=1)], axis=1)
    pos = start_pos + jnp.arange(t)
    outs = []
    for gi, win in enumerate(POOL_WINDOWS):
        sl = slice(gi * POOL_GROUP, (gi + 1) * POOL_GROUP)
        end = cs[:, POOL_BUF + 1:POOL_BUF + 1 + t, sl]
        begin = cs[:, POOL_BUF + 1 - win:POOL_BUF + 1 - win + t, sl]
        cnt = jnp.minimum(win, pos + 1).astype(jnp.float32)[None, :, None]
        outs.append((end - begin) / cnt - xf[:, POOL_BUF:, sl])
    d = jnp.stack(outs, axis=2)
    y = jnp.einsum('btgc,gcd->btgd', d, pool_w.astype(jnp.float32)).reshape(b, t, D_MODEL)
    y = y * pool_scale.astype(jnp.float32)
    return y.astype(xn.dtype), xcat[:, -POOL_BUF:]


def trunk(x, past_k, past_v, h0, conv0, pool0, start_pos,
          hyb_w_in, hyb_conv_w, hyb_conv_b, hyb_rg_w, hyb_rg_b, hyb_ig_w, hyb_ig_b, hyb_lambda, hyb_w_out,
          pool_w, pool_scale, norm_mix, norm_ffn, ffn_gate, ffn_up, ffn_down, norm_final):
    ks, vs, hs, cs, ps = [], [], [], [], []
    for layer in range(DEPTH):
        j = layer // 2
        xn = rmsnorm(x, norm_mix[layer])
        if layer % 2 == 0:
            pk = None if past_k is None else past_k[j]
            pv = None if past_v is None else past_v[j]
            y, k_new, v_new, h_new, c_new = hybrid_mixer(
                xn, pk, pv, h0[j], conv0[j], hyb_w_in[j], hyb_conv_w[j], hyb_conv_b[j],
                hyb_rg_w[j], hyb_rg_b[j], hyb_ig_w[j], hyb_ig_b[j], hyb_lambda[j], hyb_w_out[j])
            ks.append(k_new)
            vs.append(v_new)
            hs.append(h_new)
            cs.append(c_new)
        else:
            y, p_new = pool_mixer(xn, pool0[j], start_pos, pool_w[j], pool_scale[j])
            ps.append(p_new)
        x = x + y
        x = x + swiglu(rmsnorm(x, norm_ffn[layer]), ffn_gate[layer], ffn_up[layer], ffn_down[layer]).astype(x.dtype)
    return (rmsnorm(x, norm_final), jnp.stack(ks), jnp.stack(vs), jnp.stack(hs), jnp.stack(cs), jnp.stack(ps))


def setup_inputs(seed: int = 0) -> dict:
    key = jax.random.key(seed)
    ks = jax.random.split(key, 32)

    def nrm(k, shape, scale=1.0):
        return jax.random.normal(k, shape, jnp.float32) * scale

    a0 = jax.random.uniform(ks[14], (N_EVEN, D_RNN), jnp.float32, minval=0.9, maxval=0.999) ** (1.0 / LRU_C)
    return {
        "x_prompt": nrm(ks[0], (BATCH, SEQ, D_MODEL)),
        "x_sample": nrm(ks[1], (DEC_BATCH, DEC_SEQ, D_MODEL)),
        "cache_sb_k": nrm(ks[2], (N_EVEN, DEC_BATCH, PAST_LEN, N_SB_HEADS, SB_HEAD_DIM)),
        "cache_sb_v": nrm(ks[3], (N_EVEN, DEC_BATCH, PAST_LEN, N_SB_HEADS, SB_HEAD_DIM)),
        "state_lru_h": nrm(ks[4], (N_EVEN, DEC_BATCH, D_RNN), 0.5),
        "state_lru_conv": nrm(ks[5], (N_EVEN, DEC_BATCH, CONV_W - 1, D_RNN)),
        "state_pool": nrm(ks[6], (N_ODD, DEC_BATCH, POOL_BUF, D_MODEL)),
        "hyb_w_in": nrm(ks[7], (N_EVEN, D_MODEL, IN_WIDTH), D_MODEL ** -0.5),
        "hyb_conv_w": nrm(ks[8], (N_EVEN, CONV_W, D_RNN), CONV_W ** -0.5),
        "hyb_conv_b": nrm(ks[9], (N_EVEN, D_RNN), 0.01),
        "hyb_rg_w": nrm(ks[10], (N_EVEN, N_LRU_BLOCKS, LRU_BLOCK, LRU_BLOCK), LRU_BLOCK ** -0.5),
        "hyb_rg_b": nrm(ks[11], (N_EVEN, D_RNN), 0.01),
        "hyb_ig_w": nrm(ks[12], (N_EVEN, N_LRU_BLOCKS, LRU_BLOCK, LRU_BLOCK), LRU_BLOCK ** -0.5),
        "hyb_ig_b": nrm(ks[13], (N_EVEN, D_RNN), 0.01),
        "hyb_lambda": jnp.log(a0) - jnp.log1p(-a0),
        "hyb_w_out": nrm(ks[15], (N_EVEN, MIX_WIDTH, D_MODEL), MIX_WIDTH ** -0.5),
        "pool_w": nrm(ks[16], (N_ODD, N_POOL_GROUPS, POOL_GROUP, POOL_GROUP), POOL_GROUP ** -0.5),
        "pool_scale": 1.0 + nrm(ks[17], (N_ODD, D_MODEL), 0.1),
        "norm_mix": 1.0 + nrm(ks[18], (DEPTH, D_MODEL), 0.1),
        "norm_ffn": 1.0 + nrm(ks[19], (DEPTH, D_MODEL), 0.1),
        "ffn_gate": nrm(ks[20], (DEPTH, D_MODEL, D_FF), D_MODEL ** -0.5),
        "ffn_up": nrm(ks[21], (DEPTH, D_MODEL, D_FF), D_MODEL ** -0.5),
        "ffn_down": nrm(ks[22], (DEPTH, D_FF, D_MODEL), D_FF ** -0.5),
        "norm_final": 1.0 + nrm(ks[23], (D_MODEL,), 0.1),
    }


def reference(x_prompt, x_sample, cache_sb_k, cache_sb_v, state_lru_h, state_lru_conv, state_pool,
              hyb_w_in, hyb_conv_w, hyb_conv_b, hyb_rg_w, hyb_rg_b, hyb_ig_w, hyb_ig_b, hyb_lambda, hyb_w_out,
              pool_w, pool_scale, norm_mix, norm_ffn, ffn_gate, ffn_up, ffn_down, norm_final):
    b = x_prompt.shape[0]
    dt = x_prompt.dtype
    h0_p = jnp.zeros((N_EVEN, b, D_RNN), dt)
    conv0_p = jnp.zeros((N_EVEN, b, CONV_W - 1, D_RNN), dt)
    pool0_p = jnp.zeros((N_ODD, b, POOL_BUF, D_MODEL), dt)
    y_prompt, k_p, v_p, h_p, conv_p, pool_p = trunk(
        x_prompt, None, None, h0_p, conv0_p, pool0_p, 0,
        hyb_w_in, hyb_conv_w, hyb_conv_b, hyb_rg_w, hyb_rg_b, hyb_ig_w, hyb_ig_b, hyb_lambda, hyb_w_out,
        pool_w, pool_scale, norm_mix, norm_ffn, ffn_gate, ffn_up, ffn_down, norm_final)
    y_sample, k_s, v_s, h_s, conv_s, pool_s = trunk(
        x_sample, cache_sb_k, cache_sb_v, state_lru_h, state_lru_conv, state_pool, cache_sb_k.shape[2],
        hyb_w_in, hyb_conv_w, hyb_conv_b, hyb_rg_w, hyb_rg_b, hyb_ig_w, hyb_ig_b, hyb_lambda, hyb_w_out,
        pool_w, pool_scale, norm_mix, norm_ffn, ffn_gate, ffn_up, ffn_down, norm_final)
    return (y_prompt, y_sample, k_p, v_p, h_p, conv_p, pool_p, k_s, v_s, h_s, conv_s, pool_s)
```

```python
import numpy as np
from contextlib import ExitStack
import concourse.bass as bass
import concourse.mybir as mybir
from concourse.bass_utils import run_bass_kernel_spmd

F32 = mybir.dt.float32
BF16 = mybir.dt.bfloat16
AF = mybir.ActivationFunctionType
ALU = mybir.AluOpType

D = 1024
DFF = 2816
NFC = 22
POOL_WINDOWS = (2, 4, 8, 16)
EPS = 1e-6
EPOCH = 30000
DEBUG = False
STOP = None


class StopBuild(Exception):
    pass


def ckpt(k):
    if STOP is not None and STOP == k:
        raise StopBuild()
WSLOT = 4096

def chunk_list():
    L = []
    for s in (3, 4, 0, 1, 2):
        L.append(("KN", s, 0, 4096))
    for s in range(2):
        L.append(("KO", s, 0, 4096))

    def ffn(layer):
        for half in range(2):
            f0 = half * 11
            for i in range(5):
                L.append(("GU", layer, f0 + 2 * i, 4096))
            L.append(("GU1", layer, f0 + 10, 2048))
            for q in range(4):
                L.append(("DN", layer, half * 4 + q, 2816))
    ffn(0)
    L.append(("PW", 0, 0, 2048))
    ffn(1)
    return L


CHUNKS = chunk_list()
NCH = len(CHUNKS)


def host_chunks(w_in, w_out, gate, up, down, pool_w):
    out = np.zeros((NCH, 128, WSLOT), np.float32)
    for i, (k, a, b, nel) in enumerate(CHUNKS):
        if k == "KN":
            out[i] = w_in[:, a * 512:(a + 1) * 512].reshape(8, 128, 512).transpose(1, 0, 2).reshape(128, 4096)
        elif k == "KO":
            out[i] = w_out[:, a * 512:(a + 1) * 512].reshape(8, 128, 512).transpose(1, 0, 2).reshape(128, 4096)
        elif k == "GU":
            g = gate[a][:, b * 128:(b + 2) * 128].reshape(8, 128, 256)
            u = up[a][:, b * 128:(b + 2) * 128].reshape(8, 128, 256)
            out[i] = np.concatenate([g, u], axis=2).transpose(1, 0, 2).reshape(128, 4096)
        elif k == "GU1":
            g = gate[a][:, b * 128:(b + 1) * 128].reshape(8, 128, 128)
            u = up[a][:, b * 128:(b + 1) * 128].reshape(8, 128, 128)
            out[i, :, :2048] = np.concatenate([g, u], axis=2).transpose(1, 0, 2).reshape(128, 2048)
        elif k == "DN":
            half, q = b // 4, b % 4
            blk = down[a][half * 1408:(half + 1) * 1408, q * 256:(q + 1) * 256]
            out[i, :, :2816] = blk.reshape(11, 128, 256).transpose(1, 0, 2).reshape(128, 2816)
        elif k == "PW":
            out[i, :, :2048] = pool_w.reshape(4, 2, 128, 256).transpose(2, 0, 1, 3).reshape(128, 2048)
    return out


V_NM = (0, 8)
V_NF = (16, 24)
V_NFIN = 32
V_PS = 40
V_CW = 48
V_CB = 64
V_RGB = 68
V_IGB = 72
V_LAM = 76
NV = 80
C_ID = 0
C_TRI = 128
C_MASK = 384
C_ONES = 512
C_INV = 640
C_M32 = 704
C_MASK2 = 960
NCST = 1216


def host_consts():
    c = np.zeros((128, NCST), np.float32)
    c[:, C_ID:C_ID + 128] = np.eye(128, dtype=np.float32)
    j = np.arange(128)[:, None]
    k = np.arange(128)[None, :]
    c[:, C_TRI:C_TRI + 128] = -1.0 * (j >= k)
    c[:, C_TRI + 128:C_TRI + 256] = -1.0 * (j < k)
    c[:, C_MASK:C_MASK + 128] = (j < k)
    c[:, C_ONES:C_ONES + 128] = 1.0 / D
    c[:, C_MASK2:C_MASK2 + 128] = (j < k)
    c[:, C_MASK2 + 128:C_MASK2 + 256] = (j < k)
    for g, w in enumerate(POOL_WINDOWS):
        c[:, C_INV + 16 * g:C_INV + 16 * (g + 1)] = 1.0 / np.minimum(w, np.arange(16) + 1.0)
    m32 = (np.arange(32)[:, None] < np.arange(32)[None, :]).astype(np.float32)
    c[:32, C_M32:C_M32 + 256] = np.tile(m32, (1, 8))
    return c


def fm(v):
    return np.ascontiguousarray(v.reshape(-1, 128).T)


class Buf:
    __slots__ = ("t", "writer", "readers", "name")

    def __init__(self, t, name=""):
        self.t = t
        self.writer = None
        self.readers = {}
        self.name = name

    def __getitem__(self, idx):
        return self.t[idx]


class Sched:
    def __init__(self, nc, es, n_epochs=6, n_dma_sems=12):
        self.nc = nc
        self.eng = {"pe": nc.tensor, "act": nc.scalar, "dve": nc.vector, "pool": nc.gpsimd, "sp": nc.sync}
        self.sems = {}
        self.count = {}
        self.waited = {k: {} for k in self.eng}
        for k in self.eng:
            self.count[k] = 0
            if k == "sp":
                continue
            self.sems[k] = [es.enter_context(nc.semaphore(name=f"s_{k}{i}")) for i in range(n_epochs)]
        self.dsems = {}
        self.dcount = {}
        self.dnext = {}
        for q in ("sp", "act"):
            self.dsems[q] = [es.enter_context(nc.semaphore(name=f"d_{q}{i}")) for i in range(n_dma_sems)]
            self.dcount[q] = [0] * n_dma_sems
            self.dnext[q] = 0
        self.semobj = {}
        for k, lst in self.sems.items():
            for i, s in enumerate(lst):
                self.semobj[(k, i)] = s
        for q, lst in self.dsems.items():
            for i, s in enumerate(lst):
                self.semobj[("d" + q, i)] = s
        self.nwaits = 0

    def _wait(self, engname, key, val):
        w = self.waited[engname]
        if w.get(key, 0) >= val:
            return
        self.eng[engname].wait_ge(self.semobj[key], val)
        w[key] = val
        self.nwaits += 1

    def _deps(self, engname, reads, writes):
        need = {}
        for b in reads:
            if b.writer is not None:
                k, v = b.writer
                if need.get(k, 0) < v:
                    need[k] = v
        for b in writes:
            if b.writer is not None:
                k, v = b.writer
                if need.get(k, 0) < v:
                    need[k] = v
            for k, v in b.readers.items():
                if need.get(k, 0) < v:
                    need[k] = v
        for k, v in need.items():
            if engname == "pe" and k[0] == "pe":
                continue
            self._wait(engname, k, v)

    def _mark(self, tok, reads, writes):
        k, v = tok
        for b in reads:
            if b.readers.get(k, 0) < v:
                b.readers[k] = v
        for b in writes:
            b.writer = tok
            b.readers = {}

    def op(self, engname, fn, reads=(), writes=()):
        self._deps(engname, reads, writes)
        inst = fn(self.eng[engname])
        c = self.count[engname]
        ep, v = c // EPOCH, c % EPOCH + 1
        inst.then_inc(self.sems[engname][ep], 1)
        self.count[engname] = c + 1
        tok = ((engname, ep), v)
        self._mark(tok, reads, writes)
        return tok

    def dma(self, q, out, in_, reads=(), writes=(), **kw):
        self._deps(q, reads, writes)
        i = self.dnext[q]
        self.dnext[q] = (i + 1) % len(self.dsems[q])
        key = ("d" + q, i)
        prev = self.dcount[q][i]
        if prev > 0:
            self._wait(q, key, prev)
        inst = self.eng[q].dma_start(out=out, in_=in_, **kw)
        val = prev + 16
        inst.then_inc(self.dsems[q][i], 16)
        self.dcount[q][i] = val
        tok = (key, val)
        self._mark(tok, reads, writes)
        return tok

    def barrier(self):
        toks = {}
        for k in ("pe", "act", "dve", "pool"):
            c = self.count[k]
            if c > 0:
                toks[(k, (c - 1) // EPOCH)] = (c - 1) % EPOCH + 1
        for q in ("sp", "act"):
            for i, v in enumerate(self.dcount[q]):
                if v > 0:
                    toks[("d" + q, i)] = v
        for e in ("pe", "act", "dve", "pool", "sp"):
            for key, v in toks.items():
                if key[0] == e:
                    continue
                self._wait(e, key, v)

    def finish(self, bufs):
        for b in bufs:
            if b.writer is not None:
                self._wait("sp", b.writer[0], b.writer[1])


def build(SEQ, NS, LS, PAST):
    T = 512
    NT = SEQ // T
    TS = NS * LS
    NPB = PAST // 128
    nc = bass.Bass("TRN2", target_bir_lowering=False)

    def din(name, shape, dt=F32):
        return nc.dram_tensor(name, list(shape), dt, kind="ExternalInput").ap()

    def dout(name, shape):
        return nc.dram_tensor(name, list(shape), F32, kind="ExternalOutput").ap()

    xp_d = din("xp", (SEQ, D))
    xs_d = din("xs", (TS, D))
    ck_d = din("ck", (NS, PAST, 512))
    cv_d = din("cv", (NS, PAST, 512))
    h0_d = din("h0", (128, 4, NS))
    conv0_d = din("conv0", (128, 4, NS, 3))
    pool0_d = din("pool0", (128, 8, NS, 15))
    wsrc_d = din("wsrc", (NCH, 128, WSLOT))
    vecs_d = din("vecs", (128, NV))
    gatew_d = din("gatew", (128, 8 * 128))
    cst_d = din("cst", (128, NCST))
    yp_d = dout("yp", (SEQ, D))
    ys_d = dout("ys", (TS, D))
    kp_d = dout("kp", (SEQ, 512))
    vp_d = dout("vp", (SEQ, 512))
    hp_d = dout("hp", (1, 512))
    convp_d = dout("convp", (3, 512))
    poolp_d = dout("poolp", (15, D))
    ks_d = dout("ks", (TS, 512))
    vs_d = dout("vs", (TS, 512))
    hs_d = dout("hs", (NS, 512))
    convs_d = dout("convs", (NS * 3, 512))
    pools_d = dout("pools", (NS * 15, D))
    wscr_d = nc.dram_tensor("wscr", [NCH, 128, WSLOT], BF16, kind="Internal").ap()

    es = ExitStack()
    with es:
        S = Sched(nc, es)
        cnt = [0]
        dbg_list = []

        def dbg(name, buf, ap, shape, dt=F32):
            if not DEBUG:
                return
            d = nc.dram_tensor('dbg_' + name, list(shape), dt, kind='ExternalOutput').ap()
            S.dma('sp', d, ap, reads=[buf])
            dbg_list.append(name)

        def sb(shape, dt, name=None):
            cnt[0] += 1
            nm = "sb_" + (name or f"t{cnt[0]}")
            return Buf(es.enter_context(nc.sbuf_tensor(nm, list(shape), dt)), nm)

        P2 = [es.enter_context(nc.psum_tensor(f"pbank{i}", [128, 1024], F32)) for i in range(4)]
        banks = [Buf(P2[i // 2][:, (i % 2) * 512:(i % 2 + 1) * 512], f"bank{i}") for i in range(8)]
        bank_rr = [0]

        def bank():
            b = banks[bank_rr[0] % 8]
            bank_rr[0] += 1
            return b

        KT = [[sb([128, T], BF16) for _ in range(NT)] for _ in range(4)]
        VV = [sb([128, 4, 512], BF16) for _ in range(max(NT, 8))]
        XT = sb([128, 8, T], F32, "XT")
        xstage = [sb([128, D], F32) for _ in range(2)]
        XN = sb([128, 8, T], BF16, "XN")
        XNB = [Buf(XN.t[:, c_, :], f"xn{c_}") for c_ in range(8)]
        HH = sb([128, 12 * T], BF16, "HH")
        hh3 = HH.t[:].rearrange("p (f t) -> p f t", t=T)

        def QT(j, cols):
            return HH.t[:, j * T + cols.start:j * T + cols.stop]

        def MIX(c, cols):
            return HH.t[:, (4 + c) * T + cols.start:(4 + c) * T + cols.stop]

        UE = [sb([128, 3 + T], F32) for _ in range(4)]
        PH = [sb([128, 15], F32) for _ in range(8)]
        HC = sb([128, 4], F32, "HC")
        HCs = sb([128, 4, NS], F32, "HCs")
        NTMP = 12
        TB = [es.enter_context(nc.sbuf_tensor(f"sb_tb{k_}", [128, 1088], F32)) for k_ in range(NTMP // 2)]
        tmps = [Buf(TB[k_ // 2][:, (k_ % 2) * 544:(k_ % 2 + 1) * 544], f"tmp{k_}") for k_ in range(NTMP)]

        def tmp2():
            if tmp_rr[0] % 2 == 1:
                tmp_rr[0] += 1
            k_ = (tmp_rr[0] % NTMP) // 2
            a_, b_ = tmps[2 * k_], tmps[2 * k_ + 1]
            tmp_rr[0] += 2
            return a_, b_, TB[k_][:, :].rearrange("p (s l) -> p s l", s=2)[:, :, 0:T]
        tmp_rr = [0]

        def tmp():
            b = tmps[tmp_rr[0] % NTMP]
            tmp_rr[0] += 1
            return b

        LB = [sb([128, 2, T], BF16) for _ in range(3)]
        lbuf = [Buf(LB[k_].t[:, 0, :], f"lb{k_}") for k_ in range(3)]
        WB = [sb([128, 2, T], BF16) for _ in range(2)]
        wbuf = [Buf(WB[k_ % 2].t[:, k_ // 2, :], f"wb{k_}") for k_ in range(3)]
        ucb = [sb([128, T], BF16) for _ in range(2)]
        dB = ucb
        kvst = [sb([128, 512], F32) for _ in range(2)]
        wring = [sb([128, WSLOT], BF16) for _ in range(3)]
        wstg = [sb([128, 1024], F32) for _ in range(2)]
        KTb = [Buf(VV[5].t[:, k_, :].rearrange("p (j t) -> p j t", j=4), f"ktb{k_}") for k_ in range(2)]
        Vb = [Buf(VV[6].t[:, k_, :], f"vb{k_}") for k_ in range(3)]
        Vnew = [Buf(VV[s_].t[:, 0, :], f"vnew{s_}") for s_ in range(NS)]
        QTZ = Buf(VV[4].t[:, 0:2, :].rearrange("p a n -> p (a n)")[:, 0:8 * TS].rearrange("p (j i t) -> p j i t", j=4, i=2), "qtz")
        UEs = [Buf(VV[c_].t[:, 1, :].bitcast(F32)[:, 0:NS * (3 + LS)].rearrange("p (s l) -> p s l", s=NS), f"ues{c_}") for c_ in range(4)]
        PHs = [Buf(VV[c_].t[:, 3, :].bitcast(F32)[:, 128:128 + NS * 15].rearrange("p (s l) -> p s l", s=NS), f"phs{c_}") for c_ in range(8)]
        Kbf = [Buf(VV[4 + k_].t[:, 2, :], f"kbf{k_}") for k_ in range(2)]
        KTs = [Buf(VV[7].t[:, j_, 0:TS], f"kts{j_}") for j_ in range(4)]
        cst = sb([128, NCST], F32, "cst")
        vecs = sb([128, NV], F32, "vecs")
        cvec = sb([128, 16], F32, "cvec")
        gwf = wstg[0]
        gwb = sb([128, 1024], BF16, "gwb")
        trib = sb([128, 256], BF16, "trib")
        onesb = sb([128, 128], BF16, "onesb")
        identb = sb([128, 128], BF16, "identb")
        sq = [sb([128, T], BF16) for _ in range(2)]
        rstd = sb([128, T], F32, "rstd")
        epsv = sb([128, 1], F32, "epsv")
        ost = xstage[0]
        WS = [Buf(None, f"ws{i}") for i in range(NCH)]

        ident = cst.t[:, C_ID:C_ID + 128]
        mask = cst.t[:, C_MASK:C_MASK + 128]
        m32 = cst.t[0:32, C_M32:C_M32 + 256]
        mask2 = cst.t[:, C_MASK2:C_MASK2 + 256].rearrange("p (s k) -> p s k", s=2)

        S.op("pool", lambda e: e.memset(epsv[:], EPS), writes=[epsv])
        S.dma("sp", cst[:], cst_d, writes=[cst])
        S.dma("sp", vecs[:], vecs_d, writes=[vecs])
        S.dma("sp", gwf[:], gatew_d, writes=[gwf])
        S.op("dve", lambda e: e.tensor_copy(out=trib[:], in_=cst.t[:, C_TRI:C_TRI + 256]), reads=[cst], writes=[trib])
        S.op("dve", lambda e: e.tensor_copy(out=onesb[:], in_=cst.t[:, C_ONES:C_ONES + 128]), reads=[cst], writes=[onesb])
        S.op("dve", lambda e: e.tensor_copy(out=identb[:], in_=cst.t[:, C_ID:C_ID + 128]), reads=[cst], writes=[identb])
        S.op("pool", lambda e: e.tensor_copy(out=gwb[:], in_=gwf[:]), reads=[gwf], writes=[gwb])
        S.op("act", lambda e: e.activation(out=cvec.t[:, 8:12], in_=vecs.t[:, V_LAM:V_LAM + 4], func=AF.Exp, scale=-1.0), reads=[vecs], writes=[cvec])
        S.op("act", lambda e: e.activation(out=cvec.t[:, 12:16], in_=cvec.t[:, 8:12], func=AF.Ln, bias=1.0), reads=[cvec], writes=[cvec])
        S.op("dve", lambda e: e.tensor_scalar(out=cvec.t[:, 0:4], in0=cvec.t[:, 12:16], scalar1=-8.0, scalar2=0.0, op0=ALU.mult, op1=ALU.add), reads=[cvec], writes=[cvec])
        S.op("dve", lambda e: e.tensor_scalar(out=cvec.t[:, 4:8], in0=cvec.t[:, 12:16], scalar1=-16.0, scalar2=0.0, op0=ALU.mult, op1=ALU.add), reads=[cvec], writes=[cvec])
        for c in range(4):
            S.op("pool", lambda e, c=c: e.memset(UE[c].t[:, 0:3], 0.0), writes=[UE[c]])
        S.op("pool", lambda e: e.memset(HC[:], 0.0), writes=[HC])
        S.dma("sp", HCs[:], h0_d, writes=[HCs])
        for c in range(8):
            S.op("pool", lambda e, c=c: e.memset(PH[c][:], 0.0), writes=[PH[c]])

        order = []
        for ti in range(NT + 1):
            for ci in range(NCH):
                order.append((ti, ci))
        wstate = {"emitted": 0, "next": 0, "stg": 0, "cast": 0}
        LOOK = 2

        stg_pool = list(wstg) + [Buf(VV[k_].t[:, :, :].rearrange("p a n -> p (a n)").bitcast(F32), f"vstg{k_}") for k_ in range(1, len(VV))]
        NSTG = (len(stg_pool) // 4) * 4
        LD = NSTG // 4 - 1
        wload = {"emitted": 0, "map": {}}

        def w_load(i):
            ti, ci = order[i]
            if ti != 0:
                return
            nel = CHUNKS[ci][3]
            q = nel // 4
            lst = []
            for k in range(4):
                st = stg_pool[wstate["stg"] % NSTG]
                wstate["stg"] += 1
                S.dma("sp", st.t[:, 0:q], wsrc_d[ci, :, k * q:(k + 1) * q], writes=[st])
                lst.append(st)
            wload["map"][i] = lst

        def w_emit(i):
            ti, ci = order[i]
            buf = wring[i % 3]
            nel = CHUNKS[ci][3]
            if ti == 0:
                while wload["emitted"] <= min(i + LD, NCH - 1):
                    w_load(wload["emitted"])
                    wload["emitted"] += 1
                q = nel // 4
                eng = "act" if (wstate["cast"] % 2 == 1) else "dve"
                wstate["cast"] += 1
                for k, st in enumerate(wload["map"].pop(i)):
                    if eng == "act":
                        S.op("act", lambda e, k=k, st=st: e.activation(out=buf.t[:, k * q:(k + 1) * q], in_=st.t[:, 0:q], func=AF.Copy), reads=[st], writes=[buf])
                    else:
                        S.op("dve", lambda e, k=k, st=st: e.tensor_copy(out=buf.t[:, k * q:(k + 1) * q], in_=st.t[:, 0:q]), reads=[st], writes=[buf])
                S.dma("sp", wscr_d[ci, :, 0:nel], buf.t[:, 0:nel], reads=[buf], writes=[WS[ci]])
            else:
                S.dma("sp", buf.t[:, 0:nel], wscr_d[ci, :, 0:nel], reads=[WS[ci]], writes=[buf])

        def wnext(kind):
            i = wstate["next"]
            wstate["next"] += 1
            assert CHUNKS[order[i][1]][0] == kind, (CHUNKS[order[i][1]], kind)
            while wstate["emitted"] <= min(i + LOOK, len(order) - 1):
                w_emit(wstate["emitted"])
                wstate["emitted"] += 1
            return wring[i % 3]

        def mm(out, lhsT, rhs, start, stop, R, W):
            S.op("pe", lambda e: e.matmul(out, lhsT=lhsT, rhs=rhs, start=start, stop=stop, skip_group_check=True), reads=R, writes=W)

        def norm_stats(TW):
            bk = bank()
            for c in range(8):
                s = sq[c % 2]
                if c % 2 == 0:
                    S.op("act", lambda e, c=c, s=s: e.activation(out=s.t[:, :TW], in_=XT.t[:, c, :TW], func=AF.Square), reads=[XT], writes=[s])
                else:
                    S.op("dve", lambda e, c=c, s=s: e.tensor_tensor(out=s.t[:, :TW], in0=XT.t[:, c, :TW], in1=XT.t[:, c, :TW], op=ALU.mult), reads=[XT], writes=[s])
                mm(bk.t[:, :TW], onesb[:], s.t[:, :TW], c == 0, c == 7, [onesb, s], [bk])
            t1 = tmp()
            S.op("act", lambda e: e.activation(out=t1.t[:, :TW], in_=bk.t[:, :TW], func=AF.Ln, bias=epsv.t[:, 0:1]), reads=[bk, epsv], writes=[t1])
            S.op("act", lambda e: e.activation(out=rstd.t[:, :TW], in_=t1.t[:, :TW], func=AF.Exp, scale=-0.5), reads=[t1], writes=[rstd])

        def norm_to_xn(TW, gcol):
            norm_stats(TW)
            for c in range(8):
                norm_apply(c, gcol, TW, XN.t[:, c, :TW], XNB[c])

        def norm_apply(c, gcol, TW, out_ap, out_buf, in_ap=None, rs_ap=None):
            in_ap = XT.t[:, c, :TW] if in_ap is None else in_ap
            rs_ap = rstd.t[:, :TW] if rs_ap is None else rs_ap
            gsc = vecs.t[:, gcol + c:gcol + c + 1]
            if c % 4 != 3:
                S.op("dve", lambda e: e.scalar_tensor_tensor(out=out_ap, in0=in_ap, scalar=gsc, in1=rs_ap, op0=ALU.mult, op1=ALU.mult), reads=[XT, vecs, rstd], writes=[out_buf])
            else:
                t_ = tmp()
                tv = t_.t[:, :TW] if len(in_ap.shape) == 2 else t_.t[:, :TW].rearrange("p (s l) -> p s l", s=in_ap.shape[1])
                S.op("act", lambda e: e.activation(out=tv, in_=in_ap, func=AF.Copy, scale=gsc), reads=[XT, vecs], writes=[t_])
                S.op("pool", lambda e: e.tensor_tensor(out=out_ap, in0=tv, in1=rs_ap, op=ALU.mult), reads=[t_, rstd], writes=[out_buf])

        def proj_fm(wb, j, TW, rhs_fn, R):
            bk = bank()
            w3 = wb.t[:].rearrange("p (c n) -> p c n", c=8)
            for c in range(8):
                mm(bk.t[:, :TW], w3[:, c, j * 128:(j + 1) * 128], rhs_fn(c), c == 0, c == 7, [wb] + (R(c) if callable(R) else R), [bk])
            return bk

        xpref = set()

        def process_tile(ti):
            sample = ti == NT
            TW = TS if sample else T
            nseg = NS if sample else 1
            L = LS if sample else T
            first = ti == 0
            t0 = 0 if sample else ti * T
            x_d = xs_d if sample else xp_d
            ntb = TW // 128
            cols = slice(0, TW)

            if sample:
                S.barrier()
                for s_ in range(NS):
                    S.op("pool", lambda e, s_=s_: e.memset(Vnew[s_].t[:, :], 0.0), writes=[Vnew[s_]])
                S.op("pool", lambda e: e.memset(QTZ.t[:, :, :, :], 0.0), writes=[QTZ])
                for c_ in range(4):
                    S.dma("sp", UEs[c_].t[:, :, 0:3], conv0_d[:, c_], writes=[UEs[c_]])
                for c_ in range(8):
                    S.dma("sp", PHs[c_].t[:, :, :], pool0_d[:, c_], writes=[PHs[c_]])
            for tb in range(ntb):
                xs_ = xstage[tb % 2]
                if (ti, tb) not in xpref:
                    S.dma("sp", xs_[:], x_d[t0 + tb * 128:t0 + (tb + 1) * 128, :], writes=[xs_])
                for half in range(2):
                    bk = bank()
                    for jj in range(4):
                        c = half * 4 + jj
                        S.op("pe", lambda e, c=c, jj=jj, bk=bk, xs_=xs_: e.transpose(out=bk.t[:, jj * 128:(jj + 1) * 128], in_=xs_.t[:, c * 128:(c + 1) * 128], identity=ident), reads=[xs_, cst], writes=[bk])
                    eng = "act" if half == 0 else "dve"
                    S.op(eng, lambda e, half=half, bk=bk, tb=tb: (e.activation(out=XT.t[:, half * 4:half * 4 + 4, tb * 128:(tb + 1) * 128], in_=bk.t[:].rearrange("p (j t) -> p j t", j=4), func=AF.Copy) if half == 0 else
                                                                   e.tensor_copy(out=XT.t[:, half * 4:half * 4 + 4, tb * 128:(tb + 1) * 128], in_=bk.t[:].rearrange("p (j t) -> p j t", j=4))), reads=[bk], writes=[XT])

            ckpt(100 * ti + 1)
            norm_to_xn(TW, V_NM[0])
            xn_rhs = lambda c: XN.t[:, c, :TW]
            if ti == 0:
                dbg('xt0', XT, XT.t[:, :, :], [128, 8, T])
                pass
                dbg('rstd0', rstd, rstd.t[:, :], [128, T])

            ckpt(100 * ti + 2)
            if sample:
                tblocks = [(s_ * LS, LS, s_ * LS) for s_ in range(NS)]
            else:
                tblocks = [(tb * 128, 128, t0 + tb * 128) for tb in range(ntb)]
            k_out = ks_d if sample else kp_d
            v_out = vs_d if sample else vp_d
            hc = HCs if sample else HC
            gw3 = gwb.t[:].rearrange("p (k n) -> p k n", k=8)

            def v3(ap2):
                return ap2.rearrange("p (s l) -> p s l", s=nseg)

            def ue_of(c):
                ue = UEs[c] if sample else UE[c]
                return ue, (ue.t[:] if sample else ue.t[:].rearrange("p (s l) -> p s l", s=1))

            wu = wnext("KN")
            for c in range(4):
                ue, ue3 = ue_of(c)
                bku = proj_fm(wu, c, TW, xn_rhs, lambda cc: [XNB[cc]])
                S.op("act", lambda e, bku=bku, ue3=ue3: e.activation(out=ue3[:, :, 3:3 + L], in_=v3(bku.t[:, :TW]), func=AF.Copy), reads=[bku], writes=[ue])
            wg = wnext("KN")
            gts, g2s = [], []
            for c in range(4):
                bkg = proj_fm(wg, c, TW, xn_rhs, lambda cc: [XNB[cc]])
                gt = tmp()
                S.op("act", lambda e, bkg=bkg, gt=gt: e.activation(out=gt.t[:, :TW], in_=bkg.t[:, :TW], func=AF.Copy), reads=[bkg], writes=[gt])
                gts.append(gt)
            for c in range(4):
                gt = gts[c]
                g2 = tmp()
                g2s.append(g2)
                S.op("pool", lambda e, gt=gt, g2=g2: e.tensor_tensor(out=g2.t[:, :TW], in0=gt.t[:, :TW], in1=gt.t[:, :TW], op=ALU.mult), reads=[gt], writes=[g2])
                S.op("dve", lambda e, g2=g2: e.tensor_scalar(out=g2.t[:, :TW], in0=g2.t[:, :TW], scalar1=0.044715, scalar2=1.0, op0=ALU.mult, op1=ALU.add), reads=[g2], writes=[g2])
                S.op("pool", lambda e, gt=gt, g2=g2: e.tensor_tensor(out=g2.t[:, :TW], in0=g2.t[:, :TW], in1=gt.t[:, :TW], op=ALU.mult), reads=[g2, gt], writes=[g2])
            for c in range(4):
                g2 = g2s[c]
                S.op("act", lambda e, g2=g2: e.activation(out=g2.t[:, :TW], in_=g2.t[:, :TW], func=AF.Sigmoid, scale=1.5957691216057308), reads=[g2], writes=[g2])
            for c in range(4):
                gt, g2 = gts[c], g2s[c]
                S.op("dve", lambda e, gt=gt, g2=g2, c=c: e.tensor_tensor(out=ATT(c, cols), in0=gt.t[:, :TW], in1=g2.t[:, :TW], op=ALU.mult), reads=[gt, g2], writes=[HH])

            def piece_q():
                wq = wnext("KN")
                for j in range(4):
                    bk = proj_fm(wq, j, TW, xn_rhs, lambda c: [XNB[c]])
                    if sample:
                        S.op("act", lambda e, j=j, bk=bk: e.activation(out=QTZ.t[0:64, j, 0, :], in_=bk.t[0:64, :TW], func=AF.Copy, scale=0.125), reads=[bk], writes=[QTZ])
                        S.op("dve", lambda e, j=j, bk=bk: e.tensor_scalar(out=QTZ.t[64:128, j, 1, :], in0=bk.t[64:128, :TW], scalar1=0.125, scalar2=0.0, op0=ALU.mult, op1=ALU.add), reads=[bk], writes=[QTZ])
                    else:
                        S.op("act", lambda e, j=j, bk=bk: e.activation(out=QT(j, cols), in_=bk.t[:, :TW], func=AF.Copy, scale=0.125), reads=[bk], writes=[HH])

            kstate = {}

            def piece_kfm():
                wk = wnext("KN")
                kstate["wk"] = wk
                for j in range(4):
                    bk = proj_fm(wk, j, TW, xn_rhs, lambda c: [XNB[c]])
                    dst = KTs[j] if sample else KT[j][ti]
                    S.op("dve", lambda e, bk=bk, dst=dst: e.tensor_copy(out=dst.t[:, :TW], in_=bk.t[:, :TW]), reads=[bk], writes=[dst])

            def piece_ktm():
                wk = kstate["wk"]
                wk3 = wk.t[:].rearrange("p (c n) -> p c n", c=8)
                for bi, (c0, n, r0) in enumerate(tblocks):
                    bk = bank()
                    for c in range(8):
                        mm(bk.t[0:n, :], XN.t[:, c, c0:c0 + n], wk3[:, c, :], c == 0, c == 7, [XNB[c], wk], [bk])
                    st = kvst[bi % 2]
                    S.op("act", lambda e, bk=bk, st=st, n=n: e.activation(out=st.t[0:n, :], in_=bk.t[0:n, :], func=AF.Copy), reads=[bk], writes=[st])
                    S.dma("sp", k_out[r0:r0 + n, :], st.t[0:n, :], reads=[st])

            def piece_v():
                wv = wnext("KN")
                wv3 = wv.t[:].rearrange("p (c n) -> p c n", c=8)
                for bi, (c0, n, r0) in enumerate(tblocks):
                    bk = bank()
                    for c in range(8):
                        mm(bk.t[0:n, :], XN.t[:, c, c0:c0 + n], wv3[:, c, :], c == 0, c == 7, [XNB[c], wv], [bk])
                    st = kvst[(bi + 1) % 2]
                    S.op("act", lambda e, bk=bk, st=st, n=n: e.activation(out=st.t[0:n, :], in_=bk.t[0:n, :], func=AF.Copy), reads=[bk], writes=[st])
                    S.dma("sp", v_out[r0:r0 + n, :], st.t[0:n, :], reads=[st])
                    if sample:
                        S.op("dve", lambda e, bk=bk, bi=bi, n=n: e.tensor_copy(out=Vnew[bi].t[0:n, :], in_=bk.t[0:n, :]), reads=[bk], writes=[Vnew[bi]])
                    else:
                        S.op("dve", lambda e, bk=bk, bi=bi: e.tensor_copy(out=VV[ti].t[:, bi, :], in_=bk.t[:, :]), reads=[bk], writes=[VV[ti]])

            pieces = [piece_q, piece_kfm, piece_ktm, piece_v]

            def lruA(c):
                ue, ue3 = ue_of(c)
                uc = tmp()
                cw = lambda i: vecs.t[:, V_CW + 4 * i + c:V_CW + 4 * i + c + 1]
                S.op("dve", lambda e: e.tensor_scalar(out=v3(uc.t[:, :TW]), in0=ue3[:, :, 3:3 + L], scalar1=cw(3), scalar2=vecs.t[:, V_CB + c:V_CB + c + 1], op0=ALU.mult, op1=ALU.add), reads=[ue, vecs], writes=[uc])
                for i in (2, 1, 0):
                    S.op("dve", lambda e, i=i: e.scalar_tensor_tensor(out=v3(uc.t[:, :TW]), in0=ue3[:, :, i:i + L], scalar=cw(i), in1=v3(uc.t[:, :TW]), op0=ALU.mult, op1=ALU.add), reads=[ue, vecs, uc], writes=[uc])
                S.op("pool", lambda e: e.tensor_copy(out=ue3[:, :, 0:3], in_=ue3[:, :, L:L + 3]), reads=[ue], writes=[ue])
                ub = ucb[c % 2]
                S.op("act", lambda e: e.activation(out=ub.t[:, :TW], in_=uc.t[:, :TW], func=AF.Copy), reads=[uc], writes=[ub])
                return uc, ub

            def lruB(c, uc, ub):
                bkr = bank()
                mm(bkr.t[:, :TW], gw3[:, c, :], ub.t[:, :TW], True, True, [gwb, ub], [bkr])
                bki = bank()
                mm(bki.t[:, :TW], gw3[:, 4 + c, :], ub.t[:, :TW], True, True, [gwb, ub], [bki])
                rr = tmp()
                ii = tmp()
                S.op("act", lambda e: e.activation(out=rr.t[:, :TW], in_=bkr.t[:, :TW], func=AF.Sigmoid, bias=vecs.t[:, V_RGB + c:V_RGB + c + 1]), reads=[bkr, vecs], writes=[rr])
                S.op("act", lambda e: e.activation(out=ii.t[:, :TW], in_=bki.t[:, :TW], func=AF.Sigmoid, bias=vecs.t[:, V_IGB + c:V_IGB + c + 1]), reads=[bki, vecs], writes=[ii])
                a2 = tmp()
                aa = rr
                S.op("act", lambda e: e.activation(out=a2.t[:, :TW], in_=rr.t[:, :TW], func=AF.Exp, scale=cvec.t[:, 4 + c:5 + c]), reads=[rr, cvec], writes=[a2])
                S.op("act", lambda e: e.activation(out=aa.t[:, :TW], in_=rr.t[:, :TW], func=AF.Exp, scale=cvec.t[:, c:c + 1]), reads=[rr, cvec], writes=[aa])
                S.op("act", lambda e: e.activation(out=a2.t[:, :TW], in_=a2.t[:, :TW], func=AF.Ln, scale=-1.0, bias=1.0), reads=[a2], writes=[a2])
                S.op("act", lambda e: e.activation(out=a2.t[:, :TW], in_=a2.t[:, :TW], func=AF.Exp, scale=0.5), reads=[a2], writes=[a2])
                S.op("pool", lambda e: e.tensor_tensor(out=ii.t[:, :TW], in0=ii.t[:, :TW], in1=uc.t[:, :TW], op=ALU.mult), reads=[ii, uc], writes=[ii])
                S.op("dve", lambda e: e.tensor_tensor(out=ii.t[:, :TW], in0=ii.t[:, :TW], in1=a2.t[:, :TW], op=ALU.mult), reads=[ii, a2], writes=[ii])
                hh_ = a2
                for s_ in range(nseg):
                    init = hc.t[:, c, s_:s_ + 1] if sample else hc.t[:, c:c + 1]
                    S.op("dve", lambda e, s_=s_, init=init: e.tensor_tensor_scan(out=hh_.t[:, s_ * L:(s_ + 1) * L], data0=aa.t[:, s_ * L:(s_ + 1) * L], data1=ii.t[:, s_ * L:(s_ + 1) * L], initial=init, op0=ALU.mult, op1=ALU.add), reads=[aa, ii, hc], writes=[hh_])
                hdst = hc.t[:, c, :] if sample else hc.t[:, c:c + 1]
                S.op("pool", lambda e: e.tensor_copy(out=hdst, in_=v3(hh_.t[:, :TW])[:, :, L - 1]), reads=[hh_], writes=[hc])
                S.op("dve", lambda e: e.tensor_tensor(out=MIX(c, cols), in0=hh_.t[:, :TW], in1=ATT(c, cols), op=ALU.mult), reads=[hh_, HH], writes=[HH])

            for c in range(4):
                uc, ub = lruA(c)
                pieces[c]()
                lruB(c, uc, ub)

            ckpt(100 * ti + 4)
            if sample:
                attn_sample()
            else:
                attn_prompt(ti)

            ckpt(100 * ti + 5)
            for s_ in range(2):
                wo = wnext("KO")
                for j in range(4):
                    n = s_ * 4 + j
                    bk = proj_fm(wo, j, TW, lambda c: MIXALL(c, TW), [HH])
                    S.op("dve", lambda e, bk=bk, n=n: e.tensor_tensor(out=XT.t[:, n, :TW], in0=bk.t[:, :TW], in1=XT.t[:, n, :TW], op=ALU.add), reads=[bk, XT], writes=[XT])

            ckpt(100 * ti + 6)
            ffn(0, TW)
            ckpt(100 * ti + 7)
            pool_mixer(ti, TW, nseg, L, sample, first)
            ckpt(100 * ti + 8)
            if ti + 1 <= NT:
                nsample = (ti + 1 == NT)
                nx_d = xs_d if nsample else xp_d
                nt0 = 0 if nsample else (ti + 1) * T
                for tb in range(1 if nsample else 2):
                    S.dma("sp", xstage[tb][:], nx_d[nt0 + tb * 128:nt0 + (tb + 1) * 128, :], writes=[xstage[tb]])
                    xpref.add((ti + 1, tb))
            ffn(1, TW)
            ckpt(100 * ti + 9)
            norm_stats(TW)
            for c in range(8):
                norm_apply(c, V_NFIN, TW, XT.t[:, c, :TW], XT)
            y_d = ys_d if sample else yp_d
            for tb in range(ntb):
                ya, yb, y3 = tmp2()
                for half in range(2):
                    bk = bank()
                    for jj in range(4):
                        c = half * 4 + jj
                        S.op("pe", lambda e, c=c, jj=jj, bk=bk, tb=tb: e.transpose(out=bk.t[:, jj * 128:(jj + 1) * 128], in_=XT.t[:, c, tb * 128:(tb + 1) * 128], identity=ident), reads=[XT, cst], writes=[bk])
                    if half == 0:
                        S.op("act", lambda e, bk=bk, y3=y3: e.activation(out=y3[:, 0, :], in_=bk.t[:, :], func=AF.Copy), reads=[bk], writes=[ya])
                    else:
                        S.op("dve", lambda e, bk=bk, y3=y3: e.tensor_copy(out=y3[:, 1, :], in_=bk.t[:, :]), reads=[bk], writes=[yb])
                S.dma("sp", y_d[t0 + tb * 128:t0 + (tb + 1) * 128, :].rearrange("t (h m) -> t h m", h=2), y3, reads=[ya, yb])

        def MIXALL(c, TW):
            if c < 4:
                return HH.t[:, (8 + c) * T:(8 + c) * T + TW]
            return HH.t[:, c * T:c * T + TW]

        def ATT(j, cols):
            return HH.t[:, (8 + j) * T + cols.start:(8 + j) * T + cols.stop]

        def ffn(layer, TW):
            norm_to_xn(TW, V_NF[0] + 8 * layer)
            for half in range(2):
                slot = 0
                for i in range(6):
                    nf = 2 if i < 5 else 1
                    wb = wnext("GU" if nf == 2 else "GU1")
                    w3 = wb.t[:, 0:8 * 2 * nf * 128].rearrange("p (c n) -> p c n", c=8)
                    for f in range(nf):
                        bg = bank()
                        for c in range(8):
                            mm(bg.t[:, :TW], w3[:, c, f * 128:(f + 1) * 128], XN.t[:, c, :TW], c == 0, c == 7, [wb, XNB[c]], [bg])
                        bu = bank()
                        for c in range(8):
                            mm(bu.t[:, :TW], w3[:, c, (nf + f) * 128:(nf + f + 1) * 128], XN.t[:, c, :TW], c == 0, c == 7, [wb, XNB[c]], [bu])
                        sg = tmp()
                        S.op("act", lambda e, bg=bg, sg=sg: e.activation(out=sg.t[:, :TW], in_=bg.t[:, :TW], func=AF.Silu), reads=[bg], writes=[sg])
                        S.op("dve", lambda e, bu=bu, sg=sg, slot=slot: e.tensor_tensor(out=hh3[:, slot, :TW], in0=bu.t[:, :TW], in1=sg.t[:, :TW], op=ALU.mult), reads=[bu, sg], writes=[HH])
                        slot += 1
                for q in range(4):
                    wb = wnext("DN")
                    w3 = wb.t[:, 0:2816].rearrange("p (f n) -> p f n", f=11)
                    for o in range(2):
                        n = q * 2 + o
                        bk = bank()
                        for f in range(11):
                            mm(bk.t[:, :TW], w3[:, f, o * 128:(o + 1) * 128], hh3[:, f, :TW], f == 0, f == 10, [wb, HH], [bk])
                        S.op("dve", lambda e, bk=bk, n=n: e.tensor_tensor(out=XT.t[:, n, :TW], in0=bk.t[:, :TW], in1=XT.t[:, n, :TW], op=ALU.add), reads=[bk, XT], writes=[XT])

        def pool_mixer(ti, TW, nseg, L, sample, first):
            norm_stats(TW)
            wb = wnext("PW")
            w4 = wb.t[:, 0:2048].rearrange("p (g c n) -> p g c n", g=4, c=2)
            for g in range(4):
                win = POOL_WINDOWS[g]
                for ci in range(2):
                    c = 2 * g + ci
                    ph = PHs[c] if sample else PH[c]
                    ph3 = ph.t[:] if sample else ph.t[:].rearrange("p (s l) -> p s l", s=1)
                    W_ = 15 + L
                    ext = tmp()
                    ext3 = ext.t[:, 0:nseg * W_].rearrange("p (s l) -> p s l", s=nseg)
                    norm_apply(c, V_NM[1], TW, ext3[:, :, 15:15 + L], ext, in_ap=XT.t[:, c, :TW].rearrange("p (s l) -> p s l", s=nseg), rs_ap=rstd.t[:, :TW].rearrange("p (s l) -> p s l", s=nseg))
                    S.op("pool", lambda e: e.tensor_copy(out=ext3[:, :, 0:15], in_=ph3), reads=[ph], writes=[ext])
                    S.op("pool", lambda e: e.tensor_copy(out=ph3, in_=ext3[:, :, L:L + 15]), reads=[ext], writes=[ph])
                    cur = ext
                    cur3 = ext3
                    sh = 1
                    while sh < win:
                        nxt = tmp()
                        nxt3 = nxt.t[:, 0:nseg * W_].rearrange("p (s l) -> p s l", s=nseg)
                        lo = 2 * sh - 1
                        eng = "pool" if sh in (1, 4) else "dve"
                        S.op(eng, lambda e, cur3=cur3, nxt3=nxt3, lo=lo, sh=sh: e.tensor_tensor(out=nxt3[:, :, lo:W_], in0=cur3[:, :, lo:W_], in1=cur3[:, :, lo - sh:W_ - sh], op=ALU.add), reads=[cur], writes=[nxt])
                        cur, cur3 = nxt, nxt3
                        sh *= 2
                    db = dB[ci]
                    S.op("dve", lambda e, cur3=cur3, db=db: e.scalar_tensor_tensor(out=db.t[:, :TW].rearrange("p (s l) -> p s l", s=nseg), in0=cur3[:, :, 15:15 + L], scalar=1.0 / win, in1=ext3[:, :, 15:15 + L], op0=ALU.mult, op1=ALU.subtract),
                         reads=[cur, ext], writes=[db])
                    if first:
                        t16 = tmp()
                        S.op("dve", lambda e, cur=cur, t16=t16: e.tensor_tensor(out=t16.t[:, 0:16], in0=cur.t[:, 15:31], in1=cst.t[:, C_INV + 16 * g:C_INV + 16 * g + 16], op=ALU.mult), reads=[cur, cst], writes=[t16])
                        S.op("dve", lambda e, t16=t16, db=db: e.tensor_tensor(out=db.t[:, 0:16], in0=t16.t[:, 0:16], in1=ext.t[:, 15:31], op=ALU.subtract), reads=[t16, ext, db], writes=[db])
                for o in range(2):
                    n = 2 * g + o
                    bk = bank()
                    for ci in range(2):
                        mm(bk.t[:, :TW], w4[:, g, ci, o * 128:(o + 1) * 128], dB[ci].t[:, :TW], ci == 0, ci == 1, [wb, dB[ci]], [bk])
                    S.op("dve", lambda e, bk=bk, n=n: e.scalar_tensor_tensor(out=XT.t[:, n, :TW], in0=bk.t[:, :TW], scalar=vecs.t[:, V_PS + n:V_PS + n + 1], in1=XT.t[:, n, :TW], op0=ALU.mult, op1=ALU.add),
                         reads=[bk, vecs, XT], writes=[XT])

        rot = {"l": 0, "w": 0}

        def attn_prompt(g):
            A2 = P2[0][:, :].rearrange("p (s t) -> p s t", s=2)
            C2 = P2[1][:, :].rearrange("p (s t) -> p s t", s=2)
            Ab, Cb = [banks[0], banks[1]], [banks[2], banks[3]]
            Oall = [[banks[4], banks[5]], [banks[6], banks[7]]]
            nst = 4 * g + 4
            seq = [(hp, st) for hp in range(4) for st in range(nst)]
            info = {}

            def geom(st):
                kb = nst - 1 - st
                jj = kb - 4 * g
                c0 = 128 * jj if jj > 0 else 0
                return kb, jj, c0, kb // 4, kb % 4

            def emitA(x):
                hp, st = seq[x]
                kb, jj, c0, tt, bi = geom(st)
                kt = KT[hp][tt]
                for s_ in range(2):
                    mm(A2[:, s_, c0:T], kt.t[64 * s_:64 * s_ + 64, bi * 128:(bi + 1) * 128], HH.t[64 * s_:64 * s_ + 64, hp * T + c0:hp * T + T], True, True, [kt, HH], [Ab[s_]])

            def S1(x):
                hp, st = seq[x]
                kb, jj, c0, tt, bi = geom(st)
                ea, eb, e3 = tmp2()
                S.op("act", lambda e: e.activation(out=e3[:, :, c0:T], in_=A2[:, :, c0:T], func=AF.Exp), reads=Ab, writes=[ea, eb])
                if jj >= 0:
                    S.op("pool", lambda e: e.tensor_tensor(out=e3[:, :, c0:c0 + 128], in0=e3[:, :, c0:c0 + 128], in1=mask2, op=ALU.mult), reads=[ea, eb, cst], writes=[ea, eb])
                k_ = rot["l"] % 3
                rot["l"] += 1
                S.op("act", lambda e: e.activation(out=LB[k_].t[:, :, c0:T], in_=e3[:, :, c0:T], func=AF.Ln, bias=1.0), reads=[ea, eb], writes=[LB[k_]])
                info[x] = (hp, st, c0, tt, bi, ea, eb, e3, k_)

            def S2(x):
                hp, st, c0, tt, bi, ea, eb, e3, k_ = info[x]
                for s_ in range(2):
                    mm(C2[:, s_, c0:T], trib.t[:, 0:128], LB[k_].t[:, s_, c0:T], st == 0, False, [trib, LB[k_]], [Cb[s_]])

            def S2p(x):
                hp, st, c0, tt, bi, ea, eb, e3, k_ = info[x]
                pa, pb, p3 = tmp2()
                S.op("act", lambda e: e.activation(out=p3[:, :, c0:T], in_=C2[:, :, c0:T], func=AF.Exp), reads=Cb, writes=[pa, pb])
                info[x] = info[x] + (pa, pb, p3)

            def S3a(x):
                hp, st, c0, tt, bi, ea, eb, e3, k_, pa, pb, p3 = info[x]
                if st < nst - 1:
                    for s_ in range(2):
                        mm(C2[:, s_, c0:T], trib.t[:, 128:256], LB[k_].t[:, s_, c0:T], False, False, [trib, LB[k_]], [Cb[s_]])

            def S3b(x):
                hp, st, c0, tt, bi, ea, eb, e3, k_, pa, pb, p3 = info[x]
                O = Oall[hp % 2]
                w_ = WB[rot["w"] % 2]
                rot["w"] += 1
                S.op("dve", lambda e: e.tensor_tensor(out=w_.t[:, :, c0:T], in0=e3[:, :, c0:T], in1=p3[:, :, c0:T], op=ALU.mult), reads=[ea, eb, pa, pb], writes=[w_])
                for s_ in range(2):
                    mm(O[s_].t[:, c0:T], VV[tt].t[:, bi, hp * 128:(hp + 1) * 128], w_.t[:, s_, c0:T], st == 0, st == nst - 1, [VV[tt], w_], [O[s_]])
                if st == nst - 1:
                    S.op("act", lambda e: e.activation(out=ATT(hp, slice(0, T))[0:64, :], in_=O[0].t[0:64, :], func=AF.Copy), reads=[O[0]], writes=[HH])
                    S.op("dve", lambda e: e.tensor_copy(out=ATT(hp, slice(0, T))[64:128, :], in_=O[1].t[64:128, :]), reads=[O[1]], writes=[HH])
                del info[x]

            N = len(seq)
            emitA(0)
            for n in range(N + 2):
                if n < N:
                    S1(n)
                if n >= 2:
                    S3a(n - 2)
                if 1 <= n < N + 1:
                    S2(n - 1)
                if n + 1 < N:
                    emitA(n + 1)
                if 1 <= n < N + 1:
                    S2p(n - 1)
                if n >= 2:
                    S3b(n - 2)

        def attn_sample():
            NQ = 8 * LS
            nst = NPB + 1
            rotc = {"k": 0, "v": 0}
            for sp_ in range(NS // 2):
                slots = []
                for st in range(nst):
                    for s in range(2):
                        slots.append((s, st))
                info = {}
                A = [banks[0], banks[1]]
                ACC = [banks[2], banks[3]]
                O = [banks[4], banks[5]]
                TR = [banks[6], banks[7]]
                pre = {}

                def load(n):
                    s, st = slots[n]
                    kb = NPB - st
                    if kb == NPB:
                        return
                    seq = 2 * sp_ + s
                    kf = tmp()
                    vf = tmp()
                    S.dma("sp", kf.t[:, 0:512], ck_d[seq, kb * 128:(kb + 1) * 128, :], writes=[kf])
                    S.dma("sp", vf.t[:, 0:512], cv_d[seq, kb * 128:(kb + 1) * 128, :], writes=[vf])
                    pre[n] = (kf, vf)

                ktmap = {}

                def S0(n):
                    s, st = slots[n]
                    if NPB - st == NPB:
                        return
                    kf, vf = pre.pop(n)
                    kb16 = Kbf[rotc["k"] % 2]
                    if s == 0:
                        S.op("act", lambda e: e.activation(out=kb16.t[:, :], in_=kf.t[:, 0:512], func=AF.Copy), reads=[kf], writes=[kb16])
                    else:
                        S.op("dve", lambda e: e.tensor_copy(out=kb16.t[:, :], in_=kf.t[:, 0:512]), reads=[kf], writes=[kb16])
                    trb = TR[s].t[:, :].bitcast(BF16)
                    for j in range(4):
                        S.op("pe", lambda e, j=j: e.transpose(out=trb[:, j * 128:(j + 1) * 128], in_=kb16.t[:, j * 128:(j + 1) * 128], identity=identb[:]), reads=[kb16, identb], writes=[TR[s]])
                    ktb = KTb[rotc["k"] % 2]
                    rotc["k"] += 1
                    S.op("dve", lambda e: e.tensor_copy(out=ktb.t[:, :, :], in_=trb[:, 0:512].rearrange("p (j k) -> p j k", j=4)), reads=[TR[s]], writes=[ktb])
                    ktmap[n] = (ktb, vf)

                def S1(n):
                    s, st = slots[n]
                    seq = 2 * sp_ + s
                    kb = NPB - st
                    new = kb == NPB
                    rows = 128
                    qc = slice(seq * LS, (seq + 1) * LS)
                    if new:
                        vsrc = Vnew[seq]
                        for j in range(4):
                            mm(A[s].t[0:LS, j * 2 * LS:(j + 1) * 2 * LS].rearrange("p (i q) -> p i q", i=2), KTs[j].t[:, qc], QTZ.t[:, j, :, qc], True, True, [KTs[j], QTZ], [A[s]])
                    else:
                        ktb, vf = ktmap.pop(n)
                        vsrc = Vb[rotc["v"] % 3]
                        rotc["v"] += 1
                        if s == 0:
                            S.op("dve", lambda e: e.tensor_copy(out=vsrc.t[:, :], in_=vf.t[:, 0:512]), reads=[vf], writes=[vsrc])
                        else:
                            S.op("act", lambda e: e.activation(out=vsrc.t[:, :], in_=vf.t[:, 0:512], func=AF.Copy), reads=[vf], writes=[vsrc])
                        for j in range(4):
                            mm(A[s].t[:, j * 2 * LS:(j + 1) * 2 * LS].rearrange("p (i q) -> p i q", i=2), ktb.t[:, j, :], QTZ.t[:, j, :, qc], True, True, [ktb, QTZ], [A[s]])
                    e_ = tmp()
                    if new:
                        S.op("pool", lambda e: e.memset(e_.t[:, 0:NQ], 0.0), writes=[e_])
                        S.op("act", lambda e: e.activation(out=e_.t[0:LS, 0:NQ], in_=A[s].t[0:LS, 0:NQ], func=AF.Exp), reads=[A[s]], writes=[e_])
                        S.op("pool", lambda e: e.tensor_tensor(out=e_.t[0:LS, 0:NQ], in0=e_.t[0:LS, 0:NQ], in1=m32, op=ALU.mult), reads=[e_, cst], writes=[e_])
                    else:
                        S.op("act", lambda e: e.activation(out=e_.t[0:rows, 0:NQ], in_=A[s].t[0:rows, 0:NQ], func=AF.Exp), reads=[A[s]], writes=[e_])
                    l_ = lbuf[rot["l"] % 3]
                    rot["l"] += 1
                    S.op("act", lambda e: e.activation(out=l_.t[0:rows, 0:NQ], in_=e_.t[0:rows, 0:NQ], func=AF.Ln, bias=1.0), reads=[e_], writes=[l_])
                    info[n] = (s, st, seq, rows, e_, l_, vsrc)

                def S2(n):
                    s, st, seq, rows, e_, l_, vsrc = info[n]
                    mm(ACC[s].t[:, 0:NQ], trib.t[0:rows, 0:128], l_.t[0:rows, 0:NQ], st == 0, False, [trib, l_], [ACC[s]])
                    p_ = tmp()
                    S.op("act", lambda e: e.activation(out=p_.t[0:rows, 0:NQ], in_=ACC[s].t[0:rows, 0:NQ], func=AF.Exp), reads=[ACC[s]], writes=[p_])
                    info[n] = info[n] + (p_,)

                def S3(n):
                    s, st, seq, rows, e_, l_, vsrc, p_ = info[n]
                    if st < nst - 1:
                        mm(ACC[s].t[:, 0:NQ], trib.t[0:rows, 128:256], l_.t[0:rows, 0:NQ], False, False, [trib, l_], [ACC[s]])
                    w_ = wbuf[rot["w"] % 3]
                    rot["w"] += 1
                    S.op("dve", lambda e: e.tensor_tensor(out=w_.t[0:rows, 0:NQ], in0=e_.t[0:rows, 0:NQ], in1=p_.t[0:rows, 0:NQ], op=ALU.mult), reads=[e_, p_], writes=[w_])
                    for j in range(4):
                        mm(O[s].t[:, j * 2 * LS:(j + 1) * 2 * LS], vsrc.t[0:rows, j * 128:(j + 1) * 128], w_.t[0:rows, j * 2 * LS:(j + 1) * 2 * LS], st == 0 and j == 0, st == nst - 1, [vsrc, w_], [O[s]])
                    if st == nst - 1:
                        qc = slice(seq * LS, (seq + 1) * LS)
                        for h in range(8):
                            j, i = h // 2, h % 2
                            eng = "act" if h % 2 == 0 else "dve"
                            if eng == "act":
                                S.op("act", lambda e, h=h, j=j, i=i: e.activation(out=ATT(j, qc)[64 * i:64 * i + 64, :], in_=O[s].t[64 * i:64 * i + 64, h * LS:(h + 1) * LS], func=AF.Copy), reads=[O[s]], writes=[HH])
                            else:
                                S.op("dve", lambda e, h=h, j=j, i=i: e.tensor_copy(out=ATT(j, qc)[64 * i:64 * i + 64, :], in_=O[s].t[64 * i:64 * i + 64, h * LS:(h + 1) * LS]), reads=[O[s]], writes=[HH])
                    del info[n]

                N = len(slots)
                for n in range(min(3, N)):
                    load(n)
                S0(0)
                for n in range(N + 2):
                    if n + 1 < N:
                        S0(n + 1)
                    if n < N:
                        S1(n)
                    if n + 3 < N:
                        load(n + 3)
                    if 1 <= n < N + 1:
                        S2(n - 1)
                    if n >= 2:
                        S3(n - 2)

        try:
            ckpt(0)
            for ti in range(NT + 1):
                process_tile(ti)
                if ti == 0:
                    S.barrier()
                ckpt(10 + ti)
        except StopBuild:
            pass

        with nc.allow_non_contiguous_dma(reason="small state outputs"):
            S.dma("sp", hp_d.rearrange("o (c p) -> p (o c)", p=128), HC[:], reads=[HC])
            for s in range(NS):
                S.dma("sp", hs_d[s:s + 1, :].rearrange("o (c p) -> p (o c)", p=128), HCs.t[:, :, s], reads=[HCs])
            for c in range(4):
                S.dma("sp", convp_d[:, c * 128:(c + 1) * 128].rearrange("r p -> p r"), UE[c].t[:, 0:3], reads=[UE[c]])
                for s in range(NS):
                    S.dma("sp", convs_d[s * 3:(s + 1) * 3, c * 128:(c + 1) * 128].rearrange("r p -> p r"), UEs[c].t[:, s, 0:3], reads=[UEs[c]])
        def pool_out(srcs, dst):
            bks = [bank(), bank()]
            for c in range(8):
                bk = bks[c // 4]
                S.op("pe", lambda e, c=c, bk=bk: e.transpose(out=bk.t[0:15, (c % 4) * 128:(c % 4 + 1) * 128], in_=srcs[c], identity=ident), reads=[cst] + list(PH) + list(PHs), writes=[bk])
            S.op("act", lambda e: e.activation(out=ost.t[0:15, 0:512], in_=bks[0].t[0:15, :], func=AF.Copy), reads=[bks[0]], writes=[ost])
            S.op("dve", lambda e: e.tensor_copy(out=ost.t[0:15, 512:1024], in_=bks[1].t[0:15, :]), reads=[bks[1]], writes=[ost])
            S.dma("sp", dst, ost.t[0:15, :], reads=[ost])

        pool_out([PH[c].t[:, :] for c in range(8)], poolp_d[:, :])
        for s in range(NS):
            pool_out([PHs[c].t[:, s, :] for c in range(8)], pools_d[s * 15:(s + 1) * 15, :])
        for q in ("sp", "act"):
            for i, v in enumerate(S.dcount[q]):
                if v > 0:
                    S._wait("sp", ("d" + q, i), v)
        build.stats = dict(S.count, waits=S.nwaits)
    return nc


def run(inputs, n_cores, SEQ, NS, LS, PAST):
    f = lambda k: np.asarray(inputs[k], dtype=np.float32)
    xp, xs = f("x_prompt"), f("x_sample")
    ck, cv = f("cache_sb_k")[0], f("cache_sb_v")[0]
    h0, conv0, pool0 = f("state_lru_h")[0], f("state_lru_conv")[0], f("state_pool")[0]
    wsrc = host_chunks(f("hyb_w_in")[0], f("hyb_w_out")[0], f("ffn_gate"), f("ffn_up"), f("ffn_down"), f("pool_w")[0])
    vecs = np.zeros((128, NV), np.float32)
    nm, nf = f("norm_mix"), f("norm_ffn")
    vecs[:, 0:8] = fm(nm[0]); vecs[:, 8:16] = fm(nm[1]); vecs[:, 16:24] = fm(nf[0]); vecs[:, 24:32] = fm(nf[1])
    vecs[:, 32:40] = fm(f("norm_final")); vecs[:, 40:48] = fm(f("pool_scale")[0])
    cw = f("hyb_conv_w")[0]
    for i in range(4):
        vecs[:, 48 + 4 * i:52 + 4 * i] = fm(cw[i])
    vecs[:, 64:68] = fm(f("hyb_conv_b")[0]); vecs[:, 68:72] = fm(f("hyb_rg_b")[0]); vecs[:, 72:76] = fm(f("hyb_ig_b")[0]); vecs[:, 76:80] = fm(f("hyb_lambda")[0])
    gatew = np.zeros((128, 8, 128), np.float32)
    rg, ig = f("hyb_rg_w")[0], f("hyb_ig_w")[0]
    for c in range(4):
        for kk, wsel in enumerate((rg, ig)):
            gatew[0:64, 4 * kk + c, 0:64] = wsel[2 * c]
            gatew[64:128, 4 * kk + c, 64:128] = wsel[2 * c + 1]
    gatew = gatew.reshape(128, 1024)
    cst = host_consts()
    in_maps = []
    for r in range(n_cores):
        sl = slice(r * NS, (r + 1) * NS)
        in_maps.append({
            "xp": np.ascontiguousarray(xp[r]),
            "xs": np.ascontiguousarray(xs[sl].reshape(NS * LS, D)),
            "ck": np.ascontiguousarray(ck[sl].reshape(NS, PAST, 512)),
            "cv": np.ascontiguousarray(cv[sl].reshape(NS, PAST, 512)),
            "h0": np.ascontiguousarray(h0[sl].reshape(NS, 4, 128).transpose(2, 1, 0)),
            "conv0": np.ascontiguousarray(conv0[sl].reshape(NS, 3, 4, 128).transpose(3, 2, 0, 1)),
            "pool0": np.ascontiguousarray(pool0[sl].reshape(NS, 15, 8, 128).transpose(3, 2, 0, 1)),
            "wsrc": wsrc, "vecs": vecs, "gatew": gatew, "cst": cst,
        })
    nc = build(SEQ, NS, LS, PAST)
    res = run_bass_kernel_spmd(nc, in_maps, core_ids=list(range(n_cores)))
    R = res.results
    cat = lambda k: np.stack([np.asarray(R[r][k], dtype=np.float32) for r in range(n_cores)])
    y_p = cat("yp")
    y_s = cat("ys").reshape(n_cores * NS, LS, D)
    k_p = cat("kp").reshape(1, n_cores, SEQ, 8, 64)
    v_p = cat("vp").reshape(1, n_cores, SEQ, 8, 64)
    h_p = cat("hp").reshape(1, n_cores, 512)
    conv_p = cat("convp").reshape(1, n_cores, 3, 512)
    pool_p = cat("poolp").reshape(1, n_cores, 15, D)
    k_s = cat("ks").reshape(1, n_cores * NS, LS, 8, 64)
    v_s = cat("vs").reshape(1, n_cores * NS, LS, 8, 64)
    h_s = cat("hs").reshape(1, n_cores * NS, 512)
    conv_s = cat("convs").reshape(1, n_cores * NS, 3, 512)
    pool_s = cat("pools").reshape(1, n_cores * NS, 15, D)
    if DEBUG:
        run.dbg = {k: np.asarray(v) for k, v in R[0].items() if k.startswith('dbg_')}
    return (y_p, y_s, k_p, v_p, h_p, conv_p, pool_p, k_s, v_s, h_s, conv_s, pool_s)


def kernel(**inputs):
    return run(inputs, 8, 4096, 4, 32, 4096)
```

```python
import numpy as np
from contextlib import ExitStack
import concourse.bass as bass
import concourse.mybir as mybir
from concourse.bass_utils import run_bass_kernel_spmd

F32 = mybir.dt.float32
BF16 = mybir.dt.bfloat16
AF = mybir.ActivationFunctionType
ALU = mybir.AluOpType

D = 1024
DFF = 2816
NFC = 22
POOL_WINDOWS = (2, 4, 8, 16)
EPS = 1e-6
EPOCH = 30000
DEBUG = False
STOP = None


class StopBuild(Exception):
    pass


def ckpt(k):
    if STOP is not None and STOP == k:
        raise StopBuild()
WSLOT = 4096

def chunk_list():
    L = []
    for s in (3, 4, 0, 1, 2):
        L.append(("KN", s, 0, 4096))
    for s in range(2):
        L.append(("KO", s, 0, 4096))

    def ffn(layer):
        for half in range(2):
            f0 = half * 11
            for i in range(5):
                L.append(("GU", layer, f0 + 2 * i, 4096))
            L.append(("GU1", layer, f0 + 10, 2048))
            for q in range(4):
                L.append(("DN", layer, half * 4 + q, 2816))
    ffn(0)
    L.append(("PW", 0, 0, 2048))
    ffn(1)
    return L


CHUNKS = chunk_list()
NCH = len(CHUNKS)


def host_chunks(w_in, w_out, gate, up, down, pool_w):
    out = np.zeros((NCH, 128, WSLOT), np.float32)
    for i, (k, a, b, nel) in enumerate(CHUNKS):
        if k == "KN":
            out[i] = w_in[:, a * 512:(a + 1) * 512].reshape(8, 128, 512).transpose(1, 0, 2).reshape(128, 4096)
        elif k == "KO":
            out[i] = w_out[:, a * 512:(a + 1) * 512].reshape(8, 128, 512).transpose(1, 0, 2).reshape(128, 4096)
        elif k == "GU":
            g = gate[a][:, b * 128:(b + 2) * 128].reshape(8, 128, 256)
            u = up[a][:, b * 128:(b + 2) * 128].reshape(8, 128, 256)
            out[i] = np.concatenate([g, u], axis=2).transpose(1, 0, 2).reshape(128, 4096)
        elif k == "GU1":
            g = gate[a][:, b * 128:(b + 1) * 128].reshape(8, 128, 128)
            u = up[a][:, b * 128:(b + 1) * 128].reshape(8, 128, 128)
            out[i, :, :2048] = np.concatenate([g, u], axis=2).transpose(1, 0, 2).reshape(128, 2048)
        elif k == "DN":
            half, q = b // 4, b % 4
            blk = down[a][half * 1408:(half + 1) * 1408, q * 256:(q + 1) * 256]
            out[i, :, :2816] = blk.reshape(11, 128, 256).transpose(1, 0, 2).reshape(128, 2816)
        elif k == "PW":
            out[i, :, :2048] = pool_w.reshape(4, 2, 128, 256).transpose(2, 0, 1, 3).reshape(128, 2048)
    return out


V_NM = (0, 8)
V_NF = (16, 24)
V_NFIN = 32
V_PS = 40
V_CW = 48
V_CB = 64
V_RGB = 68
V_IGB = 72
V_LAM = 76
NV = 80
C_ID = 0
C_TRI = 128
C_MASK = 384
C_ONES = 512
C_INV = 640
C_M32 = 704
C_MASK2 = 960
NCST = 1216


def host_consts():
    c = np.zeros((128, NCST), np.float32)
    c[:, C_ID:C_ID + 128] = np.eye(128, dtype=np.float32)
    j = np.arange(128)[:, None]
    k = np.arange(128)[None, :]
    c[:, C_TRI:C_TRI + 128] = -1.0 * (j >= k)
    c[:, C_TRI + 128:C_TRI + 256] = -1.0 * (j < k)
    c[:, C_MASK:C_MASK + 128] = (j < k)
    c[:, C_ONES:C_ONES + 128] = 1.0 / D
    c[:, C_MASK2:C_MASK2 + 128] = (j < k)
    c[:, C_MASK2 + 128:C_MASK2 + 256] = (j < k)
    for g, w in enumerate(POOL_WINDOWS):
        c[:, C_INV + 16 * g:C_INV + 16 * (g + 1)] = 1.0 / np.minimum(w, np.arange(16) + 1.0)
    m32 = (np.arange(32)[:, None] < np.arange(32)[None, :]).astype(np.float32)
    c[:32, C_M32:C_M32 + 256] = np.tile(m32, (1, 8))
    return c


def fm(v):
    return np.ascontiguousarray(v.reshape(-1, 128).T)


class Buf:
    __slots__ = ("t", "writer", "readers", "name")

    def __init__(self, t, name=""):
        self.t = t
        self.writer = None
        self.readers = {}
        self.name = name

    def __getitem__(self, idx):
        return self.t[idx]


class Sched:
    def __init__(self, nc, es, n_epochs=6, n_dma_sems=12):
        self.nc = nc
        self.eng = {"pe": nc.tensor, "act": nc.scalar, "dve": nc.vector, "pool": nc.gpsimd, "sp": nc.sync}
        self.sems = {}
        self.count = {}
        self.waited = {k: {} for k in self.eng}
        for k in self.eng:
            self.count[k] = 0
            if k == "sp":
                continue
            self.sems[k] = [es.enter_context(nc.semaphore(name=f"s_{k}{i}")) for i in range(n_epochs)]
        self.dsems = {}
        self.dcount = {}
        self.dnext = {}
        for q in ("sp", "act"):
            self.dsems[q] = [es.enter_context(nc.semaphore(name=f"d_{q}{i}")) for i in range(n_dma_sems)]
            self.dcount[q] = [0] * n_dma_sems
            self.dnext[q] = 0
        self.semobj = {}
        for k, lst in self.sems.items():
            for i, s in enumerate(lst):
                self.semobj[(k, i)] = s
        for q, lst in self.dsems.items():
            for i, s in enumerate(lst):
                self.semobj[("d" + q, i)] = s
        self.nwaits = 0

    def _wait(self, engname, key, val):
        w = self.waited[engname]
        if w.get(key, 0) >= val:
            return
        self.eng[engname].wait_ge(self.semobj[key], val)
        w[key] = val
        self.nwaits += 1

    def _deps(self, engname, reads, writes):
        need = {}
        for b in reads:
            if b.writer is not None:
                k, v = b.writer
                if need.get(k, 0) < v:
                    need[k] = v
        for b in writes:
            if b.writer is not None:
                k, v = b.writer
                if need.get(k, 0) < v:
                    need[k] = v
            for k, v in b.readers.items():
                if need.get(k, 0) < v:
                    need[k] = v
        for k, v in need.items():
            if engname == "pe" and k[0] == "pe":
                continue
            self._wait(engname, k, v)

    def _mark(self, tok, reads, writes):
        k, v = tok
        for b in reads:
            if b.readers.get(k, 0) < v:
                b.readers[k] = v
        for b in writes:
            b.writer = tok
            b.readers = {}

    def op(self, engname, fn, reads=(), writes=()):
        self._deps(engname, reads, writes)
        inst = fn(self.eng[engname])
        c = self.count[engname]
        ep, v = c // EPOCH, c % EPOCH + 1
        inst.then_inc(self.sems[engname][ep], 1)
        self.count[engname] = c + 1
        tok = ((engname, ep), v)
        self._mark(tok, reads, writes)
        return tok

    def dma(self, q, out, in_, reads=(), writes=(), **kw):
        self._deps(q, reads, writes)
        i = self.dnext[q]
        self.dnext[q] = (i + 1) % len(self.dsems[q])
        key = ("d" + q, i)
        prev = self.dcount[q][i]
        if prev > 0:
            self._wait(q, key, prev)
        inst = self.eng[q].dma_start(out=out, in_=in_, **kw)
        val = prev + 16
        inst.then_inc(self.dsems[q][i], 16)
        self.dcount[q][i] = val
        tok = (key, val)
        self._mark(tok, reads, writes)
        return tok

    def barrier(self):
        toks = {}
        for k in ("pe", "act", "dve", "pool"):
            c = self.count[k]
            if c > 0:
                toks[(k, (c - 1) // EPOCH)] = (c - 1) % EPOCH + 1
        for q in ("sp", "act"):
            for i, v in enumerate(self.dcount[q]):
                if v > 0:
                    toks[("d" + q, i)] = v
        for e in ("pe", "act", "dve", "pool", "sp"):
            for key, v in toks.items():
                if key[0] == e:
                    continue
                self._wait(e, key, v)

    def finish(self, bufs):
        for b in bufs:
            if b.writer is not None:
                self._wait("sp", b.writer[0], b.writer[1])


def build(SEQ, NS, LS, PAST):
    T = 512
    NT = SEQ // T
    TS = NS * LS
    NPB = PAST // 128
    nc = bass.Bass("TRN2", target_bir_lowering=False)

    def din(name, shape, dt=F32):
        return nc.dram_tensor(name, list(shape), dt, kind="ExternalInput").ap()

    def dout(name, shape):
        return nc.dram_tensor(name, list(shape), F32, kind="ExternalOutput").ap()

    xp_d = din("xp", (SEQ, D))
    xs_d = din("xs", (TS, D))
    ck_d = din("ck", (NS, PAST, 512))
    cv_d = din("cv", (NS, PAST, 512))
    h0_d = din("h0", (128, 4, NS))
    conv0_d = din("conv0", (128, 4, NS, 3))
    pool0_d = din("pool0", (128, 8, NS, 15))
    wsrc_d = din("wsrc", (NCH, 128, WSLOT))
    vecs_d = din("vecs", (128, NV))
    gatew_d = din("gatew", (128, 8 * 128))
    cst_d = din("cst", (128, NCST))
    yp_d = dout("yp", (SEQ, D))
    ys_d = dout("ys", (TS, D))
    kp_d = dout("kp", (SEQ, 512))
    vp_d = dout("vp", (SEQ, 512))
    hp_d = dout("hp", (1, 512))
    convp_d = dout("convp", (3, 512))
    poolp_d = dout("poolp", (15, D))
    ks_d = dout("ks", (TS, 512))
    vs_d = dout("vs", (TS, 512))
    hs_d = dout("hs", (NS, 512))
    convs_d = dout("convs", (NS * 3, 512))
    pools_d = dout("pools", (NS * 15, D))
    wscr_d = nc.dram_tensor("wscr", [NCH, 128, WSLOT], BF16, kind="Internal").ap()

    es = ExitStack()
    with es:
        S = Sched(nc, es)
        cnt = [0]
        dbg_list = []

        def dbg(name, buf, ap, shape, dt=F32):
            if not DEBUG:
                return
            d = nc.dram_tensor('dbg_' + name, list(shape), dt, kind='ExternalOutput').ap()
            S.dma('sp', d, ap, reads=[buf])
            dbg_list.append(name)

        def sb(shape, dt, name=None):
            cnt[0] += 1
            nm = "sb_" + (name or f"t{cnt[0]}")
            return Buf(es.enter_context(nc.sbuf_tensor(nm, list(shape), dt)), nm)

        P2 = [es.enter_context(nc.psum_tensor(f"pbank{i}", [128, 1024], F32)) for i in range(4)]
        banks = [Buf(P2[i // 2][:, (i % 2) * 512:(i % 2 + 1) * 512], f"bank{i}") for i in range(8)]
        bank_rr = [0]

        def bank():
            b = banks[bank_rr[0] % 8]
            bank_rr[0] += 1
            return b

        KT = [[sb([128, T], BF16) for _ in range(NT)] for _ in range(4)]
        VV = [sb([128, 4, 512], BF16) for _ in range(max(NT, 8))]
        XT = sb([128, 8, T], F32, "XT")
        xstage = [sb([128, D], F32) for _ in range(2)]
        XN = sb([128, 8, T], BF16, "XN")
        XNB = [Buf(XN.t[:, c_, :], f"xn{c_}") for c_ in range(8)]
        HH = sb([128, 12 * T], BF16, "HH")
        hh3 = HH.t[:].rearrange("p (f t) -> p f t", t=T)

        def QT(j, cols):
            return HH.t[:, j * T + cols.start:j * T + cols.stop]

        def MIX(c, cols):
            return HH.t[:, (4 + c) * T + cols.start:(4 + c) * T + cols.stop]

        UE = [sb([128, 3 + T], F32) for _ in range(4)]
        PH = [sb([128, 15], F32) for _ in range(8)]
        HC = sb([128, 4], F32, "HC")
        HCs = sb([128, 4, NS], F32, "HCs")
        NTMP = 12
        TB = [es.enter_context(nc.sbuf_tensor(f"sb_tb{k_}", [128, 1088], F32)) for k_ in range(NTMP // 2)]
        tmps = [Buf(TB[k_ // 2][:, (k_ % 2) * 544:(k_ % 2 + 1) * 544], f"tmp{k_}") for k_ in range(NTMP)]

        def tmp2():
            if tmp_rr[0] % 2 == 1:
                tmp_rr[0] += 1
            k_ = (tmp_rr[0] % NTMP) // 2
            a_, b_ = tmps[2 * k_], tmps[2 * k_ + 1]
            tmp_rr[0] += 2
            return a_, b_, TB[k_][:, :].rearrange("p (s l) -> p s l", s=2)[:, :, 0:T]
        tmp_rr = [0]

        def tmp():
            b = tmps[tmp_rr[0] % NTMP]
            tmp_rr[0] += 1
            return b

        LB = [sb([128, 2, T], BF16) for _ in range(3)]
        lbuf = [Buf(LB[k_].t[:, 0, :], f"lb{k_}") for k_ in range(3)]
        WB = [sb([128, 2, T], BF16) for _ in range(2)]
        wbuf = [Buf(WB[k_ % 2].t[:, k_ // 2, :], f"wb{k_}") for k_ in range(3)]
        ucb = [sb([128, T], BF16) for _ in range(2)]
        dB = ucb
        kvst = [sb([128, 512], F32) for _ in range(2)]
        wring = [sb([128, WSLOT], BF16) for _ in range(3)]
        wstg = [sb([128, 1024], F32) for _ in range(2)]
        KTb = [Buf(VV[5].t[:, k_, :].rearrange("p (j t) -> p j t", j=4), f"ktb{k_}") for k_ in range(2)]
        Vb = [Buf(VV[6].t[:, k_, :], f"vb{k_}") for k_ in range(3)]
        Vnew = [Buf(VV[s_].t[:, 0, :], f"vnew{s_}") for s_ in range(NS)]
        QTZ = Buf(VV[4].t[:, 0:2, :].rearrange("p a n -> p (a n)")[:, 0:8 * TS].rearrange("p (j i t) -> p j i t", j=4, i=2), "qtz")
        UEs = [Buf(VV[c_].t[:, 1, :].bitcast(F32)[:, 0:NS * (3 + LS)].rearrange("p (s l) -> p s l", s=NS), f"ues{c_}") for c_ in range(4)]
        PHs = [Buf(VV[c_].t[:, 3, :].bitcast(F32)[:, 128:128 + NS * 15].rearrange("p (s l) -> p s l", s=NS), f"phs{c_}") for c_ in range(8)]
        KTs = [Buf(VV[7].t[:, j_, 0:TS], f"kts{j_}") for j_ in range(4)]
        cst = sb([128, NCST], F32, "cst")
        vecs = sb([128, NV], F32, "vecs")
        cvec = sb([128, 16], F32, "cvec")
        gwf = wstg[0]
        gwb = sb([128, 1024], BF16, "gwb")
        trib = sb([128, 256], BF16, "trib")
        onesb = sb([128, 128], BF16, "onesb")
        sq = [sb([128, T], BF16) for _ in range(2)]
        rstd = sb([128, T], F32, "rstd")
        epsv = sb([128, 1], F32, "epsv")
        ost = xstage[0]
        WS = [Buf(None, f"ws{i}") for i in range(NCH)]

        ident = cst.t[:, C_ID:C_ID + 128]
        mask = cst.t[:, C_MASK:C_MASK + 128]
        m32 = cst.t[0:32, C_M32:C_M32 + 256]
        mask2 = cst.t[:, C_MASK2:C_MASK2 + 256].rearrange("p (s k) -> p s k", s=2)

        S.op("pool", lambda e: e.memset(epsv[:], EPS), writes=[epsv])
        S.dma("sp", cst[:], cst_d, writes=[cst])
        S.dma("sp", vecs[:], vecs_d, writes=[vecs])
        S.dma("sp", gwf[:], gatew_d, writes=[gwf])
        S.op("dve", lambda e: e.tensor_copy(out=trib[:], in_=cst.t[:, C_TRI:C_TRI + 256]), reads=[cst], writes=[trib])
        S.op("dve", lambda e: e.tensor_copy(out=onesb[:], in_=cst.t[:, C_ONES:C_ONES + 128]), reads=[cst], writes=[onesb])
        S.op("pool", lambda e: e.tensor_copy(out=gwb[:], in_=gwf[:]), reads=[gwf], writes=[gwb])
        S.op("act", lambda e: e.activation(out=cvec.t[:, 8:12], in_=vecs.t[:, V_LAM:V_LAM + 4], func=AF.Exp, scale=-1.0), reads=[vecs], writes=[cvec])
        S.op("act", lambda e: e.activation(out=cvec.t[:, 12:16], in_=cvec.t[:, 8:12], func=AF.Ln, bias=1.0), reads=[cvec], writes=[cvec])
        S.op("dve", lambda e: e.tensor_scalar(out=cvec.t[:, 0:4], in0=cvec.t[:, 12:16], scalar1=-8.0, scalar2=0.0, op0=ALU.mult, op1=ALU.add), reads=[cvec], writes=[cvec])
        S.op("dve", lambda e: e.tensor_scalar(out=cvec.t[:, 4:8], in0=cvec.t[:, 12:16], scalar1=-16.0, scalar2=0.0, op0=ALU.mult, op1=ALU.add), reads=[cvec], writes=[cvec])
        for c in range(4):
            S.op("pool", lambda e, c=c: e.memset(UE[c].t[:, 0:3], 0.0), writes=[UE[c]])
        S.op("pool", lambda e: e.memset(HC[:], 0.0), writes=[HC])
        S.dma("sp", HCs[:], h0_d, writes=[HCs])
        for c in range(8):
            S.op("pool", lambda e, c=c: e.memset(PH[c][:], 0.0), writes=[PH[c]])

        order = []
        for ti in range(NT + 1):
            for ci in range(NCH):
                order.append((ti, ci))
        wstate = {"emitted": 0, "next": 0, "stg": 0, "cast": 0}
        LOOK = 2

        stg_pool = list(wstg) + [Buf(VV[k_].t[:, :, :].rearrange("p a n -> p (a n)").bitcast(F32), f"vstg{k_}") for k_ in range(1, len(VV))]
        NSTG = (len(stg_pool) // 4) * 4
        LD = NSTG // 4 - 1
        wload = {"emitted": 0, "map": {}}

        def w_load(i):
            ti, ci = order[i]
            if ti != 0:
                return
            nel = CHUNKS[ci][3]
            q = nel // 4
            lst = []
            for k in range(4):
                st = stg_pool[wstate["stg"] % NSTG]
                wstate["stg"] += 1
                S.dma("sp", st.t[:, 0:q], wsrc_d[ci, :, k * q:(k + 1) * q], writes=[st])
                lst.append(st)
            wload["map"][i] = lst

        def w_emit(i):
            ti, ci = order[i]
            buf = wring[i % 3]
            nel = CHUNKS[ci][3]
            if ti == 0:
                while wload["emitted"] <= min(i + LD, NCH - 1):
                    w_load(wload["emitted"])
                    wload["emitted"] += 1
                q = nel // 4
                eng = "act" if (wstate["cast"] % 2 == 1) else "dve"
                wstate["cast"] += 1
                for k, st in enumerate(wload["map"].pop(i)):
                    if eng == "act":
                        S.op("act", lambda e, k=k, st=st: e.activation(out=buf.t[:, k * q:(k + 1) * q], in_=st.t[:, 0:q], func=AF.Copy), reads=[st], writes=[buf])
                    else:
                        S.op("dve", lambda e, k=k, st=st: e.tensor_copy(out=buf.t[:, k * q:(k + 1) * q], in_=st.t[:, 0:q]), reads=[st], writes=[buf])
                S.dma("sp", wscr_d[ci, :, 0:nel], buf.t[:, 0:nel], reads=[buf], writes=[WS[ci]])
            else:
                S.dma("sp", buf.t[:, 0:nel], wscr_d[ci, :, 0:nel], reads=[WS[ci]], writes=[buf])

        def wnext(kind):
            i = wstate["next"]
            wstate["next"] += 1
            assert CHUNKS[order[i][1]][0] == kind, (CHUNKS[order[i][1]], kind)
            while wstate["emitted"] <= min(i + LOOK, len(order) - 1):
                w_emit(wstate["emitted"])
                wstate["emitted"] += 1
            return wring[i % 3]

        def mm(out, lhsT, rhs, start, stop, R, W):
            S.op("pe", lambda e: e.matmul(out, lhsT=lhsT, rhs=rhs, start=start, stop=stop, skip_group_check=True), reads=R, writes=W)

        def norm_stats(TW):
            bk = bank()
            for c in range(8):
                s = sq[c % 2]
                if c % 2 == 0:
                    S.op("act", lambda e, c=c, s=s: e.activation(out=s.t[:, :TW], in_=XT.t[:, c, :TW], func=AF.Square), reads=[XT], writes=[s])
                else:
                    S.op("dve", lambda e, c=c, s=s: e.tensor_tensor(out=s.t[:, :TW], in0=XT.t[:, c, :TW], in1=XT.t[:, c, :TW], op=ALU.mult), reads=[XT], writes=[s])
                mm(bk.t[:, :TW], onesb[:], s.t[:, :TW], c == 0, c == 7, [onesb, s], [bk])
            t1 = tmp()
            S.op("act", lambda e: e.activation(out=t1.t[:, :TW], in_=bk.t[:, :TW], func=AF.Ln, bias=epsv.t[:, 0:1]), reads=[bk, epsv], writes=[t1])
            S.op("act", lambda e: e.activation(out=rstd.t[:, :TW], in_=t1.t[:, :TW], func=AF.Exp, scale=-0.5), reads=[t1], writes=[rstd])

        def norm_to_xn(TW, gcol):
            norm_stats(TW)
            for c in range(8):
                norm_apply(c, gcol, TW, XN.t[:, c, :TW], XNB[c])

        def norm_apply(c, gcol, TW, out_ap, out_buf, in_ap=None, rs_ap=None):
            in_ap = XT.t[:, c, :TW] if in_ap is None else in_ap
            rs_ap = rstd.t[:, :TW] if rs_ap is None else rs_ap
            gsc = vecs.t[:, gcol + c:gcol + c + 1]
            if c % 4 != 3:
                S.op("dve", lambda e: e.scalar_tensor_tensor(out=out_ap, in0=in_ap, scalar=gsc, in1=rs_ap, op0=ALU.mult, op1=ALU.mult), reads=[XT, vecs, rstd], writes=[out_buf])
            else:
                t_ = tmp()
                tv = t_.t[:, :TW] if len(in_ap.shape) == 2 else t_.t[:, :TW].rearrange("p (s l) -> p s l", s=in_ap.shape[1])
                S.op("act", lambda e: e.activation(out=tv, in_=in_ap, func=AF.Copy, scale=gsc), reads=[XT, vecs], writes=[t_])
                S.op("pool", lambda e: e.tensor_tensor(out=out_ap, in0=tv, in1=rs_ap, op=ALU.mult), reads=[t_, rstd], writes=[out_buf])

        def proj_fm(wb, j, TW, rhs_fn, R):
            bk = bank()
            w3 = wb.t[:].rearrange("p (c n) -> p c n", c=8)
            for c in range(8):
                mm(bk.t[:, :TW], w3[:, c, j * 128:(j + 1) * 128], rhs_fn(c), c == 0, c == 7, [wb] + (R(c) if callable(R) else R), [bk])
            return bk

        xpref = set()

        def process_tile(ti):
            sample = ti == NT
            TW = TS if sample else T
            nseg = NS if sample else 1
            L = LS if sample else T
            first = ti == 0
            t0 = 0 if sample else ti * T
            x_d = xs_d if sample else xp_d
            ntb = TW // 128
            cols = slice(0, TW)

            if sample:
                S.barrier()
                for s_ in range(NS):
                    S.op("pool", lambda e, s_=s_: e.memset(Vnew[s_].t[:, :], 0.0), writes=[Vnew[s_]])
                S.op("pool", lambda e: e.memset(QTZ.t[:, :, :, :], 0.0), writes=[QTZ])
                for c_ in range(4):
                    S.dma("sp", UEs[c_].t[:, :, 0:3], conv0_d[:, c_], writes=[UEs[c_]])
                for c_ in range(8):
                    S.dma("sp", PHs[c_].t[:, :, :], pool0_d[:, c_], writes=[PHs[c_]])
            for tb in range(ntb):
                xs_ = xstage[tb % 2]
                if (ti, tb) not in xpref:
                    S.dma("sp", xs_[:], x_d[t0 + tb * 128:t0 + (tb + 1) * 128, :], writes=[xs_])
                for half in range(2):
                    bk = bank()
                    for jj in range(4):
                        c = half * 4 + jj
                        S.op("pe", lambda e, c=c, jj=jj, bk=bk, xs_=xs_: e.transpose(out=bk.t[:, jj * 128:(jj + 1) * 128], in_=xs_.t[:, c * 128:(c + 1) * 128], identity=ident), reads=[xs_, cst], writes=[bk])
                    eng = "act" if half == 0 else "dve"
                    S.op(eng, lambda e, half=half, bk=bk, tb=tb: (e.activation(out=XT.t[:, half * 4:half * 4 + 4, tb * 128:(tb + 1) * 128], in_=bk.t[:].rearrange("p (j t) -> p j t", j=4), func=AF.Copy) if half == 0 else
                                                                   e.tensor_copy(out=XT.t[:, half * 4:half * 4 + 4, tb * 128:(tb + 1) * 128], in_=bk.t[:].rearrange("p (j t) -> p j t", j=4))), reads=[bk], writes=[XT])

            ckpt(100 * ti + 1)
            norm_to_xn(TW, V_NM[0])
            xn_rhs = lambda c: XN.t[:, c, :TW]
            if ti == 0:
                dbg('xt0', XT, XT.t[:, :, :], [128, 8, T])
                pass
                dbg('rstd0', rstd, rstd.t[:, :], [128, T])

            ckpt(100 * ti + 2)
            if sample:
                tblocks = [(s_ * LS, LS, s_ * LS) for s_ in range(NS)]
            else:
                tblocks = [(tb * 128, 128, t0 + tb * 128) for tb in range(ntb)]
            k_out = ks_d if sample else kp_d
            v_out = vs_d if sample else vp_d
            hc = HCs if sample else HC
            gw3 = gwb.t[:].rearrange("p (k n) -> p k n", k=8)

            def v3(ap2):
                return ap2.rearrange("p (s l) -> p s l", s=nseg)

            def ue_of(c):
                ue = UEs[c] if sample else UE[c]
                return ue, (ue.t[:] if sample else ue.t[:].rearrange("p (s l) -> p s l", s=1))

            wu = wnext("KN")
            for c in range(4):
                ue, ue3 = ue_of(c)
                bku = proj_fm(wu, c, TW, xn_rhs, lambda cc: [XNB[cc]])
                S.op("act", lambda e, bku=bku, ue3=ue3: e.activation(out=ue3[:, :, 3:3 + L], in_=v3(bku.t[:, :TW]), func=AF.Copy), reads=[bku], writes=[ue])
            wg = wnext("KN")
            gts, g2s = [], []
            for c in range(4):
                bkg = proj_fm(wg, c, TW, xn_rhs, lambda cc: [XNB[cc]])
                gt = tmp()
                S.op("act", lambda e, bkg=bkg, gt=gt: e.activation(out=gt.t[:, :TW], in_=bkg.t[:, :TW], func=AF.Copy), reads=[bkg], writes=[gt])
                gts.append(gt)
            for c in range(4):
                gt = gts[c]
                g2 = tmp()
                g2s.append(g2)
                S.op("pool", lambda e, gt=gt, g2=g2: e.tensor_tensor(out=g2.t[:, :TW], in0=gt.t[:, :TW], in1=gt.t[:, :TW], op=ALU.mult), reads=[gt], writes=[g2])
                S.op("dve", lambda e, g2=g2: e.tensor_scalar(out=g2.t[:, :TW], in0=g2.t[:, :TW], scalar1=0.044715, scalar2=1.0, op0=ALU.mult, op1=ALU.add), reads=[g2], writes=[g2])
                S.op("pool", lambda e, gt=gt, g2=g2: e.tensor_tensor(out=g2.t[:, :TW], in0=g2.t[:, :TW], in1=gt.t[:, :TW], op=ALU.mult), reads=[g2, gt], writes=[g2])
            for c in range(4):
                g2 = g2s[c]
                S.op("act", lambda e, g2=g2: e.activation(out=g2.t[:, :TW], in_=g2.t[:, :TW], func=AF.Sigmoid, scale=1.5957691216057308), reads=[g2], writes=[g2])
            for c in range(4):
                gt, g2 = gts[c], g2s[c]
                S.op("dve", lambda e, gt=gt, g2=g2, c=c: e.tensor_tensor(out=ATT(c, cols), in0=gt.t[:, :TW], in1=g2.t[:, :TW], op=ALU.mult), reads=[gt, g2], writes=[HH])

            def piece_q():
                wq = wnext("KN")
                for j in range(4):
                    bk = proj_fm(wq, j, TW, xn_rhs, lambda c: [XNB[c]])
                    if sample:
                        S.op("act", lambda e, j=j, bk=bk: e.activation(out=QTZ.t[0:64, j, 0, :], in_=bk.t[0:64, :TW], func=AF.Copy, scale=0.125), reads=[bk], writes=[QTZ])
                        S.op("dve", lambda e, j=j, bk=bk: e.tensor_scalar(out=QTZ.t[64:128, j, 1, :], in0=bk.t[64:128, :TW], scalar1=0.125, scalar2=0.0, op0=ALU.mult, op1=ALU.add), reads=[bk], writes=[QTZ])
                    else:
                        S.op("act", lambda e, j=j, bk=bk: e.activation(out=QT(j, cols), in_=bk.t[:, :TW], func=AF.Copy, scale=0.125), reads=[bk], writes=[HH])

            kstate = {}

            def piece_kfm():
                wk = wnext("KN")
                kstate["wk"] = wk
                for j in range(4):
                    bk = proj_fm(wk, j, TW, xn_rhs, lambda c: [XNB[c]])
                    dst = KTs[j] if sample else KT[j][ti]
                    S.op("dve", lambda e, bk=bk, dst=dst: e.tensor_copy(out=dst.t[:, :TW], in_=bk.t[:, :TW]), reads=[bk], writes=[dst])

            def piece_ktm():
                wk = kstate["wk"]
                wk3 = wk.t[:].rearrange("p (c n) -> p c n", c=8)
                for bi, (c0, n, r0) in enumerate(tblocks):
                    bk = bank()
                    for c in range(8):
                        mm(bk.t[0:n, :], XN.t[:, c, c0:c0 + n], wk3[:, c, :], c == 0, c == 7, [XNB[c], wk], [bk])
                    st = kvst[bi % 2]
                    S.op("act", lambda e, bk=bk, st=st, n=n: e.activation(out=st.t[0:n, :], in_=bk.t[0:n, :], func=AF.Copy), reads=[bk], writes=[st])
                    S.dma("sp", k_out[r0:r0 + n, :], st.t[0:n, :], reads=[st])

            def piece_v():
                wv = wnext("KN")
                wv3 = wv.t[:].rearrange("p (c n) -> p c n", c=8)
                for bi, (c0, n, r0) in enumerate(tblocks):
                    bk = bank()
                    for c in range(8):
                        mm(bk.t[0:n, :], XN.t[:, c, c0:c0 + n], wv3[:, c, :], c == 0, c == 7, [XNB[c], wv], [bk])
                    st = kvst[(bi + 1) % 2]
                    S.op("act", lambda e, bk=bk, st=st, n=n: e.activation(out=st.t[0:n, :], in_=bk.t[0:n, :], func=AF.Copy), reads=[bk], writes=[st])
                    S.dma("sp", v_out[r0:r0 + n, :], st.t[0:n, :], reads=[st])
                    if sample:
                        S.op("dve", lambda e, bk=bk, bi=bi, n=n: e.tensor_copy(out=Vnew[bi].t[0:n, :], in_=bk.t[0:n, :]), reads=[bk], writes=[Vnew[bi]])
                    else:
                        S.op("dve", lambda e, bk=bk, bi=bi: e.tensor_copy(out=VV[ti].t[:, bi, :], in_=bk.t[:, :]), reads=[bk], writes=[VV[ti]])

            pieces = [piece_q, piece_kfm, piece_ktm, piece_v]

            def lruA(c):
                ue, ue3 = ue_of(c)
                uc = tmp()
                cw = lambda i: vecs.t[:, V_CW + 4 * i + c:V_CW + 4 * i + c + 1]
                S.op("dve", lambda e: e.tensor_scalar(out=v3(uc.t[:, :TW]), in0=ue3[:, :, 3:3 + L], scalar1=cw(3), scalar2=vecs.t[:, V_CB + c:V_CB + c + 1], op0=ALU.mult, op1=ALU.add), reads=[ue, vecs], writes=[uc])
                for i in (2, 1, 0):
                    S.op("dve", lambda e, i=i: e.scalar_tensor_tensor(out=v3(uc.t[:, :TW]), in0=ue3[:, :, i:i + L], scalar=cw(i), in1=v3(uc.t[:, :TW]), op0=ALU.mult, op1=ALU.add), reads=[ue, vecs, uc], writes=[uc])
                S.op("pool", lambda e: e.tensor_copy(out=ue3[:, :, 0:3], in_=ue3[:, :, L:L + 3]), reads=[ue], writes=[ue])
                ub = ucb[c % 2]
                S.op("act", lambda e: e.activation(out=ub.t[:, :TW], in_=uc.t[:, :TW], func=AF.Copy), reads=[uc], writes=[ub])
                return uc, ub

            def lruB(c, uc, ub):
                bkr = bank()
                mm(bkr.t[:, :TW], gw3[:, c, :], ub.t[:, :TW], True, True, [gwb, ub], [bkr])
                bki = bank()
                mm(bki.t[:, :TW], gw3[:, 4 + c, :], ub.t[:, :TW], True, True, [gwb, ub], [bki])
                rr = tmp()
                ii = tmp()
                S.op("act", lambda e: e.activation(out=rr.t[:, :TW], in_=bkr.t[:, :TW], func=AF.Sigmoid, bias=vecs.t[:, V_RGB + c:V_RGB + c + 1]), reads=[bkr, vecs], writes=[rr])
                S.op("act", lambda e: e.activation(out=ii.t[:, :TW], in_=bki.t[:, :TW], func=AF.Sigmoid, bias=vecs.t[:, V_IGB + c:V_IGB + c + 1]), reads=[bki, vecs], writes=[ii])
                a2 = tmp()
                aa = rr
                S.op("act", lambda e: e.activation(out=a2.t[:, :TW], in_=rr.t[:, :TW], func=AF.Exp, scale=cvec.t[:, 4 + c:5 + c]), reads=[rr, cvec], writes=[a2])
                S.op("act", lambda e: e.activation(out=aa.t[:, :TW], in_=rr.t[:, :TW], func=AF.Exp, scale=cvec.t[:, c:c + 1]), reads=[rr, cvec], writes=[aa])
                S.op("act", lambda e: e.activation(out=a2.t[:, :TW], in_=a2.t[:, :TW], func=AF.Ln, scale=-1.0, bias=1.0), reads=[a2], writes=[a2])
                S.op("act", lambda e: e.activation(out=a2.t[:, :TW], in_=a2.t[:, :TW], func=AF.Exp, scale=0.5), reads=[a2], writes=[a2])
                S.op("pool", lambda e: e.tensor_tensor(out=ii.t[:, :TW], in0=ii.t[:, :TW], in1=uc.t[:, :TW], op=ALU.mult), reads=[ii, uc], writes=[ii])
                S.op("dve", lambda e: e.tensor_tensor(out=ii.t[:, :TW], in0=ii.t[:, :TW], in1=a2.t[:, :TW], op=ALU.mult), reads=[ii, a2], writes=[ii])
                hh_ = a2
                for s_ in range(nseg):
                    init = hc.t[:, c, s_:s_ + 1] if sample else hc.t[:, c:c + 1]
                    S.op("dve", lambda e, s_=s_, init=init: e.tensor_tensor_scan(out=hh_.t[:, s_ * L:(s_ + 1) * L], data0=aa.t[:, s_ * L:(s_ + 1) * L], data1=ii.t[:, s_ * L:(s_ + 1) * L], initial=init, op0=ALU.mult, op1=ALU.add), reads=[aa, ii, hc], writes=[hh_])
                hdst = hc.t[:, c, :] if sample else hc.t[:, c:c + 1]
                S.op("pool", lambda e: e.tensor_copy(out=hdst, in_=v3(hh_.t[:, :TW])[:, :, L - 1]), reads=[hh_], writes=[hc])
                S.op("dve", lambda e: e.tensor_tensor(out=MIX(c, cols), in0=hh_.t[:, :TW], in1=ATT(c, cols), op=ALU.mult), reads=[hh_, HH], writes=[HH])

            for c in range(4):
                uc, ub = lruA(c)
                pieces[c]()
                lruB(c, uc, ub)

            ckpt(100 * ti + 4)
            if sample:
                attn_sample()
            else:
                attn_prompt(ti)

            ckpt(100 * ti + 5)
            for s_ in range(2):
                wo = wnext("KO")
                for j in range(4):
                    n = s_ * 4 + j
                    bk = proj_fm(wo, j, TW, lambda c: MIXALL(c, TW), [HH])
                    S.op("dve", lambda e, bk=bk, n=n: e.tensor_tensor(out=XT.t[:, n, :TW], in0=bk.t[:, :TW], in1=XT.t[:, n, :TW], op=ALU.add), reads=[bk, XT], writes=[XT])

            ckpt(100 * ti + 6)
            ffn(0, TW)
            ckpt(100 * ti + 7)
            pool_mixer(ti, TW, nseg, L, sample, first)
            ckpt(100 * ti + 8)
            if ti + 1 <= NT:
                nsample = (ti + 1 == NT)
                nx_d = xs_d if nsample else xp_d
                nt0 = 0 if nsample else (ti + 1) * T
                for tb in range(1 if nsample else 2):
                    S.dma("sp", xstage[tb][:], nx_d[nt0 + tb * 128:nt0 + (tb + 1) * 128, :], writes=[xstage[tb]])
                    xpref.add((ti + 1, tb))
            ffn(1, TW)
            ckpt(100 * ti + 9)
            norm_stats(TW)
            for c in range(8):
                norm_apply(c, V_NFIN, TW, XT.t[:, c, :TW], XT)
            y_d = ys_d if sample else yp_d
            for tb in range(ntb):
                ya, yb, y3 = tmp2()
                for half in range(2):
                    bk = bank()
                    for jj in range(4):
                        c = half * 4 + jj
                        S.op("pe", lambda e, c=c, jj=jj, bk=bk, tb=tb: e.transpose(out=bk.t[:, jj * 128:(jj + 1) * 128], in_=XT.t[:, c, tb * 128:(tb + 1) * 128], identity=ident), reads=[XT, cst], writes=[bk])
                    if half == 0:
                        S.op("act", lambda e, bk=bk, y3=y3: e.activation(out=y3[:, 0, :], in_=bk.t[:, :], func=AF.Copy), reads=[bk], writes=[ya])
                    else:
                        S.op("dve", lambda e, bk=bk, y3=y3: e.tensor_copy(out=y3[:, 1, :], in_=bk.t[:, :]), reads=[bk], writes=[yb])
                S.dma("sp", y_d[t0 + tb * 128:t0 + (tb + 1) * 128, :].rearrange("t (h m) -> t h m", h=2), y3, reads=[ya, yb])

        def MIXALL(c, TW):
            if c < 4:
                return HH.t[:, (8 + c) * T:(8 + c) * T + TW]
            return HH.t[:, c * T:c * T + TW]

        def ATT(j, cols):
            return HH.t[:, (8 + j) * T + cols.start:(8 + j) * T + cols.stop]

        def ffn(layer, TW):
            norm_to_xn(TW, V_NF[0] + 8 * layer)
            for half in range(2):
                slot = 0
                for i in range(6):
                    nf = 2 if i < 5 else 1
                    wb = wnext("GU" if nf == 2 else "GU1")
                    w3 = wb.t[:, 0:8 * 2 * nf * 128].rearrange("p (c n) -> p c n", c=8)
                    for f in range(nf):
                        bg = bank()
                        for c in range(8):
                            mm(bg.t[:, :TW], w3[:, c, f * 128:(f + 1) * 128], XN.t[:, c, :TW], c == 0, c == 7, [wb, XNB[c]], [bg])
                        bu = bank()
                        for c in range(8):
                            mm(bu.t[:, :TW], w3[:, c, (nf + f) * 128:(nf + f + 1) * 128], XN.t[:, c, :TW], c == 0, c == 7, [wb, XNB[c]], [bu])
                        sg = tmp()
                        S.op("act", lambda e, bg=bg, sg=sg: e.activation(out=sg.t[:, :TW], in_=bg.t[:, :TW], func=AF.Silu), reads=[bg], writes=[sg])
                        S.op("dve", lambda e, bu=bu, sg=sg, slot=slot: e.tensor_tensor(out=hh3[:, slot, :TW], in0=bu.t[:, :TW], in1=sg.t[:, :TW], op=ALU.mult), reads=[bu, sg], writes=[HH])
                        slot += 1
                for q in range(4):
                    wb = wnext("DN")
                    w3 = wb.t[:, 0:2816].rearrange("p (f n) -> p f n", f=11)
                    for o in range(2):
                        n = q * 2 + o
                        bk = bank()
                        for f in range(11):
                            mm(bk.t[:, :TW], w3[:, f, o * 128:(o + 1) * 128], hh3[:, f, :TW], f == 0, f == 10, [wb, HH], [bk])
                        S.op("dve", lambda e, bk=bk, n=n: e.tensor_tensor(out=XT.t[:, n, :TW], in0=bk.t[:, :TW], in1=XT.t[:, n, :TW], op=ALU.add), reads=[bk, XT], writes=[XT])

        def pool_mixer(ti, TW, nseg, L, sample, first):
            norm_stats(TW)
            wb = wnext("PW")
            w4 = wb.t[:, 0:2048].rearrange("p (g c n) -> p g c n", g=4, c=2)
            for g in range(4):
                win = POOL_WINDOWS[g]
                for ci in range(2):
                    c = 2 * g + ci
                    ph = PHs[c] if sample else PH[c]
                    ph3 = ph.t[:] if sample else ph.t[:].rearrange("p (s l) -> p s l", s=1)
                    W_ = 15 + L
                    ext = tmp()
                    ext3 = ext.t[:, 0:nseg * W_].rearrange("p (s l) -> p s l", s=nseg)
                    norm_apply(c, V_NM[1], TW, ext3[:, :, 15:15 + L], ext, in_ap=XT.t[:, c, :TW].rearrange("p (s l) -> p s l", s=nseg), rs_ap=rstd.t[:, :TW].rearrange("p (s l) -> p s l", s=nseg))
                    S.op("pool", lambda e: e.tensor_copy(out=ext3[:, :, 0:15], in_=ph3), reads=[ph], writes=[ext])
                    S.op("pool", lambda e: e.tensor_copy(out=ph3, in_=ext3[:, :, L:L + 15]), reads=[ext], writes=[ph])
                    cur = ext
                    cur3 = ext3
                    sh = 1
                    while sh < win:
                        nxt = tmp()
                        nxt3 = nxt.t[:, 0:nseg * W_].rearrange("p (s l) -> p s l", s=nseg)
                        lo = 2 * sh - 1
                        eng = "pool" if sh in (1, 4) else "dve"
                        S.op(eng, lambda e, cur3=cur3, nxt3=nxt3, lo=lo, sh=sh: e.tensor_tensor(out=nxt3[:, :, lo:W_], in0=cur3[:, :, lo:W_], in1=cur3[:, :, lo - sh:W_ - sh], op=ALU.add), reads=[cur], writes=[nxt])
                        cur, cur3 = nxt, nxt3
                        sh *= 2
                    db = dB[ci]
                    S.op("dve", lambda e, cur3=cur3, db=db: e.scalar_tensor_tensor(out=db.t[:, :TW].rearrange("p (s l) -> p s l", s=nseg), in0=cur3[:, :, 15:15 + L], scalar=1.0 / win, in1=ext3[:, :, 15:15 + L], op0=ALU.mult, op1=ALU.subtract),
                         reads=[cur, ext], writes=[db])
                    if first:
                        t16 = tmp()
                        S.op("dve", lambda e, cur=cur, t16=t16: e.tensor_tensor(out=t16.t[:, 0:16], in0=cur.t[:, 15:31], in1=cst.t[:, C_INV + 16 * g:C_INV + 16 * g + 16], op=ALU.mult), reads=[cur, cst], writes=[t16])
                        S.op("dve", lambda e, t16=t16, db=db: e.tensor_tensor(out=db.t[:, 0:16], in0=t16.t[:, 0:16], in1=ext.t[:, 15:31], op=ALU.subtract), reads=[t16, ext, db], writes=[db])
                for o in range(2):
                    n = 2 * g + o
                    bk = bank()
                    for ci in range(2):
                        mm(bk.t[:, :TW], w4[:, g, ci, o * 128:(o + 1) * 128], dB[ci].t[:, :TW], ci == 0, ci == 1, [wb, dB[ci]], [bk])
                    S.op("dve", lambda e, bk=bk, n=n: e.scalar_tensor_tensor(out=XT.t[:, n, :TW], in0=bk.t[:, :TW], scalar=vecs.t[:, V_PS + n:V_PS + n + 1], in1=XT.t[:, n, :TW], op0=ALU.mult, op1=ALU.add),
                         reads=[bk, vecs, XT], writes=[XT])

        rot = {"l": 0, "w": 0}

        def attn_prompt(g):
            A2 = P2[0][:, :].rearrange("p (s t) -> p s t", s=2)
            C2 = P2[1][:, :].rearrange("p (s t) -> p s t", s=2)
            Ab, Cb = [banks[0], banks[1]], [banks[2], banks[3]]
            Oall = [[banks[4], banks[5]], [banks[6], banks[7]]]
            nst = 4 * g + 4
            seq = [(hp, st) for hp in range(4) for st in range(nst)]
            info = {}

            def geom(st):
                kb = nst - 1 - st
                jj = kb - 4 * g
                c0 = 128 * jj if jj > 0 else 0
                return kb, jj, c0, kb // 4, kb % 4

            def emitA(x):
                hp, st = seq[x]
                kb, jj, c0, tt, bi = geom(st)
                kt = KT[hp][tt]
                for s_ in range(2):
                    mm(A2[:, s_, c0:T], kt.t[64 * s_:64 * s_ + 64, bi * 128:(bi + 1) * 128], HH.t[64 * s_:64 * s_ + 64, hp * T + c0:hp * T + T], True, True, [kt, HH], [Ab[s_]])

            def S1(x):
                hp, st = seq[x]
                kb, jj, c0, tt, bi = geom(st)
                ea, eb, e3 = tmp2()
                S.op("act", lambda e: e.activation(out=e3[:, :, c0:T], in_=A2[:, :, c0:T], func=AF.Exp), reads=Ab, writes=[ea, eb])
                if jj >= 0:
                    S.op("pool", lambda e: e.tensor_tensor(out=e3[:, :, c0:c0 + 128], in0=e3[:, :, c0:c0 + 128], in1=mask2, op=ALU.mult), reads=[ea, eb, cst], writes=[ea, eb])
                k_ = rot["l"] % 3
                rot["l"] += 1
                S.op("act", lambda e: e.activation(out=LB[k_].t[:, :, c0:T], in_=e3[:, :, c0:T], func=AF.Ln, bias=1.0), reads=[ea, eb], writes=[LB[k_]])
                info[x] = (hp, st, c0, tt, bi, ea, eb, e3, k_)

            def S2(x):
                hp, st, c0, tt, bi, ea, eb, e3, k_ = info[x]
                for s_ in range(2):
                    mm(C2[:, s_, c0:T], trib.t[:, 0:128], LB[k_].t[:, s_, c0:T], st == 0, False, [trib, LB[k_]], [Cb[s_]])

            def S2p(x):
                hp, st, c0, tt, bi, ea, eb, e3, k_ = info[x]
                pa, pb, p3 = tmp2()
                S.op("act", lambda e: e.activation(out=p3[:, :, c0:T], in_=C2[:, :, c0:T], func=AF.Exp), reads=Cb, writes=[pa, pb])
                info[x] = info[x] + (pa, pb, p3)

            def S3a(x):
                hp, st, c0, tt, bi, ea, eb, e3, k_, pa, pb, p3 = info[x]
                if st < nst - 1:
                    for s_ in range(2):
                        mm(C2[:, s_, c0:T], trib.t[:, 128:256], LB[k_].t[:, s_, c0:T], False, False, [trib, LB[k_]], [Cb[s_]])

            def S3b(x):
                hp, st, c0, tt, bi, ea, eb, e3, k_, pa, pb, p3 = info[x]
                O = Oall[hp % 2]
                w_ = WB[rot["w"] % 2]
                rot["w"] += 1
                S.op("dve", lambda e: e.tensor_tensor(out=w_.t[:, :, c0:T], in0=e3[:, :, c0:T], in1=p3[:, :, c0:T], op=ALU.mult), reads=[ea, eb, pa, pb], writes=[w_])
                for s_ in range(2):
                    mm(O[s_].t[:, c0:T], VV[tt].t[:, bi, hp * 128:(hp + 1) * 128], w_.t[:, s_, c0:T], st == 0, st == nst - 1, [VV[tt], w_], [O[s_]])
                if st == nst - 1:
                    S.op("act", lambda e: e.activation(out=ATT(hp, slice(0, T))[0:64, :], in_=O[0].t[0:64, :], func=AF.Copy), reads=[O[0]], writes=[HH])
                    S.op("dve", lambda e: e.tensor_copy(out=ATT(hp, slice(0, T))[64:128, :], in_=O[1].t[64:128, :]), reads=[O[1]], writes=[HH])
                del info[x]

            N = len(seq)
            emitA(0)
            for n in range(N + 2):
                if n < N:
                    S1(n)
                if n >= 2:
                    S3a(n - 2)
                if 1 <= n < N + 1:
                    S2(n - 1)
                if n + 1 < N:
                    emitA(n + 1)
                if 1 <= n < N + 1:
                    S2p(n - 1)
                if n >= 2:
                    S3b(n - 2)

        def attn_sample():
            NQ = 8 * LS
            nst = NPB + 1
            rotc = {"k": 0, "v": 0}
            for sp_ in range(NS // 2):
                slots = []
                for st in range(nst):
                    for s in range(2):
                        slots.append((s, st))
                info = {}
                A = [banks[0], banks[1]]
                ACC = [banks[2], banks[3]]
                O = [banks[4], banks[5]]
                TR = [banks[6], banks[7]]
                pre = {}

                def load(n):
                    s, st = slots[n]
                    kb = NPB - st
                    if kb == NPB:
                        return
                    seq = 2 * sp_ + s
                    kf = tmp()
                    vf = tmp()
                    S.dma("sp", kf.t[:, 0:512], ck_d[seq, kb * 128:(kb + 1) * 128, :], writes=[kf])
                    S.dma("sp", vf.t[:, 0:512], cv_d[seq, kb * 128:(kb + 1) * 128, :], writes=[vf])
                    pre[n] = (kf, vf)

                ktmap = {}

                def S0(n):
                    s, st = slots[n]
                    if NPB - st == NPB:
                        return
                    kf, vf = pre.pop(n)
                    for j in range(4):
                        S.op("pe", lambda e, j=j: e.transpose(out=TR[s].t[:, j * 128:(j + 1) * 128], in_=kf.t[:, j * 128:(j + 1) * 128], identity=ident), reads=[kf, cst], writes=[TR[s]])
                    ktb = KTb[rotc["k"] % 2]
                    rotc["k"] += 1
                    S.op("dve", lambda e: e.tensor_copy(out=ktb.t[:, :, :], in_=TR[s].t[:, :].rearrange("p (j k) -> p j k", j=4)), reads=[TR[s]], writes=[ktb])
                    ktmap[n] = (ktb, vf)

                def S1(n):
                    s, st = slots[n]
                    seq = 2 * sp_ + s
                    kb = NPB - st
                    new = kb == NPB
                    rows = 128
                    qc = slice(seq * LS, (seq + 1) * LS)
                    if new:
                        vsrc = Vnew[seq]
                        for j in range(4):
                            mm(A[s].t[0:LS, j * 2 * LS:(j + 1) * 2 * LS].rearrange("p (i q) -> p i q", i=2), KTs[j].t[:, qc], QTZ.t[:, j, :, qc], True, True, [KTs[j], QTZ], [A[s]])
                    else:
                        ktb, vf = ktmap.pop(n)
                        vsrc = Vb[rotc["v"] % 3]
                        rotc["v"] += 1
                        if s == 0:
                            S.op("dve", lambda e: e.tensor_copy(out=vsrc.t[:, :], in_=vf.t[:, 0:512]), reads=[vf], writes=[vsrc])
                        else:
                            S.op("act", lambda e: e.activation(out=vsrc.t[:, :], in_=vf.t[:, 0:512], func=AF.Copy), reads=[vf], writes=[vsrc])
                        for j in range(4):
                            mm(A[s].t[:, j * 2 * LS:(j + 1) * 2 * LS].rearrange("p (i q) -> p i q", i=2), ktb.t[:, j, :], QTZ.t[:, j, :, qc], True, True, [ktb, QTZ], [A[s]])
                    e_ = tmp()
                    if new:
                        S.op("pool", lambda e: e.memset(e_.t[:, 0:NQ], 0.0), writes=[e_])
                        S.op("act", lambda e: e.activation(out=e_.t[0:LS, 0:NQ], in_=A[s].t[0:LS, 0:NQ], func=AF.Exp), reads=[A[s]], writes=[e_])
                        S.op("pool", lambda e: e.tensor_tensor(out=e_.t[0:LS, 0:NQ], in0=e_.t[0:LS, 0:NQ], in1=m32, op=ALU.mult), reads=[e_, cst], writes=[e_])
                    else:
                        S.op("act", lambda e: e.activation(out=e_.t[0:rows, 0:NQ], in_=A[s].t[0:rows, 0:NQ], func=AF.Exp), reads=[A[s]], writes=[e_])
                    l_ = lbuf[rot["l"] % 3]
                    rot["l"] += 1
                    S.op("act", lambda e: e.activation(out=l_.t[0:rows, 0:NQ], in_=e_.t[0:rows, 0:NQ], func=AF.Ln, bias=1.0), reads=[e_], writes=[l_])
                    info[n] = (s, st, seq, rows, e_, l_, vsrc)

                def S2(n):
                    s, st, seq, rows, e_, l_, vsrc = info[n]
                    mm(ACC[s].t[:, 0:NQ], trib.t[0:rows, 0:128], l_.t[0:rows, 0:NQ], st == 0, False, [trib, l_], [ACC[s]])
                    p_ = tmp()
                    S.op("act", lambda e: e.activation(out=p_.t[0:rows, 0:NQ], in_=ACC[s].t[0:rows, 0:NQ], func=AF.Exp), reads=[ACC[s]], writes=[p_])
                    info[n] = info[n] + (p_,)

                def S3(n):
                    s, st, seq, rows, e_, l_, vsrc, p_ = info[n]
                    if st < nst - 1:
                        mm(ACC[s].t[:, 0:NQ], trib.t[0:rows, 128:256], l_.t[0:rows, 0:NQ], False, False, [trib, l_], [ACC[s]])
                    w_ = wbuf[rot["w"] % 3]
                    rot["w"] += 1
                    S.op("dve", lambda e: e.tensor_tensor(out=w_.t[0:rows, 0:NQ], in0=e_.t[0:rows, 0:NQ], in1=p_.t[0:rows, 0:NQ], op=ALU.mult), reads=[e_, p_], writes=[w_])
                    for j in range(4):
                        mm(O[s].t[:, j * 2 * LS:(j + 1) * 2 * LS], vsrc.t[0:rows, j * 128:(j + 1) * 128], w_.t[0:rows, j * 2 * LS:(j + 1) * 2 * LS], st == 0 and j == 0, st == nst - 1, [vsrc, w_], [O[s]])
                    if st == nst - 1:
                        qc = slice(seq * LS, (seq + 1) * LS)
                        for h in range(8):
                            j, i = h // 2, h % 2
                            eng = "act" if h % 2 == 0 else "dve"
                            if eng == "act":
                                S.op("act", lambda e, h=h, j=j, i=i: e.activation(out=ATT(j, qc)[64 * i:64 * i + 64, :], in_=O[s].t[64 * i:64 * i + 64, h * LS:(h + 1) * LS], func=AF.Copy), reads=[O[s]], writes=[HH])
                            else:
                                S.op("dve", lambda e, h=h, j=j, i=i: e.tensor_copy(out=ATT(j, qc)[64 * i:64 * i + 64, :], in_=O[s].t[64 * i:64 * i + 64, h * LS:(h + 1) * LS]), reads=[O[s]], writes=[HH])
                    del info[n]

                N = len(slots)
                for n in range(min(3, N)):
                    load(n)
                S0(0)
                for n in range(N + 2):
                    if n + 1 < N:
                        S0(n + 1)
                    if n < N:
                        S1(n)
                    if n + 3 < N:
                        load(n + 3)
                    if 1 <= n < N + 1:
                        S2(n - 1)
                    if n >= 2:
                        S3(n - 2)

        try:
            ckpt(0)
            for ti in range(NT + 1):
                process_tile(ti)
                if ti == 0:
                    S.barrier()
                ckpt(10 + ti)
        except StopBuild:
            pass

        with nc.allow_non_contiguous_dma(reason="small state outputs"):
            S.dma("sp", hp_d.rearrange("o (c p) -> p (o c)", p=128), HC[:], reads=[HC])
            for s in range(NS):
                S.dma("sp", hs_d[s:s + 1, :].rearrange("o (c p) -> p (o c)", p=128), HCs.t[:, :, s], reads=[HCs])
            for c in range(4):
                S.dma("sp", convp_d[:, c * 128:(c + 1) * 128].rearrange("r p -> p r"), UE[c].t[:, 0:3], reads=[UE[c]])
                for s in range(NS):
                    S.dma("sp", convs_d[s * 3:(s + 1) * 3, c * 128:(c + 1) * 128].rearrange("r p -> p r"), UEs[c].t[:, s, 0:3], reads=[UEs[c]])
        def pool_out(srcs, dst):
            bks = [bank(), bank()]
            for c in range(8):
                bk = bks[c // 4]
                S.op("pe", lambda e, c=c, bk=bk: e.transpose(out=bk.t[0:15, (c % 4) * 128:(c % 4 + 1) * 128], in_=srcs[c], identity=ident), reads=[cst] + list(PH) + list(PHs), writes=[bk])
            S.op("act", lambda e: e.activation(out=ost.t[0:15, 0:512], in_=bks[0].t[0:15, :], func=AF.Copy), reads=[bks[0]], writes=[ost])
            S.op("dve", lambda e: e.tensor_copy(out=ost.t[0:15, 512:1024], in_=bks[1].t[0:15, :]), reads=[bks[1]], writes=[ost])
            S.dma("sp", dst, ost.t[0:15, :], reads=[ost])

        pool_out([PH[c].t[:, :] for c in range(8)], poolp_d[:, :])
        for s in range(NS):
            pool_out([PHs[c].t[:, s, :] for c in range(8)], pools_d[s * 15:(s + 1) * 15, :])
        for q in ("sp", "act"):
            for i, v in enumerate(S.dcount[q]):
                if v > 0:
                    S._wait("sp", ("d" + q, i), v)
        build.stats = dict(S.count, waits=S.nwaits)
    return nc


def run(inputs, n_cores, SEQ, NS, LS, PAST):
    f = lambda k: np.asarray(inputs[k], dtype=np.float32)
    xp, xs = f("x_prompt"), f("x_sample")
    ck, cv = f("cache_sb_k")[0], f("cache_sb_v")[0]
    h0, conv0, pool0 = f("state_lru_h")[0], f("state_lru_conv")[0], f("state_pool")[0]
    wsrc = host_chunks(f("hyb_w_in")[0], f("hyb_w_out")[0], f("ffn_gate"), f("ffn_up"), f("ffn_down"), f("pool_w")[0])
    vecs = np.zeros((128, NV), np.float32)
    nm, nf = f("norm_mix"), f("norm_ffn")
    vecs[:, 0:8] = fm(nm[0]); vecs[:, 8:16] = fm(nm[1]); vecs[:, 16:24] = fm(nf[0]); vecs[:, 24:32] = fm(nf[1])
    vecs[:, 32:40] = fm(f("norm_final")); vecs[:, 40:48] = fm(f("pool_scale")[0])
    cw = f("hyb_conv_w")[0]
    for i in range(4):
        vecs[:, 48 + 4 * i:52 + 4 * i] = fm(cw[i])
    vecs[:, 64:68] = fm(f("hyb_conv_b")[0]); vecs[:, 68:72] = fm(f("hyb_rg_b")[0]); vecs[:, 72:76] = fm(f("hyb_ig_b")[0]); vecs[:, 76:80] = fm(f("hyb_lambda")[0])
    gatew = np.zeros((128, 8, 128), np.float32)
    rg, ig = f("hyb_rg_w")[0], f("hyb_ig_w")[0]
    for c in range(4):
        for kk, wsel in enumerate((rg, ig)):
            gatew[0:64, 4 * kk + c, 0:64] = wsel[2 * c]
            gatew[64:128, 4 * kk + c, 64:128] = wsel[2 * c + 1]
    gatew = gatew.reshape(128, 1024)
    cst = host_consts()
    in_maps = []
    for r in range(n_cores):
        sl = slice(r * NS, (r + 1) * NS)
        in_maps.append({
            "xp": np.ascontiguousarray(xp[r]),
            "xs": np.ascontiguousarray(xs[sl].reshape(NS * LS, D)),
            "ck": np.ascontiguousarray(ck[sl].reshape(NS, PAST, 512)),
            "cv": np.ascontiguousarray(cv[sl].reshape(NS, PAST, 512)),
            "h0": np.ascontiguousarray(h0[sl].reshape(NS, 4, 128).transpose(2, 1, 0)),
            "conv0": np.ascontiguousarray(conv0[sl].reshape(NS, 3, 4, 128).transpose(3, 2, 0, 1)),
            "pool0": np.ascontiguousarray(pool0[sl].reshape(NS, 15, 8, 128).transpose(3, 2, 0, 1)),
            "wsrc": wsrc, "vecs": vecs, "gatew": gatew, "cst": cst,
        })
    nc = build(SEQ, NS, LS, PAST)
    res = run_bass_kernel_spmd(nc, in_maps, core_ids=list(range(n_cores)))
    R = res.results
    cat = lambda k: np.stack([np.asarray(R[r][k], dtype=np.float32) for r in range(n_cores)])
    y_p = cat("yp")
    y_s = cat("ys").reshape(n_cores * NS, LS, D)
    k_p = cat("kp").reshape(1, n_cores, SEQ, 8, 64)
    v_p = cat("vp").reshape(1, n_cores, SEQ, 8, 64)
    h_p = cat("hp").reshape(1, n_cores, 512)
    conv_p = cat("convp").reshape(1, n_cores, 3, 512)
    pool_p = cat("poolp").reshape(1, n_cores, 15, D)
    k_s = cat("ks").reshape(1, n_cores * NS, LS, 8, 64)
    v_s = cat("vs").reshape(1, n_cores * NS, LS, 8, 64)
    h_s = cat("hs").reshape(1, n_cores * NS, 512)
    conv_s = cat("convs").reshape(1, n_cores * NS, 3, 512)
    pool_s = cat("pools").reshape(1, n_cores * NS, 15, D)
    if DEBUG:
        run.dbg = {k: np.asarray(v) for k, v in R[0].items() if k.startswith('dbg_')}
    return (y_p, y_s, k_p, v_p, h_p, conv_p, pool_p, k_s, v_s, h_s, conv_s, pool_s)


def kernel(**inputs):
    return run(inputs, 8, 4096, 4, 32, 4096)
```

```python
import numpy as np
from contextlib import ExitStack
import concourse.bass as bass
import concourse.mybir as mybir
from concourse.bass_utils import run_bass_kernel_spmd

F32 = mybir.dt.float32
BF16 = mybir.dt.bfloat16
AF = mybir.ActivationFunctionType
ALU = mybir.AluOpType

D = 1024
DFF = 2816
NFC = 22
POOL_WINDOWS = (2, 4, 8, 16)
EPS = 1e-6
EPOCH = 30000
DEBUG = False
STOP = None


class StopBuild(Exception):
    pass


def ckpt(k):
    if STOP is not None and STOP == k:
        raise StopBuild()
WSLOT = 4096

def chunk_list():
    L = []
    for s in (3, 4, 0, 1, 2):
        L.append(("KN", s, 0, 4096))
    for s in range(2):
        L.append(("KO", s, 0, 4096))

    def ffn(layer):
        for half in range(2):
            f0 = half * 11
            for i in range(5):
                L.append(("GU", layer, f0 + 2 * i, 4096))
            L.append(("GU1", layer, f0 + 10, 2048))
            for q in range(4):
                L.append(("DN", layer, half * 4 + q, 2816))
    ffn(0)
    L.append(("PW", 0, 0, 2048))
    ffn(1)
    return L


CHUNKS = chunk_list()
NCH = len(CHUNKS)


def host_chunks(w_in, w_out, gate, up, down, pool_w):
    out = np.zeros((NCH, 128, WSLOT), np.float32)
    for i, (k, a, b, nel) in enumerate(CHUNKS):
        if k == "KN":
            out[i] = w_in[:, a * 512:(a + 1) * 512].reshape(8, 128, 512).transpose(1, 0, 2).reshape(128, 4096)
        elif k == "KO":
            out[i] = w_out[:, a * 512:(a + 1) * 512].reshape(8, 128, 512).transpose(1, 0, 2).reshape(128, 4096)
        elif k == "GU":
            g = gate[a][:, b * 128:(b + 2) * 128].reshape(8, 128, 256)
            u = up[a][:, b * 128:(b + 2) * 128].reshape(8, 128, 256)
            out[i] = np.concatenate([g, u], axis=2).transpose(1, 0, 2).reshape(128, 4096)
        elif k == "GU1":
            g = gate[a][:, b * 128:(b + 1) * 128].reshape(8, 128, 128)
            u = up[a][:, b * 128:(b + 1) * 128].reshape(8, 128, 128)
            out[i, :, :2048] = np.concatenate([g, u], axis=2).transpose(1, 0, 2).reshape(128, 2048)
        elif k == "DN":
            half, q = b // 4, b % 4
            blk = down[a][half * 1408:(half + 1) * 1408, q * 256:(q + 1) * 256]
            out[i, :, :2816] = blk.reshape(11, 128, 256).transpose(1, 0, 2).reshape(128, 2816)
        elif k == "PW":
            out[i, :, :2048] = pool_w.reshape(4, 2, 128, 256).transpose(2, 0, 1, 3).reshape(128, 2048)
    return out


V_NM = (0, 8)
V_NF = (16, 24)
V_NFIN = 32
V_PS = 40
V_CW = 48
V_CB = 64
V_RGB = 68
V_IGB = 72
V_LAM = 76
NV = 80
C_ID = 0
C_TRI = 128
C_MASK = 384
C_ONES = 512
C_INV = 640
C_M32 = 704
C_MASK2 = 960
NCST = 1216


def host_consts():
    c = np.zeros((128, NCST), np.float32)
    c[:, C_ID:C_ID + 128] = np.eye(128, dtype=np.float32)
    j = np.arange(128)[:, None]
    k = np.arange(128)[None, :]
    c[:, C_TRI:C_TRI + 128] = -1.0 * (j >= k)
    c[:, C_TRI + 128:C_TRI + 256] = -1.0 * (j < k)
    c[:, C_MASK:C_MASK + 128] = (j < k)
    c[:, C_ONES:C_ONES + 128] = 1.0 / D
    c[:, C_MASK2:C_MASK2 + 128] = (j < k)
    c[:, C_MASK2 + 128:C_MASK2 + 256] = (j < k)
    for g, w in enumerate(POOL_WINDOWS):
        c[:, C_INV + 16 * g:C_INV + 16 * (g + 1)] = 1.0 / np.minimum(w, np.arange(16) + 1.0)
    m32 = (np.arange(32)[:, None] < np.arange(32)[None, :]).astype(np.float32)
    c[:32, C_M32:C_M32 + 256] = np.tile(m32, (1, 8))
    return c


def fm(v):
    return np.ascontiguousarray(v.reshape(-1, 128).T)


class Buf:
    __slots__ = ("t", "writer", "readers", "name")

    def __init__(self, t, name=""):
        self.t = t
        self.writer = None
        self.readers = {}
        self.name = name

    def __getitem__(self, idx):
        return self.t[idx]


class Sched:
    def __init__(self, nc, es, n_epochs=6, n_dma_sems=12):
        self.nc = nc
        self.eng = {"pe": nc.tensor, "act": nc.scalar, "dve": nc.vector, "pool": nc.gpsimd, "sp": nc.sync}
        self.sems = {}
        self.count = {}
        self.waited = {k: {} for k in self.eng}
        for k in self.eng:
            self.count[k] = 0
            if k == "sp":
                continue
            self.sems[k] = [es.enter_context(nc.semaphore(name=f"s_{k}{i}")) for i in range(n_epochs)]
        self.dsems = {}
        self.dcount = {}
        self.dnext = {}
        for q in ("sp", "act"):
            self.dsems[q] = [es.enter_context(nc.semaphore(name=f"d_{q}{i}")) for i in range(n_dma_sems)]
            self.dcount[q] = [0] * n_dma_sems
            self.dnext[q] = 0
        self.semobj = {}
        for k, lst in self.sems.items():
            for i, s in enumerate(lst):
                self.semobj[(k, i)] = s
        for q, lst in self.dsems.items():
            for i, s in enumerate(lst):
                self.semobj[("d" + q, i)] = s
        self.nwaits = 0

    def _wait(self, engname, key, val):
        w = self.waited[engname]
        if w.get(key, 0) >= val:
            return
        self.eng[engname].wait_ge(self.semobj[key], val)
        w[key] = val
        self.nwaits += 1

    def _deps(self, engname, reads, writes):
        need = {}
        for b in reads:
            if b.writer is not None:
                k, v = b.writer
                if need.get(k, 0) < v:
                    need[k] = v
        for b in writes:
            if b.writer is not None:
                k, v = b.writer
                if need.get(k, 0) < v:
                    need[k] = v
            for k, v in b.readers.items():
                if need.get(k, 0) < v:
                    need[k] = v
        for k, v in need.items():
            if engname == "pe" and k[0] == "pe":
                continue
            self._wait(engname, k, v)

    def _mark(self, tok, reads, writes):
        k, v = tok
        for b in reads:
            if b.readers.get(k, 0) < v:
                b.readers[k] = v
        for b in writes:
            b.writer = tok
            b.readers = {}

    def op(self, engname, fn, reads=(), writes=()):
        self._deps(engname, reads, writes)
        inst = fn(self.eng[engname])
        c = self.count[engname]
        ep, v = c // EPOCH, c % EPOCH + 1
        inst.then_inc(self.sems[engname][ep], 1)
        self.count[engname] = c + 1
        tok = ((engname, ep), v)
        self._mark(tok, reads, writes)
        return tok

    def dma(self, q, out, in_, reads=(), writes=(), **kw):
        self._deps(q, reads, writes)
        i = self.dnext[q]
        self.dnext[q] = (i + 1) % len(self.dsems[q])
        key = ("d" + q, i)
        prev = self.dcount[q][i]
        if prev > 0:
            self._wait(q, key, prev)
        inst = self.eng[q].dma_start(out=out, in_=in_, **kw)
        val = prev + 16
        inst.then_inc(self.dsems[q][i], 16)
        self.dcount[q][i] = val
        tok = (key, val)
        self._mark(tok, reads, writes)
        return tok

    def barrier(self):
        toks = {}
        for k in ("pe", "act", "dve", "pool"):
            c = self.count[k]
            if c > 0:
                toks[(k, (c - 1) // EPOCH)] = (c - 1) % EPOCH + 1
        for q in ("sp", "act"):
            for i, v in enumerate(self.dcount[q]):
                if v > 0:
                    toks[("d" + q, i)] = v
        for e in ("pe", "act", "dve", "pool", "sp"):
            for key, v in toks.items():
                if key[0] == e:
                    continue
                self._wait(e, key, v)

    def finish(self, bufs):
        for b in bufs:
            if b.writer is not None:
                self._wait("sp", b.writer[0], b.writer[1])


def build(SEQ, NS, LS, PAST):
    T = 512
    NT = SEQ // T
    TS = NS * LS
    NPB = PAST // 128
    nc = bass.Bass("TRN2", target_bir_lowering=False)

    def din(name, shape, dt=F32):
        return nc.dram_tensor(name, list(shape), dt, kind="ExternalInput").ap()

    def dout(name, shape):
        return nc.dram_tensor(name, list(shape), F32, kind="ExternalOutput").ap()

    xp_d = din("xp", (SEQ, D))
    xs_d = din("xs", (TS, D))
    ck_d = din("ck", (NS, PAST, 512))
    cv_d = din("cv", (NS, PAST, 512))
    h0_d = din("h0", (128, 4, NS))
    conv0_d = din("conv0", (128, 4, NS, 3))
    pool0_d = din("pool0", (128, 8, NS, 15))
    wsrc_d = din("wsrc", (NCH, 128, WSLOT))
    vecs_d = din("vecs", (128, NV))
    gatew_d = din("gatew", (128, 8 * 128))
    cst_d = din("cst", (128, NCST))
    yp_d = dout("yp", (SEQ, D))
    ys_d = dout("ys", (TS, D))
    kp_d = dout("kp", (SEQ, 512))
    vp_d = dout("vp", (SEQ, 512))
    hp_d = dout("hp", (1, 512))
    convp_d = dout("convp", (3, 512))
    poolp_d = dout("poolp", (15, D))
    ks_d = dout("ks", (TS, 512))
    vs_d = dout("vs", (TS, 512))
    hs_d = dout("hs", (NS, 512))
    convs_d = dout("convs", (NS * 3, 512))
    pools_d = dout("pools", (NS * 15, D))
    wscr_d = nc.dram_tensor("wscr", [NCH, 128, WSLOT], BF16, kind="Internal").ap()

    es = ExitStack()
    with es:
        S = Sched(nc, es)
        cnt = [0]
        dbg_list = []

        def dbg(name, buf, ap, shape, dt=F32):
            if not DEBUG:
                return
            d = nc.dram_tensor('dbg_' + name, list(shape), dt, kind='ExternalOutput').ap()
            S.dma('sp', d, ap, reads=[buf])
            dbg_list.append(name)

        def sb(shape, dt, name=None):
            cnt[0] += 1
            nm = "sb_" + (name or f"t{cnt[0]}")
            return Buf(es.enter_context(nc.sbuf_tensor(nm, list(shape), dt)), nm)

        P2 = [es.enter_context(nc.psum_tensor(f"pbank{i}", [128, 1024], F32)) for i in range(4)]
        banks = [Buf(P2[i // 2][:, (i % 2) * 512:(i % 2 + 1) * 512], f"bank{i}") for i in range(8)]
        bank_rr = [0]

        def bank():
            b = banks[bank_rr[0] % 8]
            bank_rr[0] += 1
            return b

        KT = [[sb([128, T], BF16) for _ in range(NT)] for _ in range(4)]
        VV = [sb([128, 4, 512], BF16) for _ in range(max(NT, 8))]
        XT = sb([128, 8, T], F32, "XT")
        xstage = [sb([128, D], F32) for _ in range(2)]
        XN = sb([128, 8, T], BF16, "XN")
        XNB = [Buf(XN.t[:, c_, :], f"xn{c_}") for c_ in range(8)]
        HH = sb([128, 12 * T], BF16, "HH")
        hh3 = HH.t[:].rearrange("p (f t) -> p f t", t=T)

        def QT(j, cols):
            return HH.t[:, j * T + cols.start:j * T + cols.stop]

        def MIX(c, cols):
            return HH.t[:, (4 + c) * T + cols.start:(4 + c) * T + cols.stop]

        UE = [sb([128, 3 + T], F32) for _ in range(4)]
        PH = [sb([128, 15], F32) for _ in range(8)]
        HC = sb([128, 4], F32, "HC")
        HCs = sb([128, 4, NS], F32, "HCs")
        NTMP = 12
        TB = [es.enter_context(nc.sbuf_tensor(f"sb_tb{k_}", [128, 1088], F32)) for k_ in range(NTMP // 2)]
        tmps = [Buf(TB[k_ // 2][:, (k_ % 2) * 544:(k_ % 2 + 1) * 544], f"tmp{k_}") for k_ in range(NTMP)]

        def tmp2():
            if tmp_rr[0] % 2 == 1:
                tmp_rr[0] += 1
            k_ = (tmp_rr[0] % NTMP) // 2
            a_, b_ = tmps[2 * k_], tmps[2 * k_ + 1]
            tmp_rr[0] += 2
            return a_, b_, TB[k_][:, :].rearrange("p (s l) -> p s l", s=2)[:, :, 0:T]
        tmp_rr = [0]

        def tmp():
            b = tmps[tmp_rr[0] % NTMP]
            tmp_rr[0] += 1
            return b

        LB = [sb([128, 2, T], BF16) for _ in range(3)]
        lbuf = [Buf(LB[k_].t[:, 0, :], f"lb{k_}") for k_ in range(3)]
        WB = [sb([128, 2, T], BF16) for _ in range(2)]
        wbuf = [Buf(WB[k_ % 2].t[:, k_ // 2, :], f"wb{k_}") for k_ in range(3)]
        ucb = [sb([128, T], BF16) for _ in range(2)]
        dB = ucb
        kvst = [sb([128, 512], F32) for _ in range(2)]
        wring = [sb([128, WSLOT], BF16) for _ in range(3)]
        wstg = [sb([128, 1024], F32) for _ in range(2)]
        KTb = [Buf(VV[5].t[:, k_, :].rearrange("p (j t) -> p j t", j=4), f"ktb{k_}") for k_ in range(2)]
        Vb = [Buf(VV[6].t[:, k_, :], f"vb{k_}") for k_ in range(3)]
        Vnew = [Buf(VV[s_].t[:, 0, :], f"vnew{s_}") for s_ in range(NS)]
        QTZ = Buf(VV[4].t[:, 0:2, :].rearrange("p a n -> p (a n)")[:, 0:8 * TS].rearrange("p (j i t) -> p j i t", j=4, i=2), "qtz")
        UEs = [Buf(VV[c_].t[:, 1, :].bitcast(F32)[:, 0:NS * (3 + LS)].rearrange("p (s l) -> p s l", s=NS), f"ues{c_}") for c_ in range(4)]
        PHs = [Buf(VV[c_].t[:, 3, :].bitcast(F32)[:, 128:128 + NS * 15].rearrange("p (s l) -> p s l", s=NS), f"phs{c_}") for c_ in range(8)]
        KTs = [Buf(VV[7].t[:, j_, 0:TS], f"kts{j_}") for j_ in range(4)]
        cst = sb([128, NCST], F32, "cst")
        vecs = sb([128, NV], F32, "vecs")
        cvec = sb([128, 16], F32, "cvec")
        gwf = wstg[0]
        gwb = sb([128, 1024], BF16, "gwb")
        trib = sb([128, 256], BF16, "trib")
        onesb = sb([128, 128], BF16, "onesb")
        sq = [sb([128, T], BF16) for _ in range(2)]
        rstd = sb([128, T], F32, "rstd")
        epsv = sb([128, 1], F32, "epsv")
        ost = xstage[0]
        WS = [Buf(None, f"ws{i}") for i in range(NCH)]

        ident = cst.t[:, C_ID:C_ID + 128]
        mask = cst.t[:, C_MASK:C_MASK + 128]
        m32 = cst.t[0:32, C_M32:C_M32 + 256]
        mask2 = cst.t[:, C_MASK2:C_MASK2 + 256].rearrange("p (s k) -> p s k", s=2)

        S.op("pool", lambda e: e.memset(epsv[:], EPS), writes=[epsv])
        S.dma("sp", cst[:], cst_d, writes=[cst])
        S.dma("sp", vecs[:], vecs_d, writes=[vecs])
        S.dma("sp", gwf[:], gatew_d, writes=[gwf])
        S.op("dve", lambda e: e.tensor_copy(out=trib[:], in_=cst.t[:, C_TRI:C_TRI + 256]), reads=[cst], writes=[trib])
        S.op("dve", lambda e: e.tensor_copy(out=onesb[:], in_=cst.t[:, C_ONES:C_ONES + 128]), reads=[cst], writes=[onesb])
        S.op("pool", lambda e: e.tensor_copy(out=gwb[:], in_=gwf[:]), reads=[gwf], writes=[gwb])
        S.op("act", lambda e: e.activation(out=cvec.t[:, 8:12], in_=vecs.t[:, V_LAM:V_LAM + 4], func=AF.Exp, scale=-1.0), reads=[vecs], writes=[cvec])
        S.op("act", lambda e: e.activation(out=cvec.t[:, 12:16], in_=cvec.t[:, 8:12], func=AF.Ln, bias=1.0), reads=[cvec], writes=[cvec])
        S.op("dve", lambda e: e.tensor_scalar(out=cvec.t[:, 0:4], in0=cvec.t[:, 12:16], scalar1=-8.0, scalar2=0.0, op0=ALU.mult, op1=ALU.add), reads=[cvec], writes=[cvec])
        S.op("dve", lambda e: e.tensor_scalar(out=cvec.t[:, 4:8], in0=cvec.t[:, 12:16], scalar1=-16.0, scalar2=0.0, op0=ALU.mult, op1=ALU.add), reads=[cvec], writes=[cvec])
        for c in range(4):
            S.op("pool", lambda e, c=c: e.memset(UE[c].t[:, 0:3], 0.0), writes=[UE[c]])
        S.op("pool", lambda e: e.memset(HC[:], 0.0), writes=[HC])
        S.dma("sp", HCs[:], h0_d, writes=[HCs])
        for c in range(8):
            S.op("pool", lambda e, c=c: e.memset(PH[c][:], 0.0), writes=[PH[c]])

        order = []
        for ti in range(NT + 1):
            for ci in range(NCH):
                order.append((ti, ci))
        wstate = {"emitted": 0, "next": 0, "stg": 0, "cast": 0}
        LOOK = 2

        stg_pool = list(wstg) + [Buf(VV[k_].t[:, :, :].rearrange("p a n -> p (a n)").bitcast(F32), f"vstg{k_}") for k_ in range(1, len(VV))]
        NSTG = (len(stg_pool) // 4) * 4
        LD = NSTG // 4 - 1
        wload = {"emitted": 0, "map": {}}

        def w_load(i):
            ti, ci = order[i]
            if ti != 0:
                return
            nel = CHUNKS[ci][3]
            q = nel // 4
            lst = []
            for k in range(4):
                st = stg_pool[wstate["stg"] % NSTG]
                wstate["stg"] += 1
                S.dma("sp", st.t[:, 0:q], wsrc_d[ci, :, k * q:(k + 1) * q], writes=[st])
                lst.append(st)
            wload["map"][i] = lst

        def w_emit(i):
            ti, ci = order[i]
            buf = wring[i % 3]
            nel = CHUNKS[ci][3]
            if ti == 0:
                while wload["emitted"] <= min(i + LD, NCH - 1):
                    w_load(wload["emitted"])
                    wload["emitted"] += 1
                q = nel // 4
                eng = "act" if (wstate["cast"] % 2 == 1) else "dve"
                wstate["cast"] += 1
                for k, st in enumerate(wload["map"].pop(i)):
                    if eng == "act":
                        S.op("act", lambda e, k=k, st=st: e.activation(out=buf.t[:, k * q:(k + 1) * q], in_=st.t[:, 0:q], func=AF.Copy), reads=[st], writes=[buf])
                    else:
                        S.op("dve", lambda e, k=k, st=st: e.tensor_copy(out=buf.t[:, k * q:(k + 1) * q], in_=st.t[:, 0:q]), reads=[st], writes=[buf])
                S.dma("sp", wscr_d[ci, :, 0:nel], buf.t[:, 0:nel], reads=[buf], writes=[WS[ci]])
            else:
                S.dma("sp", buf.t[:, 0:nel], wscr_d[ci, :, 0:nel], reads=[WS[ci]], writes=[buf])

        def wnext(kind):
            i = wstate["next"]
            wstate["next"] += 1
            assert CHUNKS[order[i][1]][0] == kind, (CHUNKS[order[i][1]], kind)
            while wstate["emitted"] <= min(i + LOOK, len(order) - 1):
                w_emit(wstate["emitted"])
                wstate["emitted"] += 1
            return wring[i % 3]

        def mm(out, lhsT, rhs, start, stop, R, W):
            S.op("pe", lambda e: e.matmul(out, lhsT=lhsT, rhs=rhs, start=start, stop=stop, skip_group_check=True), reads=R, writes=W)

        def norm_stats(TW):
            bk = bank()
            for c in range(8):
                s = sq[c % 2]
                if c % 2 == 0:
                    S.op("act", lambda e, c=c, s=s: e.activation(out=s.t[:, :TW], in_=XT.t[:, c, :TW], func=AF.Square), reads=[XT], writes=[s])
                else:
                    S.op("dve", lambda e, c=c, s=s: e.tensor_tensor(out=s.t[:, :TW], in0=XT.t[:, c, :TW], in1=XT.t[:, c, :TW], op=ALU.mult), reads=[XT], writes=[s])
                mm(bk.t[:, :TW], onesb[:], s.t[:, :TW], c == 0, c == 7, [onesb, s], [bk])
            t1 = tmp()
            S.op("act", lambda e: e.activation(out=t1.t[:, :TW], in_=bk.t[:, :TW], func=AF.Ln, bias=epsv.t[:, 0:1]), reads=[bk, epsv], writes=[t1])
            S.op("act", lambda e: e.activation(out=rstd.t[:, :TW], in_=t1.t[:, :TW], func=AF.Exp, scale=-0.5), reads=[t1], writes=[rstd])

        def norm_to_xn(TW, gcol):
            norm_stats(TW)
            for c in range(8):
                norm_apply(c, gcol, TW, XN.t[:, c, :TW], XNB[c])

        def norm_apply(c, gcol, TW, out_ap, out_buf, in_ap=None, rs_ap=None, force_dve=False):
            in_ap = XT.t[:, c, :TW] if in_ap is None else in_ap
            rs_ap = rstd.t[:, :TW] if rs_ap is None else rs_ap
            gsc = vecs.t[:, gcol + c:gcol + c + 1]
            if c % 4 != 3 or force_dve:
                S.op("dve", lambda e: e.scalar_tensor_tensor(out=out_ap, in0=in_ap, scalar=gsc, in1=rs_ap, op0=ALU.mult, op1=ALU.mult), reads=[XT, vecs, rstd], writes=[out_buf])
            else:
                t_ = tmp()
                tv = t_.t[:, :TW] if len(in_ap.shape) == 2 else t_.t[:, :TW].rearrange("p (s l) -> p s l", s=in_ap.shape[1])
                S.op("act", lambda e: e.activation(out=tv, in_=in_ap, func=AF.Copy, scale=gsc), reads=[XT, vecs], writes=[t_])
                S.op("pool", lambda e: e.tensor_tensor(out=out_ap, in0=tv, in1=rs_ap, op=ALU.mult), reads=[t_, rstd], writes=[out_buf])

        def proj_fm(wb, j, TW, rhs_fn, R):
            bk = bank()
            w3 = wb.t[:].rearrange("p (c n) -> p c n", c=8)
            for c in range(8):
                mm(bk.t[:, :TW], w3[:, c, j * 128:(j + 1) * 128], rhs_fn(c), c == 0, c == 7, [wb] + (R(c) if callable(R) else R), [bk])
            return bk

        xpref = set()

        def process_tile(ti):
            sample = ti == NT
            TW = TS if sample else T
            nseg = NS if sample else 1
            L = LS if sample else T
            first = ti == 0
            t0 = 0 if sample else ti * T
            x_d = xs_d if sample else xp_d
            ntb = TW // 128
            cols = slice(0, TW)

            if sample:
                S.barrier()
                for s_ in range(NS):
                    S.op("pool", lambda e, s_=s_: e.memset(Vnew[s_].t[:, :], 0.0), writes=[Vnew[s_]])
                S.op("pool", lambda e: e.memset(QTZ.t[:, :, :, :], 0.0), writes=[QTZ])
                for c_ in range(4):
                    S.dma("sp", UEs[c_].t[:, :, 0:3], conv0_d[:, c_], writes=[UEs[c_]])
                for c_ in range(8):
                    S.dma("sp", PHs[c_].t[:, :, :], pool0_d[:, c_], writes=[PHs[c_]])
            for tb in range(ntb):
                xs_ = xstage[tb % 2]
                if (ti, tb) not in xpref:
                    S.dma("sp", xs_[:], x_d[t0 + tb * 128:t0 + (tb + 1) * 128, :], writes=[xs_])
                for half in range(2):
                    bk = bank()
                    for jj in range(4):
                        c = half * 4 + jj
                        S.op("pe", lambda e, c=c, jj=jj, bk=bk, xs_=xs_: e.transpose(out=bk.t[:, jj * 128:(jj + 1) * 128], in_=xs_.t[:, c * 128:(c + 1) * 128], identity=ident), reads=[xs_, cst], writes=[bk])
                    eng = "act" if half == 0 else "dve"
                    S.op(eng, lambda e, half=half, bk=bk, tb=tb: (e.activation(out=XT.t[:, half * 4:half * 4 + 4, tb * 128:(tb + 1) * 128], in_=bk.t[:].rearrange("p (j t) -> p j t", j=4), func=AF.Copy) if half == 0 else
                                                                   e.tensor_copy(out=XT.t[:, half * 4:half * 4 + 4, tb * 128:(tb + 1) * 128], in_=bk.t[:].rearrange("p (j t) -> p j t", j=4))), reads=[bk], writes=[XT])

            ckpt(100 * ti + 1)
            norm_to_xn(TW, V_NM[0])
            xn_rhs = lambda c: XN.t[:, c, :TW]
            if ti == 0:
                dbg('xt0', XT, XT.t[:, :, :], [128, 8, T])
                pass
                dbg('rstd0', rstd, rstd.t[:, :], [128, T])

            ckpt(100 * ti + 2)
            if sample:
                tblocks = [(s_ * LS, LS, s_ * LS) for s_ in range(NS)]
            else:
                tblocks = [(tb * 128, 128, t0 + tb * 128) for tb in range(ntb)]
            k_out = ks_d if sample else kp_d
            v_out = vs_d if sample else vp_d
            hc = HCs if sample else HC
            gw3 = gwb.t[:].rearrange("p (k n) -> p k n", k=8)

            def v3(ap2):
                return ap2.rearrange("p (s l) -> p s l", s=nseg)

            def ue_of(c):
                ue = UEs[c] if sample else UE[c]
                return ue, (ue.t[:] if sample else ue.t[:].rearrange("p (s l) -> p s l", s=1))

            wu = wnext("KN")
            for c in range(4):
                ue, ue3 = ue_of(c)
                bku = proj_fm(wu, c, TW, xn_rhs, lambda cc: [XNB[cc]])
                S.op("act", lambda e, bku=bku, ue3=ue3: e.activation(out=ue3[:, :, 3:3 + L], in_=v3(bku.t[:, :TW]), func=AF.Copy), reads=[bku], writes=[ue])
            wg = wnext("KN")
            gts, g2s = [], []
            for c in range(4):
                bkg = proj_fm(wg, c, TW, xn_rhs, lambda cc: [XNB[cc]])
                gt = tmp()
                S.op("act", lambda e, bkg=bkg, gt=gt: e.activation(out=gt.t[:, :TW], in_=bkg.t[:, :TW], func=AF.Copy), reads=[bkg], writes=[gt])
                gts.append(gt)
            for c in range(4):
                gt = gts[c]
                g2 = tmp()
                g2s.append(g2)
                S.op("pool", lambda e, gt=gt, g2=g2: e.tensor_tensor(out=g2.t[:, :TW], in0=gt.t[:, :TW], in1=gt.t[:, :TW], op=ALU.mult), reads=[gt], writes=[g2])
                S.op("dve", lambda e, g2=g2: e.tensor_scalar(out=g2.t[:, :TW], in0=g2.t[:, :TW], scalar1=0.044715, scalar2=1.0, op0=ALU.mult, op1=ALU.add), reads=[g2], writes=[g2])
                S.op("pool", lambda e, gt=gt, g2=g2: e.tensor_tensor(out=g2.t[:, :TW], in0=g2.t[:, :TW], in1=gt.t[:, :TW], op=ALU.mult), reads=[g2, gt], writes=[g2])
            for c in range(4):
                g2 = g2s[c]
                S.op("act", lambda e, g2=g2: e.activation(out=g2.t[:, :TW], in_=g2.t[:, :TW], func=AF.Sigmoid, scale=1.5957691216057308), reads=[g2], writes=[g2])
            for c in range(4):
                gt, g2 = gts[c], g2s[c]
                S.op("dve", lambda e, gt=gt, g2=g2, c=c: e.tensor_tensor(out=ATT(c, cols), in0=gt.t[:, :TW], in1=g2.t[:, :TW], op=ALU.mult), reads=[gt, g2], writes=[HH])

            def piece_q():
                wq = wnext("KN")
                for j in range(4):
                    bk = proj_fm(wq, j, TW, xn_rhs, lambda c: [XNB[c]])
                    if sample:
                        S.op("act", lambda e, j=j, bk=bk: e.activation(out=QTZ.t[0:64, j, 0, :], in_=bk.t[0:64, :TW], func=AF.Copy, scale=0.125), reads=[bk], writes=[QTZ])
                        S.op("dve", lambda e, j=j, bk=bk: e.tensor_scalar(out=QTZ.t[64:128, j, 1, :], in0=bk.t[64:128, :TW], scalar1=0.125, scalar2=0.0, op0=ALU.mult, op1=ALU.add), reads=[bk], writes=[QTZ])
                    else:
                        S.op("act", lambda e, j=j, bk=bk: e.activation(out=QT(j, cols), in_=bk.t[:, :TW], func=AF.Copy, scale=0.125), reads=[bk], writes=[HH])

            kstate = {}

            def piece_kfm():
                wk = wnext("KN")
                kstate["wk"] = wk
                for j in range(4):
                    bk = proj_fm(wk, j, TW, xn_rhs, lambda c: [XNB[c]])
                    dst = KTs[j] if sample else KT[j][ti]
                    S.op("dve", lambda e, bk=bk, dst=dst: e.tensor_copy(out=dst.t[:, :TW], in_=bk.t[:, :TW]), reads=[bk], writes=[dst])

            def piece_ktm():
                wk = kstate["wk"]
                wk3 = wk.t[:].rearrange("p (c n) -> p c n", c=8)
                for bi, (c0, n, r0) in enumerate(tblocks):
                    bk = bank()
                    for c in range(8):
                        mm(bk.t[0:n, :], XN.t[:, c, c0:c0 + n], wk3[:, c, :], c == 0, c == 7, [XNB[c], wk], [bk])
                    st = kvst[bi % 2]
                    S.op("act", lambda e, bk=bk, st=st, n=n: e.activation(out=st.t[0:n, :], in_=bk.t[0:n, :], func=AF.Copy), reads=[bk], writes=[st])
                    S.dma("sp", k_out[r0:r0 + n, :], st.t[0:n, :], reads=[st])

            def piece_v():
                wv = wnext("KN")
                wv3 = wv.t[:].rearrange("p (c n) -> p c n", c=8)
                for bi, (c0, n, r0) in enumerate(tblocks):
                    bk = bank()
                    for c in range(8):
                        mm(bk.t[0:n, :], XN.t[:, c, c0:c0 + n], wv3[:, c, :], c == 0, c == 7, [XNB[c], wv], [bk])
                    st = kvst[(bi + 1) % 2]
                    S.op("act", lambda e, bk=bk, st=st, n=n: e.activation(out=st.t[0:n, :], in_=bk.t[0:n, :], func=AF.Copy), reads=[bk], writes=[st])
                    S.dma("sp", v_out[r0:r0 + n, :], st.t[0:n, :], reads=[st])
                    if sample:
                        S.op("dve", lambda e, bk=bk, bi=bi, n=n: e.tensor_copy(out=Vnew[bi].t[0:n, :], in_=bk.t[0:n, :]), reads=[bk], writes=[Vnew[bi]])
                    else:
                        S.op("dve", lambda e, bk=bk, bi=bi: e.tensor_copy(out=VV[ti].t[:, bi, :], in_=bk.t[:, :]), reads=[bk], writes=[VV[ti]])

            pieces = [piece_q, piece_kfm, piece_ktm, piece_v]

            def lruA(c):
                ue, ue3 = ue_of(c)
                uc = tmp()
                cw = lambda i: vecs.t[:, V_CW + 4 * i + c:V_CW + 4 * i + c + 1]
                S.op("dve", lambda e: e.tensor_scalar(out=v3(uc.t[:, :TW]), in0=ue3[:, :, 3:3 + L], scalar1=cw(3), scalar2=vecs.t[:, V_CB + c:V_CB + c + 1], op0=ALU.mult, op1=ALU.add), reads=[ue, vecs], writes=[uc])
                for i in (2, 1, 0):
                    S.op("dve", lambda e, i=i: e.scalar_tensor_tensor(out=v3(uc.t[:, :TW]), in0=ue3[:, :, i:i + L], scalar=cw(i), in1=v3(uc.t[:, :TW]), op0=ALU.mult, op1=ALU.add), reads=[ue, vecs, uc], writes=[uc])
                S.op("pool", lambda e: e.tensor_copy(out=ue3[:, :, 0:3], in_=ue3[:, :, L:L + 3]), reads=[ue], writes=[ue])
                ub = ucb[c % 2]
                S.op("act", lambda e: e.activation(out=ub.t[:, :TW], in_=uc.t[:, :TW], func=AF.Copy), reads=[uc], writes=[ub])
                return uc, ub

            def lruB(c, uc, ub):
                bkr = bank()
                mm(bkr.t[:, :TW], gw3[:, c, :], ub.t[:, :TW], True, True, [gwb, ub], [bkr])
                bki = bank()
                mm(bki.t[:, :TW], gw3[:, 4 + c, :], ub.t[:, :TW], True, True, [gwb, ub], [bki])
                rr = tmp()
                ii = tmp()
                S.op("act", lambda e: e.activation(out=rr.t[:, :TW], in_=bkr.t[:, :TW], func=AF.Sigmoid, bias=vecs.t[:, V_RGB + c:V_RGB + c + 1]), reads=[bkr, vecs], writes=[rr])
                S.op("act", lambda e: e.activation(out=ii.t[:, :TW], in_=bki.t[:, :TW], func=AF.Sigmoid, bias=vecs.t[:, V_IGB + c:V_IGB + c + 1]), reads=[bki, vecs], writes=[ii])
                a2 = tmp()
                aa = rr
                S.op("act", lambda e: e.activation(out=a2.t[:, :TW], in_=rr.t[:, :TW], func=AF.Exp, scale=cvec.t[:, 4 + c:5 + c]), reads=[rr, cvec], writes=[a2])
                S.op("act", lambda e: e.activation(out=aa.t[:, :TW], in_=rr.t[:, :TW], func=AF.Exp, scale=cvec.t[:, c:c + 1]), reads=[rr, cvec], writes=[aa])
                S.op("act", lambda e: e.activation(out=a2.t[:, :TW], in_=a2.t[:, :TW], func=AF.Ln, scale=-1.0, bias=1.0), reads=[a2], writes=[a2])
                S.op("act", lambda e: e.activation(out=a2.t[:, :TW], in_=a2.t[:, :TW], func=AF.Exp, scale=0.5), reads=[a2], writes=[a2])
                S.op("pool", lambda e: e.tensor_tensor(out=ii.t[:, :TW], in0=ii.t[:, :TW], in1=uc.t[:, :TW], op=ALU.mult), reads=[ii, uc], writes=[ii])
                S.op("dve", lambda e: e.tensor_tensor(out=ii.t[:, :TW], in0=ii.t[:, :TW], in1=a2.t[:, :TW], op=ALU.mult), reads=[ii, a2], writes=[ii])
                hh_ = a2
                for s_ in range(nseg):
                    init = hc.t[:, c, s_:s_ + 1] if sample else hc.t[:, c:c + 1]
                    S.op("dve", lambda e, s_=s_, init=init: e.tensor_tensor_scan(out=hh_.t[:, s_ * L:(s_ + 1) * L], data0=aa.t[:, s_ * L:(s_ + 1) * L], data1=ii.t[:, s_ * L:(s_ + 1) * L], initial=init, op0=ALU.mult, op1=ALU.add), reads=[aa, ii, hc], writes=[hh_])
                hdst = hc.t[:, c, :] if sample else hc.t[:, c:c + 1]
                S.op("pool", lambda e: e.tensor_copy(out=hdst, in_=v3(hh_.t[:, :TW])[:, :, L - 1]), reads=[hh_], writes=[hc])
                S.op("dve", lambda e: e.tensor_tensor(out=MIX(c, cols), in0=hh_.t[:, :TW], in1=ATT(c, cols), op=ALU.mult), reads=[hh_, HH], writes=[HH])

            for c in range(4):
                uc, ub = lruA(c)
                pieces[c]()
                lruB(c, uc, ub)

            ckpt(100 * ti + 4)
            if sample:
                attn_sample()
            else:
                attn_prompt(ti)

            ckpt(100 * ti + 5)
            for s_ in range(2):
                wo = wnext("KO")
                for j in range(4):
                    n = s_ * 4 + j
                    bk = proj_fm(wo, j, TW, lambda c: MIXALL(c, TW), [HH])
                    S.op("dve", lambda e, bk=bk, n=n: e.tensor_tensor(out=XT.t[:, n, :TW], in0=bk.t[:, :TW], in1=XT.t[:, n, :TW], op=ALU.add), reads=[bk, XT], writes=[XT])

            ckpt(100 * ti + 6)
            ffn(0, TW)
            ckpt(100 * ti + 7)
            pool_mixer(ti, TW, nseg, L, sample, first)
            ckpt(100 * ti + 8)
            if ti + 1 <= NT:
                nsample = (ti + 1 == NT)
                nx_d = xs_d if nsample else xp_d
                nt0 = 0 if nsample else (ti + 1) * T
                for tb in range(1 if nsample else 2):
                    S.dma("sp", xstage[tb][:], nx_d[nt0 + tb * 128:nt0 + (tb + 1) * 128, :], writes=[xstage[tb]])
                    xpref.add((ti + 1, tb))
            ffn(1, TW)
            ckpt(100 * ti + 9)
            norm_stats(TW)
            for c in range(8):
                norm_apply(c, V_NFIN, TW, XT.t[:, c, :TW], XT)
            y_d = ys_d if sample else yp_d
            for tb in range(ntb):
                ya, yb, y3 = tmp2()
                for half in range(2):
                    bk = bank()
                    for jj in range(4):
                        c = half * 4 + jj
                        S.op("pe", lambda e, c=c, jj=jj, bk=bk, tb=tb: e.transpose(out=bk.t[:, jj * 128:(jj + 1) * 128], in_=XT.t[:, c, tb * 128:(tb + 1) * 128], identity=ident), reads=[XT, cst], writes=[bk])
                    if half == 0:
                        S.op("act", lambda e, bk=bk, y3=y3: e.activation(out=y3[:, 0, :], in_=bk.t[:, :], func=AF.Copy), reads=[bk], writes=[ya])
                    else:
                        S.op("dve", lambda e, bk=bk, y3=y3: e.tensor_copy(out=y3[:, 1, :], in_=bk.t[:, :]), reads=[bk], writes=[yb])
                S.dma("sp", y_d[t0 + tb * 128:t0 + (tb + 1) * 128, :].rearrange("t (h m) -> t h m", h=2), y3, reads=[ya, yb])

        def MIXALL(c, TW):
            if c < 4:
                return HH.t[:, (8 + c) * T:(8 + c) * T + TW]
            return HH.t[:, c * T:c * T + TW]

        def ATT(j, cols):
            return HH.t[:, (8 + j) * T + cols.start:(8 + j) * T + cols.stop]

        def ffn(layer, TW):
            norm_to_xn(TW, V_NF[0] + 8 * layer)
            for half in range(2):
                slot = 0
                for i in range(6):
                    nf = 2 if i < 5 else 1
                    wb = wnext("GU" if nf == 2 else "GU1")
                    w3 = wb.t[:, 0:8 * 2 * nf * 128].rearrange("p (c n) -> p c n", c=8)
                    for f in range(nf):
                        bg = bank()
                        for c in range(8):
                            mm(bg.t[:, :TW], w3[:, c, f * 128:(f + 1) * 128], XN.t[:, c, :TW], c == 0, c == 7, [wb, XNB[c]], [bg])
                        bu = bank()
                        for c in range(8):
                            mm(bu.t[:, :TW], w3[:, c, (nf + f) * 128:(nf + f + 1) * 128], XN.t[:, c, :TW], c == 0, c == 7, [wb, XNB[c]], [bu])
                        sg = tmp()
                        S.op("act", lambda e, bg=bg, sg=sg: e.activation(out=sg.t[:, :TW], in_=bg.t[:, :TW], func=AF.Silu), reads=[bg], writes=[sg])
                        S.op("dve", lambda e, bu=bu, sg=sg, slot=slot: e.tensor_tensor(out=hh3[:, slot, :TW], in0=bu.t[:, :TW], in1=sg.t[:, :TW], op=ALU.mult), reads=[bu, sg], writes=[HH])
                        slot += 1
                for q in range(4):
                    wb = wnext("DN")
                    w3 = wb.t[:, 0:2816].rearrange("p (f n) -> p f n", f=11)
                    for o in range(2):
                        n = q * 2 + o
                        bk = bank()
                        for f in range(11):
                            mm(bk.t[:, :TW], w3[:, f, o * 128:(o + 1) * 128], hh3[:, f, :TW], f == 0, f == 10, [wb, HH], [bk])
                        S.op("dve", lambda e, bk=bk, n=n: e.tensor_tensor(out=XT.t[:, n, :TW], in0=bk.t[:, :TW], in1=XT.t[:, n, :TW], op=ALU.add), reads=[bk, XT], writes=[XT])

        def pool_mixer(ti, TW, nseg, L, sample, first):
            norm_stats(TW)
            wb = wnext("PW")
            w4 = wb.t[:, 0:2048].rearrange("p (g c n) -> p g c n", g=4, c=2)
            W_ = 15 + L
            dB4 = [ucb[0], ucb[1], sq[0], sq[1]]

            def v3w(b):
                return b.t[:, 0:nseg * W_].rearrange("p (s l) -> p s l", s=nseg)

            for half in range(2):
                chunks = [4 * half + k_ for k_ in range(4)]
                st = {}
                for c in chunks:
                    ext, tA, tB = tmp(), tmp(), tmp()
                    st[c] = dict(ext=ext, ext3=v3w(ext), tmps=[tA, tB], cur=ext, cur3=v3w(ext), win=POOL_WINDOWS[c // 2])
                for c in chunks:
                    d_ = st[c]
                    norm_apply(c, V_NM[1], TW, d_["ext3"][:, :, 15:15 + L], d_["ext"], in_ap=XT.t[:, c, :TW].rearrange("p (s l) -> p s l", s=nseg),
                               rs_ap=rstd.t[:, :TW].rearrange("p (s l) -> p s l", s=nseg), force_dve=True)
                for c in chunks:
                    d_ = st[c]
                    ph = PHs[c] if sample else PH[c]
                    ph3 = ph.t[:] if sample else ph.t[:].rearrange("p (s l) -> p s l", s=1)
                    S.op("pool", lambda e, d_=d_, ph3=ph3: e.tensor_copy(out=d_["ext3"][:, :, 0:15], in_=ph3), reads=[ph], writes=[d_["ext"]])
                    S.op("pool", lambda e, d_=d_, ph3=ph3: e.tensor_copy(out=ph3, in_=d_["ext3"][:, :, L:L + 15]), reads=[d_["ext"]], writes=[ph])
                sh, lvl = 1, 0
                while sh < 16:
                    for c in chunks:
                        d_ = st[c]
                        if sh >= d_["win"]:
                            continue
                        nxt = d_["tmps"][lvl % 2]
                        nxt3 = v3w(nxt)
                        lo = 2 * sh - 1
                        eng = "pool" if sh in (1, 4) else "dve"
                        S.op(eng, lambda e, cur3=d_["cur3"], nxt3=nxt3, lo=lo, sh=sh: e.tensor_tensor(out=nxt3[:, :, lo:W_], in0=cur3[:, :, lo:W_], in1=cur3[:, :, lo - sh:W_ - sh], op=ALU.add), reads=[d_["cur"]], writes=[nxt])
                        d_["cur"], d_["cur3"] = nxt, nxt3
                    sh *= 2
                    lvl += 1
                for k_, c in enumerate(chunks):
                    d_ = st[c]
                    win = d_["win"]
                    db = dB4[k_]
                    d_["db"] = db
                    S.op("dve", lambda e, d_=d_, db=db, win=win: e.scalar_tensor_tensor(out=db.t[:, :TW].rearrange("p (s l) -> p s l", s=nseg), in0=d_["cur3"][:, :, 15:15 + L], scalar=1.0 / win, in1=d_["ext3"][:, :, 15:15 + L], op0=ALU.mult, op1=ALU.subtract),
                         reads=[d_["cur"], d_["ext"]], writes=[db])
                    if first:
                        t16 = d_["tmps"][0] if d_["cur"] is d_["tmps"][1] else d_["tmps"][1]
                        g = c // 2
                        S.op("dve", lambda e, d_=d_, t16=t16, g=g: e.tensor_tensor(out=t16.t[:, 0:16], in0=d_["cur"].t[:, 15:31], in1=cst.t[:, C_INV + 16 * g:C_INV + 16 * g + 16], op=ALU.mult), reads=[d_["cur"], cst], writes=[t16])
                        S.op("dve", lambda e, d_=d_, t16=t16, db=db: e.tensor_tensor(out=db.t[:, 0:16], in0=t16.t[:, 0:16], in1=d_["ext"].t[:, 15:31], op=ALU.subtract), reads=[t16, d_["ext"], db], writes=[db])
                for g in (2 * half, 2 * half + 1):
                    for o in range(2):
                        n = 2 * g + o
                        bk = bank()
                        for ci in range(2):
                            db = st[2 * g + ci]["db"]
                            mm(bk.t[:, :TW], w4[:, g, ci, o * 128:(o + 1) * 128], db.t[:, :TW], ci == 0, ci == 1, [wb, db], [bk])
                        S.op("dve", lambda e, bk=bk, n=n: e.scalar_tensor_tensor(out=XT.t[:, n, :TW], in0=bk.t[:, :TW], scalar=vecs.t[:, V_PS + n:V_PS + n + 1], in1=XT.t[:, n, :TW], op0=ALU.mult, op1=ALU.add),
                             reads=[bk, vecs, XT], writes=[XT])

        rot = {"l": 0, "w": 0}

        def attn_prompt(g):
            A2 = P2[0][:, :].rearrange("p (s t) -> p s t", s=2)
            C2 = P2[1][:, :].rearrange("p (s t) -> p s t", s=2)
            Ab, Cb = [banks[0], banks[1]], [banks[2], banks[3]]
            Oall = [[banks[4], banks[5]], [banks[6], banks[7]]]
            nst = 4 * g + 4
            seq = [(hp, st) for hp in range(4) for st in range(nst)]
            info = {}

            def geom(st):
                kb = nst - 1 - st
                jj = kb - 4 * g
                c0 = 128 * jj if jj > 0 else 0
                return kb, jj, c0, kb // 4, kb % 4

            def emitA(x):
                hp, st = seq[x]
                kb, jj, c0, tt, bi = geom(st)
                kt = KT[hp][tt]
                for s_ in range(2):
                    mm(A2[:, s_, c0:T], kt.t[64 * s_:64 * s_ + 64, bi * 128:(bi + 1) * 128], HH.t[64 * s_:64 * s_ + 64, hp * T + c0:hp * T + T], True, True, [kt, HH], [Ab[s_]])

            def S1(x):
                hp, st = seq[x]
                kb, jj, c0, tt, bi = geom(st)
                ea, eb, e3 = tmp2()
                S.op("act", lambda e: e.activation(out=e3[:, :, c0:T], in_=A2[:, :, c0:T], func=AF.Exp), reads=Ab, writes=[ea, eb])
                if jj >= 0:
                    S.op("pool", lambda e: e.tensor_tensor(out=e3[:, :, c0:c0 + 128], in0=e3[:, :, c0:c0 + 128], in1=mask2, op=ALU.mult), reads=[ea, eb, cst], writes=[ea, eb])
                k_ = rot["l"] % 3
                rot["l"] += 1
                S.op("act", lambda e: e.activation(out=LB[k_].t[:, :, c0:T], in_=e3[:, :, c0:T], func=AF.Ln, bias=1.0), reads=[ea, eb], writes=[LB[k_]])
                info[x] = (hp, st, c0, tt, bi, ea, eb, e3, k_)

            def S2(x):
                hp, st, c0, tt, bi, ea, eb, e3, k_ = info[x]
                for s_ in range(2):
                    mm(C2[:, s_, c0:T], trib.t[:, 0:128], LB[k_].t[:, s_, c0:T], st == 0, False, [trib, LB[k_]], [Cb[s_]])

            def S2p(x):
                hp, st, c0, tt, bi, ea, eb, e3, k_ = info[x]
                pa, pb, p3 = tmp2()
                S.op("act", lambda e: e.activation(out=p3[:, :, c0:T], in_=C2[:, :, c0:T], func=AF.Exp), reads=Cb, writes=[pa, pb])
                info[x] = info[x] + (pa, pb, p3)

            def S3a(x):
                hp, st, c0, tt, bi, ea, eb, e3, k_, pa, pb, p3 = info[x]
                if st < nst - 1:
                    for s_ in range(2):
                        mm(C2[:, s_, c0:T], trib.t[:, 128:256], LB[k_].t[:, s_, c0:T], False, False, [trib, LB[k_]], [Cb[s_]])

            def S3b(x):
                hp, st, c0, tt, bi, ea, eb, e3, k_, pa, pb, p3 = info[x]
                O = Oall[hp % 2]
                w_ = WB[rot["w"] % 2]
                rot["w"] += 1
                S.op("dve", lambda e: e.tensor_tensor(out=w_.t[:, :, c0:T], in0=e3[:, :, c0:T], in1=p3[:, :, c0:T], op=ALU.mult), reads=[ea, eb, pa, pb], writes=[w_])
                for s_ in range(2):
                    mm(O[s_].t[:, c0:T], VV[tt].t[:, bi, hp * 128:(hp + 1) * 128], w_.t[:, s_, c0:T], st == 0, st == nst - 1, [VV[tt], w_], [O[s_]])
                if st == nst - 1:
                    S.op("act", lambda e: e.activation(out=ATT(hp, slice(0, T))[0:64, :], in_=O[0].t[0:64, :], func=AF.Copy), reads=[O[0]], writes=[HH])
                    S.op("dve", lambda e: e.tensor_copy(out=ATT(hp, slice(0, T))[64:128, :], in_=O[1].t[64:128, :]), reads=[O[1]], writes=[HH])
                del info[x]

            N = len(seq)
            emitA(0)
            for n in range(N + 2):
                if n < N:
                    S1(n)
                if n >= 2:
                    S3a(n - 2)
                if 1 <= n < N + 1:
                    S2(n - 1)
                if n + 1 < N:
                    emitA(n + 1)
                if 1 <= n < N + 1:
                    S2p(n - 1)
                if n >= 2:
                    S3b(n - 2)

        def attn_sample():
            NQ = 8 * LS
            nst = NPB + 1
            rotc = {"k": 0, "v": 0}
            for sp_ in range(NS // 2):
                slots = []
                for st in range(nst):
                    for s in range(2):
                        slots.append((s, st))
                info = {}
                A = [banks[0], banks[1]]
                ACC = [banks[2], banks[3]]
                O = [banks[4], banks[5]]
                TR = [banks[6], banks[7]]
                pre = {}

                def load(n):
                    s, st = slots[n]
                    kb = NPB - st
                    if kb == NPB:
                        return
                    seq = 2 * sp_ + s
                    kf = tmp()
                    vf = tmp()
                    S.dma("sp", kf.t[:, 0:512], ck_d[seq, kb * 128:(kb + 1) * 128, :], writes=[kf])
                    S.dma("sp", vf.t[:, 0:512], cv_d[seq, kb * 128:(kb + 1) * 128, :], writes=[vf])
                    pre[n] = (kf, vf)

                ktmap = {}

                def S0(n):
                    s, st = slots[n]
                    if NPB - st == NPB:
                        return
                    kf, vf = pre.pop(n)
                    for j in range(4):
                        S.op("pe", lambda e, j=j: e.transpose(out=TR[s].t[:, j * 128:(j + 1) * 128], in_=kf.t[:, j * 128:(j + 1) * 128], identity=ident), reads=[kf, cst], writes=[TR[s]])
                    ktb = KTb[rotc["k"] % 2]
                    rotc["k"] += 1
                    S.op("dve", lambda e: e.tensor_copy(out=ktb.t[:, :, :], in_=TR[s].t[:, :].rearrange("p (j k) -> p j k", j=4)), reads=[TR[s]], writes=[ktb])
                    ktmap[n] = (ktb, vf)

                def S1(n):
                    s, st = slots[n]
                    seq = 2 * sp_ + s
                    kb = NPB - st
                    new = kb == NPB
                    rows = 128
                    qc = slice(seq * LS, (seq + 1) * LS)
                    if new:
                        vsrc = Vnew[seq]
                        for j in range(4):
                            mm(A[s].t[0:LS, j * 2 * LS:(j + 1) * 2 * LS].rearrange("p (i q) -> p i q", i=2), KTs[j].t[:, qc], QTZ.t[:, j, :, qc], True, True, [KTs[j], QTZ], [A[s]])
                    else:
                        ktb, vf = ktmap.pop(n)
                        vsrc = Vb[rotc["v"] % 3]
                        rotc["v"] += 1
                        if s == 0:
                            S.op("dve", lambda e: e.tensor_copy(out=vsrc.t[:, :], in_=vf.t[:, 0:512]), reads=[vf], writes=[vsrc])
                        else:
                            S.op("act", lambda e: e.activation(out=vsrc.t[:, :], in_=vf.t[:, 0:512], func=AF.Copy), reads=[vf], writes=[vsrc])
                        for j in range(4):
                            mm(A[s].t[:, j * 2 * LS:(j + 1) * 2 * LS].rearrange("p (i q) -> p i q", i=2), ktb.t[:, j, :], QTZ.t[:, j, :, qc], True, True, [ktb, QTZ], [A[s]])
                    e_ = tmp()
                    if new:
                        S.op("pool", lambda e: e.memset(e_.t[:, 0:NQ], 0.0), writes=[e_])
                        S.op("act", lambda e: e.activation(out=e_.t[0:LS, 0:NQ], in_=A[s].t[0:LS, 0:NQ], func=AF.Exp), reads=[A[s]], writes=[e_])
                        S.op("pool", lambda e: e.tensor_tensor(out=e_.t[0:LS, 0:NQ], in0=e_.t[0:LS, 0:NQ], in1=m32, op=ALU.mult), reads=[e_, cst], writes=[e_])
                    else:
                        S.op("act", lambda e: e.activation(out=e_.t[0:rows, 0:NQ], in_=A[s].t[0:rows, 0:NQ], func=AF.Exp), reads=[A[s]], writes=[e_])
                    l_ = lbuf[rot["l"] % 3]
                    rot["l"] += 1
                    S.op("act", lambda e: e.activation(out=l_.t[0:rows, 0:NQ], in_=e_.t[0:rows, 0:NQ], func=AF.Ln, bias=1.0), reads=[e_], writes=[l_])
                    info[n] = (s, st, seq, rows, e_, l_, vsrc)

                def S2(n):
                    s, st, seq, rows, e_, l_, vsrc = info[n]
                    mm(ACC[s].t[:, 0:NQ], trib.t[0:rows, 0:128], l_.t[0:rows, 0:NQ], st == 0, False, [trib, l_], [ACC[s]])
                    p_ = tmp()
                    S.op("act", lambda e: e.activation(out=p_.t[0:rows, 0:NQ], in_=ACC[s].t[0:rows, 0:NQ], func=AF.Exp), reads=[ACC[s]], writes=[p_])
                    info[n] = info[n] + (p_,)

                def S3(n):
                    s, st, seq, rows, e_, l_, vsrc, p_ = info[n]
                    if st < nst - 1:
                        mm(ACC[s].t[:, 0:NQ], trib.t[0:rows, 128:256], l_.t[0:rows, 0:NQ], False, False, [trib, l_], [ACC[s]])
                    w_ = wbuf[rot["w"] % 3]
                    rot["w"] += 1
                    S.op("dve", lambda e: e.tensor_tensor(out=w_.t[0:rows, 0:NQ], in0=e_.t[0:rows, 0:NQ], in1=p_.t[0:rows, 0:NQ], op=ALU.mult), reads=[e_, p_], writes=[w_])
                    for j in range(4):
                        mm(O[s].t[:, j * 2 * LS:(j + 1) * 2 * LS], vsrc.t[0:rows, j * 128:(j + 1) * 128], w_.t[0:rows, j * 2 * LS:(j + 1) * 2 * LS], st == 0 and j == 0, st == nst - 1, [vsrc, w_], [O[s]])
                    if st == nst - 1:
                        qc = slice(seq * LS, (seq + 1) * LS)
                        for h in range(8):
                            j, i = h // 2, h % 2
                            eng = "act" if h % 2 == 0 else "dve"
                            if eng == "act":
                                S.op("act", lambda e, h=h, j=j, i=i: e.activation(out=ATT(j, qc)[64 * i:64 * i + 64, :], in_=O[s].t[64 * i:64 * i + 64, h * LS:(h + 1) * LS], func=AF.Copy), reads=[O[s]], writes=[HH])
                            else:
                                S.op("dve", lambda e, h=h, j=j, i=i: e.tensor_copy(out=ATT(j, qc)[64 * i:64 * i + 64, :], in_=O[s].t[64 * i:64 * i + 64, h * LS:(h + 1) * LS]), reads=[O[s]], writes=[HH])
                    del info[n]

                N = len(slots)
                for n in range(min(3, N)):
                    load(n)
                S0(0)
                for n in range(N + 2):
                    if n + 1 < N:
                        S0(n + 1)
                    if n < N:
                        S1(n)
                    if n + 3 < N:
                        load(n + 3)
                    if 1 <= n < N + 1:
                        S2(n - 1)
                    if n >= 2:
                        S3(n - 2)

        try:
            ckpt(0)
            for ti in range(NT + 1):
                process_tile(ti)
                if ti == 0:
                    S.barrier()
                ckpt(10 + ti)
        except StopBuild:
            pass

        with nc.allow_non_contiguous_dma(reason="small state outputs"):
            S.dma("sp", hp_d.rearrange("o (c p) -> p (o c)", p=128), HC[:], reads=[HC])
            for s in range(NS):
                S.dma("sp", hs_d[s:s + 1, :].rearrange("o (c p) -> p (o c)", p=128), HCs.t[:, :, s], reads=[HCs])
            for c in range(4):
                S.dma("sp", convp_d[:, c * 128:(c + 1) * 128].rearrange("r p -> p r"), UE[c].t[:, 0:3], reads=[UE[c]])
                for s in range(NS):
                    S.dma("sp", convs_d[s * 3:(s + 1) * 3, c * 128:(c + 1) * 128].rearrange("r p -> p r"), UEs[c].t[:, s, 0:3], reads=[UEs[c]])
        def pool_out(srcs, dst):
            bks = [bank(), bank()]
            for c in range(8):
                bk = bks[c // 4]
                S.op("pe", lambda e, c=c, bk=bk: e.transpose(out=bk.t[0:15, (c % 4) * 128:(c % 4 + 1) * 128], in_=srcs[c], identity=ident), reads=[cst] + list(PH) + list(PHs), writes=[bk])
            S.op("act", lambda e: e.activation(out=ost.t[0:15, 0:512], in_=bks[0].t[0:15, :], func=AF.Copy), reads=[bks[0]], writes=[ost])
            S.op("dve", lambda e: e.tensor_copy(out=ost.t[0:15, 512:1024], in_=bks[1].t[0:15, :]), reads=[bks[1]], writes=[ost])
            S.dma("sp", dst, ost.t[0:15, :], reads=[ost])

        pool_out([PH[c].t[:, :] for c in range(8)], poolp_d[:, :])
        for s in range(NS):
            pool_out([PHs[c].t[:, s, :] for c in range(8)], pools_d[s * 15:(s + 1) * 15, :])
        for q in ("sp", "act"):
            for i, v in enumerate(S.dcount[q]):
                if v > 0:
                    S._wait("sp", ("d" + q, i), v)
        build.stats = dict(S.count, waits=S.nwaits)
    return nc


def run(inputs, n_cores, SEQ, NS, LS, PAST):
    f = lambda k: np.asarray(inputs[k], dtype=np.float32)
    xp, xs = f("x_prompt"), f("x_sample")
    ck, cv = f("cache_sb_k")[0], f("cache_sb_v")[0]
    h0, conv0, pool0 = f("state_lru_h")[0], f("state_lru_conv")[0], f("state_pool")[0]
    wsrc = host_chunks(f("hyb_w_in")[0], f("hyb_w_out")[0], f("ffn_gate"), f("ffn_up"), f("ffn_down"), f("pool_w")[0])
    vecs = np.zeros((128, NV), np.float32)
    nm, nf = f("norm_mix"), f("norm_ffn")
    vecs[:, 0:8] = fm(nm[0]); vecs[:, 8:16] = fm(nm[1]); vecs[:, 16:24] = fm(nf[0]); vecs[:, 24:32] = fm(nf[1])
    vecs[:, 32:40] = fm(f("norm_final")); vecs[:, 40:48] = fm(f("pool_scale")[0])
    cw = f("hyb_conv_w")[0]
    for i in range(4):
        vecs[:, 48 + 4 * i:52 + 4 * i] = fm(cw[i])
    vecs[:, 64:68] = fm(f("hyb_conv_b")[0]); vecs[:, 68:72] = fm(f("hyb_rg_b")[0]); vecs[:, 72:76] = fm(f("hyb_ig_b")[0]); vecs[:, 76:80] = fm(f("hyb_lambda")[0])
    gatew = np.zeros((128, 8, 128), np.float32)
    rg, ig = f("hyb_rg_w")[0], f("hyb_ig_w")[0]
    for c in range(4):
        for kk, wsel in enumerate((rg, ig)):
            gatew[0:64, 4 * kk + c, 0:64] = wsel[2 * c]
            gatew[64:128, 4 * kk + c, 64:128] = wsel[2 * c + 1]
    gatew = gatew.reshape(128, 1024)
    cst = host_consts()
    in_maps = []
    for r in range(n_cores):
        sl = slice(r * NS, (r + 1) * NS)
        in_maps.append({
            "xp": np.ascontiguousarray(xp[r]),
            "xs": np.ascontiguousarray(xs[sl].reshape(NS * LS, D)),
            "ck": np.ascontiguousarray(ck[sl].reshape(NS, PAST, 512)),
            "cv": np.ascontiguousarray(cv[sl].reshape(NS, PAST, 512)),
            "h0": np.ascontiguousarray(h0[sl].reshape(NS, 4, 128).transpose(2, 1, 0)),
            "conv0": np.ascontiguousarray(conv0[sl].reshape(NS, 3, 4, 128).transpose(3, 2, 0, 1)),
            "pool0": np.ascontiguousarray(pool0[sl].reshape(NS, 15, 8, 128).transpose(3, 2, 0, 1)),
            "wsrc": wsrc, "vecs": vecs, "gatew": gatew, "cst": cst,
        })
    nc = build(SEQ, NS, LS, PAST)
    res = run_bass_kernel_spmd(nc, in_maps, core_ids=list(range(n_cores)))
    R = res.results
    cat = lambda k: np.stack([np.asarray(R[r][k], dtype=np.float32) for r in range(n_cores)])
    y_p = cat("yp")
    y_s = cat("ys").reshape(n_cores * NS, LS, D)
    k_p = cat("kp").reshape(1, n_cores, SEQ, 8, 64)
    v_p = cat("vp").reshape(1, n_cores, SEQ, 8, 64)
    h_p = cat("hp").reshape(1, n_cores, 512)
    conv_p = cat("convp").reshape(1, n_cores, 3, 512)
    pool_p = cat("poolp").reshape(1, n_cores, 15, D)
    k_s = cat("ks").reshape(1, n_cores * NS, LS, 8, 64)
    v_s = cat("vs").reshape(1, n_cores * NS, LS, 8, 64)
    h_s = cat("hs").reshape(1, n_cores * NS, 512)
    conv_s = cat("convs").reshape(1, n_cores * NS, 3, 512)
    pool_s = cat("pools").reshape(1, n_cores * NS, 15, D)
    if DEBUG:
        run.dbg = {k: np.asarray(v) for k, v in R[0].items() if k.startswith('dbg_')}
    return (y_p, y_s, k_p, v_p, h_p, conv_p, pool_p, k_s, v_s, h_s, conv_s, pool_s)


def kernel(**inputs):
    return run(inputs, 8, 4096, 4, 32, 4096)
```

```python
import numpy as np
from contextlib import ExitStack
import concourse.bass as bass
import concourse.mybir as mybir
from concourse.bass_utils import run_bass_kernel_spmd

F32 = mybir.dt.float32
BF16 = mybir.dt.bfloat16
AF = mybir.ActivationFunctionType
ALU = mybir.AluOpType

D = 1024
DFF = 2816
NFC = 22
POOL_WINDOWS = (2, 4, 8, 16)
EPS = 1e-6
EPOCH = 30000
DEBUG = False
STOP = None


class StopBuild(Exception):
    pass


def ckpt(k):
    if STOP is not None and STOP == k:
        raise StopBuild()
WSLOT = 4096

def chunk_list():
    L = []
    for s in (3, 4, 0, 1, 2):
        L.append(("KN", s, 0, 4096))
    for s in range(2):
        L.append(("KO", s, 0, 4096))

    def ffn(layer):
        for half in range(2):
            f0 = half * 11
            for i in range(5):
                L.append(("GU", layer, f0 + 2 * i, 4096))
            L.append(("GU1", layer, f0 + 10, 2048))
            for q in range(4):
                L.append(("DN", layer, half * 4 + q, 2816))
    ffn(0)
    L.append(("PW", 0, 0, 2048))
    ffn(1)
    return L


CHUNKS = chunk_list()
NCH = len(CHUNKS)


def host_chunks(w_in, w_out, gate, up, down, pool_w):
    out = np.zeros((NCH, 128, WSLOT), np.float32)
    for i, (k, a, b, nel) in enumerate(CHUNKS):
        if k == "KN":
            out[i] = w_in[:, a * 512:(a + 1) * 512].reshape(8, 128, 512).transpose(1, 0, 2).reshape(128, 4096)
        elif k == "KO":
            out[i] = w_out[:, a * 512:(a + 1) * 512].reshape(8, 128, 512).transpose(1, 0, 2).reshape(128, 4096)
        elif k == "GU":
            g = gate[a][:, b * 128:(b + 2) * 128].reshape(8, 128, 256)
            u = up[a][:, b * 128:(b + 2) * 128].reshape(8, 128, 256)
            out[i] = np.concatenate([g, u], axis=2).transpose(1, 0, 2).reshape(128, 4096)
        elif k == "GU1":
            g = gate[a][:, b * 128:(b + 1) * 128].reshape(8, 128, 128)
            u = up[a][:, b * 128:(b + 1) * 128].reshape(8, 128, 128)
            out[i, :, :2048] = np.concatenate([g, u], axis=2).transpose(1, 0, 2).reshape(128, 2048)
        elif k == "DN":
            half, q = b // 4, b % 4
            blk = down[a][half * 1408:(half + 1) * 1408, q * 256:(q + 1) * 256]
            out[i, :, :2816] = blk.reshape(11, 128, 256).transpose(1, 0, 2).reshape(128, 2816)
        elif k == "PW":
            out[i, :, :2048] = pool_w.reshape(4, 2, 128, 256).transpose(2, 0, 1, 3).reshape(128, 2048)
    return out


V_NM = (0, 8)
V_NF = (16, 24)
V_NFIN = 32
V_PS = 40
V_CW = 48
V_CB = 64
V_RGB = 68
V_IGB = 72
V_LAM = 76
NV = 80
C_ID = 0
C_TRI = 128
C_MASK = 384
C_ONES = 512
C_INV = 640
C_M32 = 704
C_MASK2 = 960
NCST = 1216


def host_consts():
    c = np.zeros((128, NCST), np.float32)
    c[:, C_ID:C_ID + 128] = np.eye(128, dtype=np.float32)
    j = np.arange(128)[:, None]
    k = np.arange(128)[None, :]
    c[:, C_TRI:C_TRI + 128] = -1.0 * (j >= k)
    c[:, C_TRI + 128:C_TRI + 256] = -1.0 * (j < k)
    c[:, C_MASK:C_MASK + 128] = (j < k)
    c[:, C_ONES:C_ONES + 128] = 1.0 / D
    c[:, C_MASK2:C_MASK2 + 128] = (j < k)
    c[:, C_MASK2 + 128:C_MASK2 + 256] = (j < k)
    for g, w in enumerate(POOL_WINDOWS):
        c[:, C_INV + 16 * g:C_INV + 16 * (g + 1)] = 1.0 / np.minimum(w, np.arange(16) + 1.0)
    m32 = (np.arange(32)[:, None] < np.arange(32)[None, :]).astype(np.float32)
    c[:32, C_M32:C_M32 + 256] = np.tile(m32, (1, 8))
    return c


def fm(v):
    return np.ascontiguousarray(v.reshape(-1, 128).T)


class Buf:
    __slots__ = ("t", "writer", "readers", "name")

    def __init__(self, t, name=""):
        self.t = t
        self.writer = None
        self.readers = {}
        self.name = name

    def __getitem__(self, idx):
        return self.t[idx]


class Sched:
    def __init__(self, nc, es, n_epochs=6, n_dma_sems=12):
        self.nc = nc
        self.eng = {"pe": nc.tensor, "act": nc.scalar, "dve": nc.vector, "pool": nc.gpsimd, "sp": nc.sync}
        self.sems = {}
        self.count = {}
        self.waited = {k: {} for k in self.eng}
        for k in self.eng:
            self.count[k] = 0
            if k == "sp":
                continue
            self.sems[k] = [es.enter_context(nc.semaphore(name=f"s_{k}{i}")) for i in range(n_epochs)]
        self.dsems = {}
        self.dcount = {}
        self.dnext = {}
        for q in ("sp", "act"):
            self.dsems[q] = [es.enter_context(nc.semaphore(name=f"d_{q}{i}")) for i in range(n_dma_sems)]
            self.dcount[q] = [0] * n_dma_sems
            self.dnext[q] = 0
        self.semobj = {}
        for k, lst in self.sems.items():
            for i, s in enumerate(lst):
                self.semobj[(k, i)] = s
        for q, lst in self.dsems.items():
            for i, s in enumerate(lst):
                self.semobj[("d" + q, i)] = s
        self.nwaits = 0

    def _wait(self, engname, key, val):
        w = self.waited[engname]
        if w.get(key, 0) >= val:
            return
        self.eng[engname].wait_ge(self.semobj[key], val)
        w[key] = val
        self.nwaits += 1

    def _deps(self, engname, reads, writes):
        need = {}
        for b in reads:
            if b.writer is not None:
                k, v = b.writer
                if need.get(k, 0) < v:
                    need[k] = v
        for b in writes:
            if b.writer is not None:
                k, v = b.writer
                if need.get(k, 0) < v:
                    need[k] = v
            for k, v in b.readers.items():
                if need.get(k, 0) < v:
                    need[k] = v
        for k, v in need.items():
            if engname == "pe" and k[0] == "pe":
                continue
            self._wait(engname, k, v)

    def _mark(self, tok, reads, writes):
        k, v = tok
        for b in reads:
            if b.readers.get(k, 0) < v:
                b.readers[k] = v
        for b in writes:
            b.writer = tok
            b.readers = {}

    def op(self, engname, fn, reads=(), writes=()):
        self._deps(engname, reads, writes)
        inst = fn(self.eng[engname])
        c = self.count[engname]
        ep, v = c // EPOCH, c % EPOCH + 1
        inst.then_inc(self.sems[engname][ep], 1)
        self.count[engname] = c + 1
        tok = ((engname, ep), v)
        self._mark(tok, reads, writes)
        return tok

    def dma(self, q, out, in_, reads=(), writes=(), **kw):
        self._deps(q, reads, writes)
        i = self.dnext[q]
        self.dnext[q] = (i + 1) % len(self.dsems[q])
        key = ("d" + q, i)
        prev = self.dcount[q][i]
        if prev > 0:
            self._wait(q, key, prev)
        inst = self.eng[q].dma_start(out=out, in_=in_, **kw)
        val = prev + 16
        inst.then_inc(self.dsems[q][i], 16)
        self.dcount[q][i] = val
        tok = (key, val)
        self._mark(tok, reads, writes)
        return tok

    def barrier(self):
        toks = {}
        for k in ("pe", "act", "dve", "pool"):
            c = self.count[k]
            if c > 0:
                toks[(k, (c - 1) // EPOCH)] = (c - 1) % EPOCH + 1
        for q in ("sp", "act"):
            for i, v in enumerate(self.dcount[q]):
                if v > 0:
                    toks[("d" + q, i)] = v
        for e in ("pe", "act", "dve", "pool", "sp"):
            for key, v in toks.items():
                if key[0] == e:
                    continue
                self._wait(e, key, v)

    def finish(self, bufs):
        for b in bufs:
            if b.writer is not None:
                self._wait("sp", b.writer[0], b.writer[1])


def build(SEQ, NS, LS, PAST):
    T = 512
    NT = SEQ // T
    TS = NS * LS
    NPB = PAST // 128
    nc = bass.Bass("TRN2", target_bir_lowering=False)

    def din(name, shape, dt=F32):
        return nc.dram_tensor(name, list(shape), dt, kind="ExternalInput").ap()

    def dout(name, shape):
        return nc.dram_tensor(name, list(shape), F32, kind="ExternalOutput").ap()

    xp_d = din("xp", (SEQ, D))
    xs_d = din("xs", (TS, D))
    ck_d = din("ck", (NS, PAST, 512))
    cv_d = din("cv", (NS, PAST, 512))
    h0_d = din("h0", (128, 4, NS))
    conv0_d = din("conv0", (128, 4, NS, 3))
    pool0_d = din("pool0", (128, 8, NS, 15))
    wsrc_d = din("wsrc", (NCH, 128, WSLOT))
    vecs_d = din("vecs", (128, NV))
    gatew_d = din("gatew", (128, 8 * 128))
    cst_d = din("cst", (128, NCST))
    yp_d = dout("yp", (SEQ, D))
    ys_d = dout("ys", (TS, D))
    kp_d = dout("kp", (SEQ, 512))
    vp_d = dout("vp", (SEQ, 512))
    hp_d = dout("hp", (1, 512))
    convp_d = dout("convp", (3, 512))
    poolp_d = dout("poolp", (15, D))
    ks_d = dout("ks", (TS, 512))
    vs_d = dout("vs", (TS, 512))
    hs_d = dout("hs", (NS, 512))
    convs_d = dout("convs", (NS * 3, 512))
    pools_d = dout("pools", (NS * 15, D))
    wscr_d = nc.dram_tensor("wscr", [NCH, 128, WSLOT], BF16, kind="Internal").ap()

    es = ExitStack()
    with es:
        S = Sched(nc, es)
        cnt = [0]
        dbg_list = []

        def dbg(name, buf, ap, shape, dt=F32):
            if not DEBUG:
                return
            d = nc.dram_tensor('dbg_' + name, list(shape), dt, kind='ExternalOutput').ap()
            S.dma('sp', d, ap, reads=[buf])
            dbg_list.append(name)

        def sb(shape, dt, name=None):
            cnt[0] += 1
            nm = "sb_" + (name or f"t{cnt[0]}")
            return Buf(es.enter_context(nc.sbuf_tensor(nm, list(shape), dt)), nm)

        P2 = [es.enter_context(nc.psum_tensor(f"pbank{i}", [128, 1024], F32)) for i in range(4)]
        banks = [Buf(P2[i // 2][:, (i % 2) * 512:(i % 2 + 1) * 512], f"bank{i}") for i in range(8)]
        bank_rr = [0]

        def bank():
            b = banks[bank_rr[0] % 8]
            bank_rr[0] += 1
            return b

        KT = [[sb([128, T], BF16) for _ in range(NT)] for _ in range(4)]
        VV = [sb([128, 4, 512], BF16) for _ in range(max(NT, 8))]
        XT = sb([128, 8, T], F32, "XT")
        xstage = [sb([128, D], F32) for _ in range(2)]
        XN = sb([128, 8, T], BF16, "XN")
        XNB = [Buf(XN.t[:, c_, :], f"xn{c_}") for c_ in range(8)]
        HH = sb([128, 12 * T], BF16, "HH")
        hh3 = HH.t[:].rearrange("p (f t) -> p f t", t=T)
        HHQ, HHM, HHA = Buf(HH.t[:, 0:4 * T], "hhq"), Buf(HH.t[:, 4 * T:8 * T], "hhm"), Buf(HH.t[:, 8 * T:12 * T], "hha")
        HHALL = [HHQ, HHM, HHA]

        def QT(j, cols):
            return HH.t[:, j * T + cols.start:j * T + cols.stop]

        def MIX(c, cols):
            return HH.t[:, (4 + c) * T + cols.start:(4 + c) * T + cols.stop]

        UE = [sb([128, 3 + T], F32) for _ in range(4)]
        PH = [sb([128, 15], F32) for _ in range(8)]
        HC = sb([128, 4], F32, "HC")
        HCs = sb([128, 4, NS], F32, "HCs")
        NTMP = 12
        TB = [es.enter_context(nc.sbuf_tensor(f"sb_tb{k_}", [128, 1088], F32)) for k_ in range(NTMP // 2)]
        tmps = [Buf(TB[k_ // 2][:, (k_ % 2) * 544:(k_ % 2 + 1) * 544], f"tmp{k_}") for k_ in range(NTMP)]

        def tmp2():
            if tmp_rr[0] % 2 == 1:
                tmp_rr[0] += 1
            k_ = (tmp_rr[0] % NTMP) // 2
            a_, b_ = tmps[2 * k_], tmps[2 * k_ + 1]
            tmp_rr[0] += 2
            return a_, b_, TB[k_][:, :].rearrange("p (s l) -> p s l", s=2)[:, :, 0:T]
        tmp_rr = [0]

        def tmp():
            b = tmps[tmp_rr[0] % NTMP]
            tmp_rr[0] += 1
            return b

        LB = [sb([128, 2, T], BF16) for _ in range(3)]
        lbuf = [Buf(LB[k_].t[:, 0, :], f"lb{k_}") for k_ in range(3)]
        WB = [sb([128, 2, T], BF16) for _ in range(2)]
        wbuf = [Buf(WB[k_ % 2].t[:, k_ // 2, :], f"wb{k_}") for k_ in range(3)]
        ucb = [sb([128, T], BF16) for _ in range(2)]
        dB = ucb
        kvst = [sb([128, 512], F32) for _ in range(2)]
        wring = [sb([128, WSLOT], BF16) for _ in range(3)]
        wstg = [sb([128, 1024], F32) for _ in range(2)]
        KTb = [Buf(VV[5].t[:, k_, :].rearrange("p (j t) -> p j t", j=4), f"ktb{k_}") for k_ in range(2)]
        Vb = [Buf(VV[6].t[:, k_, :], f"vb{k_}") for k_ in range(3)]
        Vnew = [Buf(VV[s_].t[:, 0, :], f"vnew{s_}") for s_ in range(NS)]
        QTZ = Buf(VV[4].t[:, 0:2, :].rearrange("p a n -> p (a n)")[:, 0:8 * TS].rearrange("p (j i t) -> p j i t", j=4, i=2), "qtz")
        UEs = [Buf(VV[c_].t[:, 1, :].bitcast(F32)[:, 0:NS * (3 + LS)].rearrange("p (s l) -> p s l", s=NS), f"ues{c_}") for c_ in range(4)]
        PHs = [Buf(VV[c_].t[:, 3, :].bitcast(F32)[:, 128:128 + NS * 15].rearrange("p (s l) -> p s l", s=NS), f"phs{c_}") for c_ in range(8)]
        KTs = [Buf(VV[7].t[:, j_, 0:TS], f"kts{j_}") for j_ in range(4)]
        cst = sb([128, NCST], F32, "cst")
        vecs = sb([128, NV], F32, "vecs")
        cvec = sb([128, 16], F32, "cvec")
        gwf = wstg[0]
        gwb = sb([128, 1024], BF16, "gwb")
        trib = sb([128, 256], BF16, "trib")
        onesb = sb([128, 128], BF16, "onesb")
        sq = [sb([128, T], BF16) for _ in range(2)]
        rstd = sb([128, T], F32, "rstd")
        epsv = sb([128, 1], F32, "epsv")
        ost = xstage[0]
        WS = [Buf(None, f"ws{i}") for i in range(NCH)]

        ident = cst.t[:, C_ID:C_ID + 128]
        mask = cst.t[:, C_MASK:C_MASK + 128]
        m32 = cst.t[0:32, C_M32:C_M32 + 256]
        mask2 = cst.t[:, C_MASK2:C_MASK2 + 256].rearrange("p (s k) -> p s k", s=2)

        S.op("pool", lambda e: e.memset(epsv[:], EPS), writes=[epsv])
        S.dma("sp", cst[:], cst_d, writes=[cst])
        S.dma("sp", vecs[:], vecs_d, writes=[vecs])
        S.dma("sp", gwf[:], gatew_d, writes=[gwf])
        S.op("dve", lambda e: e.tensor_copy(out=trib[:], in_=cst.t[:, C_TRI:C_TRI + 256]), reads=[cst], writes=[trib])
        S.op("dve", lambda e: e.tensor_copy(out=onesb[:], in_=cst.t[:, C_ONES:C_ONES + 128]), reads=[cst], writes=[onesb])
        S.op("pool", lambda e: e.tensor_copy(out=gwb[:], in_=gwf[:]), reads=[gwf], writes=[gwb])
        S.op("act", lambda e: e.activation(out=cvec.t[:, 8:12], in_=vecs.t[:, V_LAM:V_LAM + 4], func=AF.Exp, scale=-1.0), reads=[vecs], writes=[cvec])
        S.op("act", lambda e: e.activation(out=cvec.t[:, 12:16], in_=cvec.t[:, 8:12], func=AF.Ln, bias=1.0), reads=[cvec], writes=[cvec])
        S.op("dve", lambda e: e.tensor_scalar(out=cvec.t[:, 0:4], in0=cvec.t[:, 12:16], scalar1=-8.0, scalar2=0.0, op0=ALU.mult, op1=ALU.add), reads=[cvec], writes=[cvec])
        S.op("dve", lambda e: e.tensor_scalar(out=cvec.t[:, 4:8], in0=cvec.t[:, 12:16], scalar1=-16.0, scalar2=0.0, op0=ALU.mult, op1=ALU.add), reads=[cvec], writes=[cvec])
        for c in range(4):
            S.op("pool", lambda e, c=c: e.memset(UE[c].t[:, 0:3], 0.0), writes=[UE[c]])
        S.op("pool", lambda e: e.memset(HC[:], 0.0), writes=[HC])
        S.dma("sp", HCs[:], h0_d, writes=[HCs])
        for c in range(8):
            S.op("pool", lambda e, c=c: e.memset(PH[c][:], 0.0), writes=[PH[c]])

        order = []
        for ti in range(NT + 1):
            for ci in range(NCH):
                order.append((ti, ci))
        wstate = {"emitted": 0, "next": 0, "stg": 0, "cast": 0}
        LOOK = 2

        stg_pool = list(wstg) + [Buf(VV[k_].t[:, :, :].rearrange("p a n -> p (a n)").bitcast(F32), f"vstg{k_}") for k_ in range(1, len(VV))]
        NSTG = (len(stg_pool) // 4) * 4
        LD = NSTG // 4 - 1
        wload = {"emitted": 0, "map": {}}

        def w_load(i):
            ti, ci = order[i]
            if ti != 0:
                return
            nel = CHUNKS[ci][3]
            q = nel // 4
            lst = []
            for k in range(4):
                st = stg_pool[wstate["stg"] % NSTG]
                wstate["stg"] += 1
                S.dma("sp", st.t[:, 0:q], wsrc_d[ci, :, k * q:(k + 1) * q], writes=[st])
                lst.append(st)
            wload["map"][i] = lst

        def w_emit(i):
            ti, ci = order[i]
            buf = wring[i % 3]
            nel = CHUNKS[ci][3]
            if ti == 0:
                while wload["emitted"] <= min(i + LD, NCH - 1):
                    w_load(wload["emitted"])
                    wload["emitted"] += 1
                q = nel // 4
                eng = "act" if (wstate["cast"] % 2 == 1) else "dve"
                wstate["cast"] += 1
                for k, st in enumerate(wload["map"].pop(i)):
                    if eng == "act":
                        S.op("act", lambda e, k=k, st=st: e.activation(out=buf.t[:, k * q:(k + 1) * q], in_=st.t[:, 0:q], func=AF.Copy), reads=[st], writes=[buf])
                    else:
                        S.op("dve", lambda e, k=k, st=st: e.tensor_copy(out=buf.t[:, k * q:(k + 1) * q], in_=st.t[:, 0:q]), reads=[st], writes=[buf])
                S.dma("sp", wscr_d[ci, :, 0:nel], buf.t[:, 0:nel], reads=[buf], writes=[WS[ci]])
            else:
                S.dma("sp", buf.t[:, 0:nel], wscr_d[ci, :, 0:nel], reads=[WS[ci]], writes=[buf])

        def wnext(kind):
            i = wstate["next"]
            wstate["next"] += 1
            assert CHUNKS[order[i][1]][0] == kind, (CHUNKS[order[i][1]], kind)
            while wstate["emitted"] <= min(i + LOOK, len(order) - 1):
                w_emit(wstate["emitted"])
                wstate["emitted"] += 1
            return wring[i % 3]

        def mm(out, lhsT, rhs, start, stop, R, W):
            S.op("pe", lambda e: e.matmul(out, lhsT=lhsT, rhs=rhs, start=start, stop=stop, skip_group_check=True), reads=R, writes=W)

        def norm_stats(TW):
            bk = bank()
            for c in range(8):
                s = sq[c % 2]
                if c % 2 == 0:
                    S.op("act", lambda e, c=c, s=s: e.activation(out=s.t[:, :TW], in_=XT.t[:, c, :TW], func=AF.Square), reads=[XT], writes=[s])
                else:
                    S.op("dve", lambda e, c=c, s=s: e.tensor_tensor(out=s.t[:, :TW], in0=XT.t[:, c, :TW], in1=XT.t[:, c, :TW], op=ALU.mult), reads=[XT], writes=[s])
                mm(bk.t[:, :TW], onesb[:], s.t[:, :TW], c == 0, c == 7, [onesb, s], [bk])
            t1 = tmp()
            S.op("act", lambda e: e.activation(out=t1.t[:, :TW], in_=bk.t[:, :TW], func=AF.Ln, bias=epsv.t[:, 0:1]), reads=[bk, epsv], writes=[t1])
            S.op("act", lambda e: e.activation(out=rstd.t[:, :TW], in_=t1.t[:, :TW], func=AF.Exp, scale=-0.5), reads=[t1], writes=[rstd])

        def norm_to_xn(TW, gcol):
            norm_stats(TW)
            for c in range(8):
                norm_apply(c, gcol, TW, XN.t[:, c, :TW], XNB[c])

        def norm_apply(c, gcol, TW, out_ap, out_buf, in_ap=None, rs_ap=None, force_dve=False):
            in_ap = XT.t[:, c, :TW] if in_ap is None else in_ap
            rs_ap = rstd.t[:, :TW] if rs_ap is None else rs_ap
            gsc = vecs.t[:, gcol + c:gcol + c + 1]
            if c % 4 != 3 or force_dve:
                S.op("dve", lambda e: e.scalar_tensor_tensor(out=out_ap, in0=in_ap, scalar=gsc, in1=rs_ap, op0=ALU.mult, op1=ALU.mult), reads=[XT, vecs, rstd], writes=[out_buf])
            else:
                t_ = tmp()
                tv = t_.t[:, :TW] if len(in_ap.shape) == 2 else t_.t[:, :TW].rearrange("p (s l) -> p s l", s=in_ap.shape[1])
                S.op("act", lambda e: e.activation(out=tv, in_=in_ap, func=AF.Copy, scale=gsc), reads=[XT, vecs], writes=[t_])
                S.op("pool", lambda e: e.tensor_tensor(out=out_ap, in0=tv, in1=rs_ap, op=ALU.mult), reads=[t_, rstd], writes=[out_buf])

        def proj_fm(wb, j, TW, rhs_fn, R):
            bk = bank()
            w3 = wb.t[:].rearrange("p (c n) -> p c n", c=8)
            for c in range(8):
                mm(bk.t[:, :TW], w3[:, c, j * 128:(j + 1) * 128], rhs_fn(c), c == 0, c == 7, [wb] + (R(c) if callable(R) else R), [bk])
            return bk

        xpref = set()

        def process_tile(ti):
            sample = ti == NT
            TW = TS if sample else T
            nseg = NS if sample else 1
            L = LS if sample else T
            first = ti == 0
            t0 = 0 if sample else ti * T
            x_d = xs_d if sample else xp_d
            ntb = TW // 128
            cols = slice(0, TW)

            if sample:
                S.barrier()
                for s_ in range(NS):
                    S.op("pool", lambda e, s_=s_: e.memset(Vnew[s_].t[:, :], 0.0), writes=[Vnew[s_]])
                S.op("pool", lambda e: e.memset(QTZ.t[:, :, :, :], 0.0), writes=[QTZ])
                for c_ in range(4):
                    S.dma("sp", UEs[c_].t[:, :, 0:3], conv0_d[:, c_], writes=[UEs[c_]])
                for c_ in range(8):
                    S.dma("sp", PHs[c_].t[:, :, :], pool0_d[:, c_], writes=[PHs[c_]])
            for tb in range(ntb):
                xs_ = xstage[tb % 2]
                if (ti, tb) not in xpref:
                    S.dma("sp", xs_[:], x_d[t0 + tb * 128:t0 + (tb + 1) * 128, :], writes=[xs_])
                for half in range(2):
                    bk = bank()
                    for jj in range(4):
                        c = half * 4 + jj
                        S.op("pe", lambda e, c=c, jj=jj, bk=bk, xs_=xs_: e.transpose(out=bk.t[:, jj * 128:(jj + 1) * 128], in_=xs_.t[:, c * 128:(c + 1) * 128], identity=ident), reads=[xs_, cst], writes=[bk])
                    eng = "act" if half == 0 else "dve"
                    S.op(eng, lambda e, half=half, bk=bk, tb=tb: (e.activation(out=XT.t[:, half * 4:half * 4 + 4, tb * 128:(tb + 1) * 128], in_=bk.t[:].rearrange("p (j t) -> p j t", j=4), func=AF.Copy) if half == 0 else
                                                                   e.tensor_copy(out=XT.t[:, half * 4:half * 4 + 4, tb * 128:(tb + 1) * 128], in_=bk.t[:].rearrange("p (j t) -> p j t", j=4))), reads=[bk], writes=[XT])

            ckpt(100 * ti + 1)
            norm_to_xn(TW, V_NM[0])
            xn_rhs = lambda c: XN.t[:, c, :TW]
            if ti == 0:
                dbg('xt0', XT, XT.t[:, :, :], [128, 8, T])
                pass
                dbg('rstd0', rstd, rstd.t[:, :], [128, T])

            ckpt(100 * ti + 2)
            if sample:
                tblocks = [(s_ * LS, LS, s_ * LS) for s_ in range(NS)]
            else:
                tblocks = [(tb * 128, 128, t0 + tb * 128) for tb in range(ntb)]
            k_out = ks_d if sample else kp_d
            v_out = vs_d if sample else vp_d
            hc = HCs if sample else HC
            gw3 = gwb.t[:].rearrange("p (k n) -> p k n", k=8)

            def v3(ap2):
                return ap2.rearrange("p (s l) -> p s l", s=nseg)

            def ue_of(c):
                ue = UEs[c] if sample else UE[c]
                return ue, (ue.t[:] if sample else ue.t[:].rearrange("p (s l) -> p s l", s=1))

            wu = wnext("KN")
            for c in range(4):
                ue, ue3 = ue_of(c)
                bku = proj_fm(wu, c, TW, xn_rhs, lambda cc: [XNB[cc]])
                S.op("act", lambda e, bku=bku, ue3=ue3: e.activation(out=ue3[:, :, 3:3 + L], in_=v3(bku.t[:, :TW]), func=AF.Copy), reads=[bku], writes=[ue])
            wg = wnext("KN")
            gts, g2s = [], []
            for c in range(4):
                bkg = proj_fm(wg, c, TW, xn_rhs, lambda cc: [XNB[cc]])
                gt = tmp()
                S.op("act", lambda e, bkg=bkg, gt=gt: e.activation(out=gt.t[:, :TW], in_=bkg.t[:, :TW], func=AF.Copy), reads=[bkg], writes=[gt])
                gts.append(gt)
            for c in range(4):
                gt = gts[c]
                g2 = tmp()
                g2s.append(g2)
                S.op("pool", lambda e, gt=gt, g2=g2: e.tensor_tensor(out=g2.t[:, :TW], in0=gt.t[:, :TW], in1=gt.t[:, :TW], op=ALU.mult), reads=[gt], writes=[g2])
                S.op("dve", lambda e, g2=g2: e.tensor_scalar(out=g2.t[:, :TW], in0=g2.t[:, :TW], scalar1=0.044715, scalar2=1.0, op0=ALU.mult, op1=ALU.add), reads=[g2], writes=[g2])
                S.op("pool", lambda e, gt=gt, g2=g2: e.tensor_tensor(out=g2.t[:, :TW], in0=g2.t[:, :TW], in1=gt.t[:, :TW], op=ALU.mult), reads=[g2, gt], writes=[g2])
            for c in range(4):
                g2 = g2s[c]
                S.op("act", lambda e, g2=g2: e.activation(out=g2.t[:, :TW], in_=g2.t[:, :TW], func=AF.Sigmoid, scale=1.5957691216057308), reads=[g2], writes=[g2])
            for c in range(4):
                gt, g2 = gts[c], g2s[c]
                S.op("dve", lambda e, gt=gt, g2=g2, c=c: e.tensor_tensor(out=ATT(c, cols), in0=gt.t[:, :TW], in1=g2.t[:, :TW], op=ALU.mult), reads=[gt, g2], writes=[HHA])

            def piece_q():
                wq = wnext("KN")
                for j in range(4):
                    bk = proj_fm(wq, j, TW, xn_rhs, lambda c: [XNB[c]])
                    if sample:
                        S.op("act", lambda e, j=j, bk=bk: e.activation(out=QTZ.t[0:64, j, 0, :], in_=bk.t[0:64, :TW], func=AF.Copy, scale=0.125), reads=[bk], writes=[QTZ])
                        S.op("dve", lambda e, j=j, bk=bk: e.tensor_scalar(out=QTZ.t[64:128, j, 1, :], in0=bk.t[64:128, :TW], scalar1=0.125, scalar2=0.0, op0=ALU.mult, op1=ALU.add), reads=[bk], writes=[QTZ])
                    else:
                        S.op("act", lambda e, j=j, bk=bk: e.activation(out=QT(j, cols), in_=bk.t[:, :TW], func=AF.Copy, scale=0.125), reads=[bk], writes=[HHQ])

            kstate = {}

            def piece_kfm():
                wk = wnext("KN")
                kstate["wk"] = wk
                for j in range(4):
                    bk = proj_fm(wk, j, TW, xn_rhs, lambda c: [XNB[c]])
                    dst = KTs[j] if sample else KT[j][ti]
                    S.op("dve", lambda e, bk=bk, dst=dst: e.tensor_copy(out=dst.t[:, :TW], in_=bk.t[:, :TW]), reads=[bk], writes=[dst])

            def piece_ktm():
                wk = kstate["wk"]
                wk3 = wk.t[:].rearrange("p (c n) -> p c n", c=8)
                for bi, (c0, n, r0) in enumerate(tblocks):
                    bk = bank()
                    for c in range(8):
                        mm(bk.t[0:n, :], XN.t[:, c, c0:c0 + n], wk3[:, c, :], c == 0, c == 7, [XNB[c], wk], [bk])
                    st = kvst[bi % 2]
                    S.op("act", lambda e, bk=bk, st=st, n=n: e.activation(out=st.t[0:n, :], in_=bk.t[0:n, :], func=AF.Copy), reads=[bk], writes=[st])
                    S.dma("sp", k_out[r0:r0 + n, :], st.t[0:n, :], reads=[st])

            def piece_v():
                wv = wnext("KN")
                wv3 = wv.t[:].rearrange("p (c n) -> p c n", c=8)
                for bi, (c0, n, r0) in enumerate(tblocks):
                    bk = bank()
                    for c in range(8):
                        mm(bk.t[0:n, :], XN.t[:, c, c0:c0 + n], wv3[:, c, :], c == 0, c == 7, [XNB[c], wv], [bk])
                    st = kvst[(bi + 1) % 2]
                    S.op("act", lambda e, bk=bk, st=st, n=n: e.activation(out=st.t[0:n, :], in_=bk.t[0:n, :], func=AF.Copy), reads=[bk], writes=[st])
                    S.dma("sp", v_out[r0:r0 + n, :], st.t[0:n, :], reads=[st])
                    if sample:
                        S.op("dve", lambda e, bk=bk, bi=bi, n=n: e.tensor_copy(out=Vnew[bi].t[0:n, :], in_=bk.t[0:n, :]), reads=[bk], writes=[Vnew[bi]])
                    else:
                        S.op("dve", lambda e, bk=bk, bi=bi: e.tensor_copy(out=VV[ti].t[:, bi, :], in_=bk.t[:, :]), reads=[bk], writes=[VV[ti]])

            pieces = [piece_q, piece_kfm, piece_ktm, piece_v]

            def lruA(c):
                ue, ue3 = ue_of(c)
                uc = tmp()
                cw = lambda i: vecs.t[:, V_CW + 4 * i + c:V_CW + 4 * i + c + 1]
                S.op("dve", lambda e: e.tensor_scalar(out=v3(uc.t[:, :TW]), in0=ue3[:, :, 3:3 + L], scalar1=cw(3), scalar2=vecs.t[:, V_CB + c:V_CB + c + 1], op0=ALU.mult, op1=ALU.add), reads=[ue, vecs], writes=[uc])
                for i in (2, 1, 0):
                    S.op("dve", lambda e, i=i: e.scalar_tensor_tensor(out=v3(uc.t[:, :TW]), in0=ue3[:, :, i:i + L], scalar=cw(i), in1=v3(uc.t[:, :TW]), op0=ALU.mult, op1=ALU.add), reads=[ue, vecs, uc], writes=[uc])
                S.op("pool", lambda e: e.tensor_copy(out=ue3[:, :, 0:3], in_=ue3[:, :, L:L + 3]), reads=[ue], writes=[ue])
                ub = ucb[c % 2]
                S.op("act", lambda e: e.activation(out=ub.t[:, :TW], in_=uc.t[:, :TW], func=AF.Copy), reads=[uc], writes=[ub])
                return uc, ub

            def lruB(c, uc, ub):
                bkr = bank()
                mm(bkr.t[:, :TW], gw3[:, c, :], ub.t[:, :TW], True, True, [gwb, ub], [bkr])
                bki = bank()
                mm(bki.t[:, :TW], gw3[:, 4 + c, :], ub.t[:, :TW], True, True, [gwb, ub], [bki])
                rr = tmp()
                ii = tmp()
                S.op("act", lambda e: e.activation(out=rr.t[:, :TW], in_=bkr.t[:, :TW], func=AF.Sigmoid, bias=vecs.t[:, V_RGB + c:V_RGB + c + 1]), reads=[bkr, vecs], writes=[rr])
                S.op("act", lambda e: e.activation(out=ii.t[:, :TW], in_=bki.t[:, :TW], func=AF.Sigmoid, bias=vecs.t[:, V_IGB + c:V_IGB + c + 1]), reads=[bki, vecs], writes=[ii])
                a2 = tmp()
                aa = rr
                S.op("act", lambda e: e.activation(out=a2.t[:, :TW], in_=rr.t[:, :TW], func=AF.Exp, scale=cvec.t[:, 4 + c:5 + c]), reads=[rr, cvec], writes=[a2])
                S.op("act", lambda e: e.activation(out=aa.t[:, :TW], in_=rr.t[:, :TW], func=AF.Exp, scale=cvec.t[:, c:c + 1]), reads=[rr, cvec], writes=[aa])
                S.op("act", lambda e: e.activation(out=a2.t[:, :TW], in_=a2.t[:, :TW], func=AF.Ln, scale=-1.0, bias=1.0), reads=[a2], writes=[a2])
                S.op("act", lambda e: e.activation(out=a2.t[:, :TW], in_=a2.t[:, :TW], func=AF.Exp, scale=0.5), reads=[a2], writes=[a2])
                S.op("pool", lambda e: e.tensor_tensor(out=ii.t[:, :TW], in0=ii.t[:, :TW], in1=uc.t[:, :TW], op=ALU.mult), reads=[ii, uc], writes=[ii])
                S.op("dve", lambda e: e.tensor_tensor(out=ii.t[:, :TW], in0=ii.t[:, :TW], in1=a2.t[:, :TW], op=ALU.mult), reads=[ii, a2], writes=[ii])
                hh_ = a2
                for s_ in range(nseg):
                    init = hc.t[:, c, s_:s_ + 1] if sample else hc.t[:, c:c + 1]
                    S.op("dve", lambda e, s_=s_, init=init: e.tensor_tensor_scan(out=hh_.t[:, s_ * L:(s_ + 1) * L], data0=aa.t[:, s_ * L:(s_ + 1) * L], data1=ii.t[:, s_ * L:(s_ + 1) * L], initial=init, op0=ALU.mult, op1=ALU.add), reads=[aa, ii, hc], writes=[hh_])
                hdst = hc.t[:, c, :] if sample else hc.t[:, c:c + 1]
                S.op("pool", lambda e: e.tensor_copy(out=hdst, in_=v3(hh_.t[:, :TW])[:, :, L - 1]), reads=[hh_], writes=[hc])
                S.op("dve", lambda e: e.tensor_tensor(out=MIX(c, cols), in0=hh_.t[:, :TW], in1=ATT(c, cols), op=ALU.mult), reads=[hh_, HHA], writes=[HHM])

            for c in range(4):
                uc, ub = lruA(c)
                pieces[c]()
                lruB(c, uc, ub)

            ckpt(100 * ti + 4)
            if sample:
                attn_sample()
            else:
                attn_prompt(ti)

            ckpt(100 * ti + 5)
            for s_ in range(2):
                wo = wnext("KO")
                for j in range(4):
                    n = s_ * 4 + j
                    bk = proj_fm(wo, j, TW, lambda c: MIXALL(c, TW), lambda c: [HHA] if c < 4 else [HHM])
                    S.op("dve", lambda e, bk=bk, n=n: e.tensor_tensor(out=XT.t[:, n, :TW], in0=bk.t[:, :TW], in1=XT.t[:, n, :TW], op=ALU.add), reads=[bk, XT], writes=[XT])

            ckpt(100 * ti + 6)
            ffn(0, TW)
            ckpt(100 * ti + 7)
            pool_mixer(ti, TW, nseg, L, sample, first)
            ckpt(100 * ti + 8)
            if ti + 1 <= NT:
                nsample = (ti + 1 == NT)
                nx_d = xs_d if nsample else xp_d
                nt0 = 0 if nsample else (ti + 1) * T
                for tb in range(1 if nsample else 2):
                    S.dma("sp", xstage[tb][:], nx_d[nt0 + tb * 128:nt0 + (tb + 1) * 128, :], writes=[xstage[tb]])
                    xpref.add((ti + 1, tb))
            ffn(1, TW)
            ckpt(100 * ti + 9)
            norm_stats(TW)
            for c in range(8):
                norm_apply(c, V_NFIN, TW, XT.t[:, c, :TW], XT)
            y_d = ys_d if sample else yp_d
            for tb in range(ntb):
                ya, yb, y3 = tmp2()
                for half in range(2):
                    bk = bank()
                    for jj in range(4):
                        c = half * 4 + jj
                        S.op("pe", lambda e, c=c, jj=jj, bk=bk, tb=tb: e.transpose(out=bk.t[:, jj * 128:(jj + 1) * 128], in_=XT.t[:, c, tb * 128:(tb + 1) * 128], identity=ident), reads=[XT, cst], writes=[bk])
                    if half == 0:
                        S.op("act", lambda e, bk=bk, y3=y3: e.activation(out=y3[:, 0, :], in_=bk.t[:, :], func=AF.Copy), reads=[bk], writes=[ya])
                    else:
                        S.op("dve", lambda e, bk=bk, y3=y3: e.tensor_copy(out=y3[:, 1, :], in_=bk.t[:, :]), reads=[bk], writes=[yb])
                S.dma("sp", y_d[t0 + tb * 128:t0 + (tb + 1) * 128, :].rearrange("t (h m) -> t h m", h=2), y3, reads=[ya, yb])

        def MIXALL(c, TW):
            if c < 4:
                return HH.t[:, (8 + c) * T:(8 + c) * T + TW]
            return HH.t[:, c * T:c * T + TW]

        def ATT(j, cols):
            return HH.t[:, (8 + j) * T + cols.start:(8 + j) * T + cols.stop]

        def ffn(layer, TW):
            norm_to_xn(TW, V_NF[0] + 8 * layer)
            for half in range(2):
                slot = 0
                for i in range(6):
                    nf = 2 if i < 5 else 1
                    wb = wnext("GU" if nf == 2 else "GU1")
                    w3 = wb.t[:, 0:8 * 2 * nf * 128].rearrange("p (c n) -> p c n", c=8)
                    for f in range(nf):
                        bg = bank()
                        for c in range(8):
                            mm(bg.t[:, :TW], w3[:, c, f * 128:(f + 1) * 128], XN.t[:, c, :TW], c == 0, c == 7, [wb, XNB[c]], [bg])
                        bu = bank()
                        for c in range(8):
                            mm(bu.t[:, :TW], w3[:, c, (nf + f) * 128:(nf + f + 1) * 128], XN.t[:, c, :TW], c == 0, c == 7, [wb, XNB[c]], [bu])
                        sg = tmp()
                        S.op("act", lambda e, bg=bg, sg=sg: e.activation(out=sg.t[:, :TW], in_=bg.t[:, :TW], func=AF.Silu), reads=[bg], writes=[sg])
                        S.op("dve", lambda e, bu=bu, sg=sg, slot=slot: e.tensor_tensor(out=hh3[:, slot, :TW], in0=bu.t[:, :TW], in1=sg.t[:, :TW], op=ALU.mult), reads=[bu, sg], writes=HHALL)
                        slot += 1
                for q in range(4):
                    wb = wnext("DN")
                    w3 = wb.t[:, 0:2816].rearrange("p (f n) -> p f n", f=11)
                    for o in range(2):
                        n = q * 2 + o
                        bk = bank()
                        for f in range(11):
                            mm(bk.t[:, :TW], w3[:, f, o * 128:(o + 1) * 128], hh3[:, f, :TW], f == 0, f == 10, [wb] + HHALL, [bk])
                        S.op("dve", lambda e, bk=bk, n=n: e.tensor_tensor(out=XT.t[:, n, :TW], in0=bk.t[:, :TW], in1=XT.t[:, n, :TW], op=ALU.add), reads=[bk, XT], writes=[XT])

        def pool_mixer(ti, TW, nseg, L, sample, first):
            norm_stats(TW)
            wb = wnext("PW")
            w4 = wb.t[:, 0:2048].rearrange("p (g c n) -> p g c n", g=4, c=2)
            W_ = 15 + L
            dB4 = [ucb[0], ucb[1], sq[0], sq[1]]

            def v3w(b):
                return b.t[:, 0:nseg * W_].rearrange("p (s l) -> p s l", s=nseg)

            for half in range(2):
                chunks = [4 * half + k_ for k_ in range(4)]
                st = {}
                for c in chunks:
                    ext, tA, tB = tmp(), tmp(), tmp()
                    st[c] = dict(ext=ext, ext3=v3w(ext), tmps=[tA, tB], cur=ext, cur3=v3w(ext), win=POOL_WINDOWS[c // 2])
                for c in chunks:
                    d_ = st[c]
                    norm_apply(c, V_NM[1], TW, d_["ext3"][:, :, 15:15 + L], d_["ext"], in_ap=XT.t[:, c, :TW].rearrange("p (s l) -> p s l", s=nseg),
                               rs_ap=rstd.t[:, :TW].rearrange("p (s l) -> p s l", s=nseg), force_dve=True)
                for c in chunks:
                    d_ = st[c]
                    ph = PHs[c] if sample else PH[c]
                    ph3 = ph.t[:] if sample else ph.t[:].rearrange("p (s l) -> p s l", s=1)
                    S.op("pool", lambda e, d_=d_, ph3=ph3: e.tensor_copy(out=d_["ext3"][:, :, 0:15], in_=ph3), reads=[ph], writes=[d_["ext"]])
                    S.op("pool", lambda e, d_=d_, ph3=ph3: e.tensor_copy(out=ph3, in_=d_["ext3"][:, :, L:L + 15]), reads=[d_["ext"]], writes=[ph])
                sh, lvl = 1, 0
                while sh < 16:
                    for c in chunks:
                        d_ = st[c]
                        if sh >= d_["win"]:
                            continue
                        nxt = d_["tmps"][lvl % 2]
                        nxt3 = v3w(nxt)
                        lo = 2 * sh - 1
                        eng = "pool" if sh in (1, 4) else "dve"
                        S.op(eng, lambda e, cur3=d_["cur3"], nxt3=nxt3, lo=lo, sh=sh: e.tensor_tensor(out=nxt3[:, :, lo:W_], in0=cur3[:, :, lo:W_], in1=cur3[:, :, lo - sh:W_ - sh], op=ALU.add), reads=[d_["cur"]], writes=[nxt])
                        d_["cur"], d_["cur3"] = nxt, nxt3
                    sh *= 2
                    lvl += 1
                for k_, c in enumerate(chunks):
                    d_ = st[c]
                    win = d_["win"]
                    db = dB4[k_]
                    d_["db"] = db
                    S.op("dve", lambda e, d_=d_, db=db, win=win: e.scalar_tensor_tensor(out=db.t[:, :TW].rearrange("p (s l) -> p s l", s=nseg), in0=d_["cur3"][:, :, 15:15 + L], scalar=1.0 / win, in1=d_["ext3"][:, :, 15:15 + L], op0=ALU.mult, op1=ALU.subtract),
                         reads=[d_["cur"], d_["ext"]], writes=[db])
                    if first:
                        t16 = d_["tmps"][0] if d_["cur"] is d_["tmps"][1] else d_["tmps"][1]
                        g = c // 2
                        S.op("dve", lambda e, d_=d_, t16=t16, g=g: e.tensor_tensor(out=t16.t[:, 0:16], in0=d_["cur"].t[:, 15:31], in1=cst.t[:, C_INV + 16 * g:C_INV + 16 * g + 16], op=ALU.mult), reads=[d_["cur"], cst], writes=[t16])
                        S.op("dve", lambda e, d_=d_, t16=t16, db=db: e.tensor_tensor(out=db.t[:, 0:16], in0=t16.t[:, 0:16], in1=d_["ext"].t[:, 15:31], op=ALU.subtract), reads=[t16, d_["ext"], db], writes=[db])
                for g in (2 * half, 2 * half + 1):
                    for o in range(2):
                        n = 2 * g + o
                        bk = bank()
                        for ci in range(2):
                            db = st[2 * g + ci]["db"]
                            mm(bk.t[:, :TW], w4[:, g, ci, o * 128:(o + 1) * 128], db.t[:, :TW], ci == 0, ci == 1, [wb, db], [bk])
                        S.op("dve", lambda e, bk=bk, n=n: e.scalar_tensor_tensor(out=XT.t[:, n, :TW], in0=bk.t[:, :TW], scalar=vecs.t[:, V_PS + n:V_PS + n + 1], in1=XT.t[:, n, :TW], op0=ALU.mult, op1=ALU.add),
                             reads=[bk, vecs, XT], writes=[XT])

        rot = {"l": 0, "w": 0}

        def attn_prompt(g):
            A2 = P2[0][:, :].rearrange("p (s t) -> p s t", s=2)
            C2 = P2[1][:, :].rearrange("p (s t) -> p s t", s=2)
            Ab, Cb = [banks[0], banks[1]], [banks[2], banks[3]]
            Oall = [[banks[4], banks[5]], [banks[6], banks[7]]]
            nst = 4 * g + 4
            seq = [(hp, st) for hp in range(4) for st in range(nst)]
            info = {}

            def geom(st):
                kb = nst - 1 - st
                jj = kb - 4 * g
                c0 = 128 * jj if jj > 0 else 0
                return kb, jj, c0, kb // 4, kb % 4

            def emitA(x):
                hp, st = seq[x]
                kb, jj, c0, tt, bi = geom(st)
                kt = KT[hp][tt]
                for s_ in range(2):
                    mm(A2[:, s_, c0:T], kt.t[64 * s_:64 * s_ + 64, bi * 128:(bi + 1) * 128], HH.t[64 * s_:64 * s_ + 64, hp * T + c0:hp * T + T], True, True, [kt, HHQ], [Ab[s_]])

            def S1(x):
                hp, st = seq[x]
                kb, jj, c0, tt, bi = geom(st)
                ea, eb, e3 = tmp2()
                S.op("act", lambda e: e.activation(out=e3[:, :, c0:T], in_=A2[:, :, c0:T], func=AF.Exp), reads=Ab, writes=[ea, eb])
                if jj >= 0:
                    S.op("pool", lambda e: e.tensor_tensor(out=e3[:, :, c0:c0 + 128], in0=e3[:, :, c0:c0 + 128], in1=mask2, op=ALU.mult), reads=[ea, eb, cst], writes=[ea, eb])
                k_ = rot["l"] % 3
                rot["l"] += 1
                S.op("act", lambda e: e.activation(out=LB[k_].t[:, :, c0:T], in_=e3[:, :, c0:T], func=AF.Ln, bias=1.0), reads=[ea, eb], writes=[LB[k_]])
                info[x] = (hp, st, c0, tt, bi, ea, eb, e3, k_)

            def S2(x):
                hp, st, c0, tt, bi, ea, eb, e3, k_ = info[x]
                for s_ in range(2):
                    mm(C2[:, s_, c0:T], trib.t[:, 0:128], LB[k_].t[:, s_, c0:T], st == 0, False, [trib, LB[k_]], [Cb[s_]])

            def S2p(x):
                hp, st, c0, tt, bi, ea, eb, e3, k_ = info[x]
                pa, pb, p3 = tmp2()
                S.op("act", lambda e: e.activation(out=p3[:, :, c0:T], in_=C2[:, :, c0:T], func=AF.Exp), reads=Cb, writes=[pa, pb])
                info[x] = info[x] + (pa, pb, p3)

            def S3a(x):
                hp, st, c0, tt, bi, ea, eb, e3, k_, pa, pb, p3 = info[x]
                if st < nst - 1:
                    for s_ in range(2):
                        mm(C2[:, s_, c0:T], trib.t[:, 128:256], LB[k_].t[:, s_, c0:T], False, False, [trib, LB[k_]], [Cb[s_]])

            def S3b(x):
                hp, st, c0, tt, bi, ea, eb, e3, k_, pa, pb, p3 = info[x]
                O = Oall[hp % 2]
                w_ = WB[rot["w"] % 2]
                rot["w"] += 1
                S.op("dve", lambda e: e.tensor_tensor(out=w_.t[:, :, c0:T], in0=e3[:, :, c0:T], in1=p3[:, :, c0:T], op=ALU.mult), reads=[ea, eb, pa, pb], writes=[w_])
                for s_ in range(2):
                    mm(O[s_].t[:, c0:T], VV[tt].t[:, bi, hp * 128:(hp + 1) * 128], w_.t[:, s_, c0:T], st == 0, st == nst - 1, [VV[tt], w_], [O[s_]])
                if st == nst - 1:
                    S.op("act", lambda e: e.activation(out=ATT(hp, slice(0, T))[0:64, :], in_=O[0].t[0:64, :], func=AF.Copy), reads=[O[0]], writes=[HHA])
                    S.op("dve", lambda e: e.tensor_copy(out=ATT(hp, slice(0, T))[64:128, :], in_=O[1].t[64:128, :]), reads=[O[1]], writes=[HHA])
                del info[x]

            N = len(seq)
            emitA(0)
            for n in range(N + 2):
                if n < N:
                    S1(n)
                if n >= 2:
                    S3a(n - 2)
                if 1 <= n < N + 1:
                    S2(n - 1)
                if n + 1 < N:
                    emitA(n + 1)
                if 1 <= n < N + 1:
                    S2p(n - 1)
                if n >= 2:
                    S3b(n - 2)

        def attn_sample():
            NQ = 8 * LS
            nst = NPB + 1
            rotc = {"k": 0, "v": 0}
            for sp_ in range(NS // 2):
                slots = []
                for st in range(nst):
                    for s in range(2):
                        slots.append((s, st))
                info = {}
                A = [banks[0], banks[1]]
                ACC = [banks[2], banks[3]]
                O = [banks[4], banks[5]]
                TR = [banks[6], banks[7]]
                pre = {}

                def load(n):
                    s, st = slots[n]
                    kb = NPB - st
                    if kb == NPB:
                        return
                    seq = 2 * sp_ + s
                    kf = tmp()
                    vf = tmp()
                    S.dma("sp", kf.t[:, 0:512], ck_d[seq, kb * 128:(kb + 1) * 128, :], writes=[kf])
                    S.dma("sp", vf.t[:, 0:512], cv_d[seq, kb * 128:(kb + 1) * 128, :], writes=[vf])
                    pre[n] = (kf, vf)

                ktmap = {}

                def S0(n):
                    s, st = slots[n]
                    if NPB - st == NPB:
                        return
                    kf, vf = pre.pop(n)
                    for j in range(4):
                        S.op("pe", lambda e, j=j: e.transpose(out=TR[s].t[:, j * 128:(j + 1) * 128], in_=kf.t[:, j * 128:(j + 1) * 128], identity=ident), reads=[kf, cst], writes=[TR[s]])
                    ktb = KTb[rotc["k"] % 2]
                    rotc["k"] += 1
                    S.op("dve", lambda e: e.tensor_copy(out=ktb.t[:, :, :], in_=TR[s].t[:, :].rearrange("p (j k) -> p j k", j=4)), reads=[TR[s]], writes=[ktb])
                    ktmap[n] = (ktb, vf)

                def S1(n):
                    s, st = slots[n]
                    seq = 2 * sp_ + s
                    kb = NPB - st
                    new = kb == NPB
                    rows = 128
                    qc = slice(seq * LS, (seq + 1) * LS)
                    if new:
                        vsrc = Vnew[seq]
                        for j in range(4):
                            mm(A[s].t[0:LS, j * 2 * LS:(j + 1) * 2 * LS].rearrange("p (i q) -> p i q", i=2), KTs[j].t[:, qc], QTZ.t[:, j, :, qc], True, True, [KTs[j], QTZ], [A[s]])
                    else:
                        ktb, vf = ktmap.pop(n)
                        vsrc = Vb[rotc["v"] % 3]
                        rotc["v"] += 1
                        if s == 0:
                            S.op("dve", lambda e: e.tensor_copy(out=vsrc.t[:, :], in_=vf.t[:, 0:512]), reads=[vf], writes=[vsrc])
                        else:
                            S.op("act", lambda e: e.activation(out=vsrc.t[:, :], in_=vf.t[:, 0:512], func=AF.Copy), reads=[vf], writes=[vsrc])
                        for j in range(4):
                            mm(A[s].t[:, j * 2 * LS:(j + 1) * 2 * LS].rearrange("p (i q) -> p i q", i=2), ktb.t[:, j, :], QTZ.t[:, j, :, qc], True, True, [ktb, QTZ], [A[s]])
                    e_ = tmp()
                    if new:
                        S.op("pool", lambda e: e.memset(e_.t[:, 0:NQ], 0.0), writes=[e_])
                        S.op("act", lambda e: e.activation(out=e_.t[0:LS, 0:NQ], in_=A[s].t[0:LS, 0:NQ], func=AF.Exp), reads=[A[s]], writes=[e_])
                        S.op("pool", lambda e: e.tensor_tensor(out=e_.t[0:LS, 0:NQ], in0=e_.t[0:LS, 0:NQ], in1=m32, op=ALU.mult), reads=[e_, cst], writes=[e_])
                    else:
                        S.op("act", lambda e: e.activation(out=e_.t[0:rows, 0:NQ], in_=A[s].t[0:rows, 0:NQ], func=AF.Exp), reads=[A[s]], writes=[e_])
                    l_ = lbuf[rot["l"] % 3]
                    rot["l"] += 1
                    S.op("act", lambda e: e.activation(out=l_.t[0:rows, 0:NQ], in_=e_.t[0:rows, 0:NQ], func=AF.Ln, bias=1.0), reads=[e_], writes=[l_])
                    info[n] = (s, st, seq, rows, e_, l_, vsrc)

                def S2(n):
                    s, st, seq, rows, e_, l_, vsrc = info[n]
                    mm(ACC[s].t[:, 0:NQ], trib.t[0:rows, 0:128], l_.t[0:rows, 0:NQ], st == 0, False, [trib, l_], [ACC[s]])
                    p_ = tmp()
                    S.op("act", lambda e: e.activation(out=p_.t[0:rows, 0:NQ], in_=ACC[s].t[0:rows, 0:NQ], func=AF.Exp), reads=[ACC[s]], writes=[p_])
                    info[n] = info[n] + (p_,)

                def S3(n):
                    s, st, seq, rows, e_, l_, vsrc, p_ = info[n]
                    if st < nst - 1:
                        mm(ACC[s].t[:, 0:NQ], trib.t[0:rows, 128:256], l_.t[0:rows, 0:NQ], False, False, [trib, l_], [ACC[s]])
                    w_ = wbuf[rot["w"] % 3]
                    rot["w"] += 1
                    S.op("dve", lambda e: e.tensor_tensor(out=w_.t[0:rows, 0:NQ], in0=e_.t[0:rows, 0:NQ], in1=p_.t[0:rows, 0:NQ], op=ALU.mult), reads=[e_, p_], writes=[w_])
                    for j in range(4):
                        mm(O[s].t[:, j * 2 * LS:(j + 1) * 2 * LS], vsrc.t[0:rows, j * 128:(j + 1) * 128], w_.t[0:rows, j * 2 * LS:(j + 1) * 2 * LS], st == 0 and j == 0, st == nst - 1, [vsrc, w_], [O[s]])
                    if st == nst - 1:
                        qc = slice(seq * LS, (seq + 1) * LS)
                        for h in range(8):
                            j, i = h // 2, h % 2
                            eng = "act" if h % 2 == 0 else "dve"
                            if eng == "act":
                                S.op("act", lambda e, h=h, j=j, i=i: e.activation(out=ATT(j, qc)[64 * i:64 * i + 64, :], in_=O[s].t[64 * i:64 * i + 64, h * LS:(h + 1) * LS], func=AF.Copy), reads=[O[s]], writes=[HHA])
                            else:
                                S.op("dve", lambda e, h=h, j=j, i=i: e.tensor_copy(out=ATT(j, qc)[64 * i:64 * i + 64, :], in_=O[s].t[64 * i:64 * i + 64, h * LS:(h + 1) * LS]), reads=[O[s]], writes=[HHA])
                    del info[n]

                N = len(slots)
                for n in range(min(3, N)):
                    load(n)
                S0(0)
                for n in range(N + 2):
                    if n + 1 < N:
                        S0(n + 1)
                    if n < N:
                        S1(n)
                    if n + 3 < N:
                        load(n + 3)
                    if 1 <= n < N + 1:
                        S2(n - 1)
                    if n >= 2:
                        S3(n - 2)

        try:
            ckpt(0)
            for ti in range(NT + 1):
                process_tile(ti)
                if ti == 0:
                    S.barrier()
                ckpt(10 + ti)
        except StopBuild:
            pass

        with nc.allow_non_contiguous_dma(reason="small state outputs"):
            S.dma("sp", hp_d.rearrange("o (c p) -> p (o c)", p=128), HC[:], reads=[HC])
            for s in range(NS):
                S.dma("sp", hs_d[s:s + 1, :].rearrange("o (c p) -> p (o c)", p=128), HCs.t[:, :, s], reads=[HCs])
            for c in range(4):
                S.dma("sp", convp_d[:, c * 128:(c + 1) * 128].rearrange("r p -> p r"), UE[c].t[:, 0:3], reads=[UE[c]])
                for s in range(NS):
                    S.dma("sp", convs_d[s * 3:(s + 1) * 3, c * 128:(c + 1) * 128].rearrange("r p -> p r"), UEs[c].t[:, s, 0:3], reads=[UEs[c]])
        def pool_out(srcs, dst):
            bks = [bank(), bank()]
            for c in range(8):
                bk = bks[c // 4]
                S.op("pe", lambda e, c=c, bk=bk: e.transpose(out=bk.t[0:15, (c % 4) * 128:(c % 4 + 1) * 128], in_=srcs[c], identity=ident), reads=[cst] + list(PH) + list(PHs), writes=[bk])
            S.op("act", lambda e: e.activation(out=ost.t[0:15, 0:512], in_=bks[0].t[0:15, :], func=AF.Copy), reads=[bks[0]], writes=[ost])
            S.op("dve", lambda e: e.tensor_copy(out=ost.t[0:15, 512:1024], in_=bks[1].t[0:15, :]), reads=[bks[1]], writes=[ost])
            S.dma("sp", dst, ost.t[0:15, :], reads=[ost])

        pool_out([PH[c].t[:, :] for c in range(8)], poolp_d[:, :])
        for s in range(NS):
            pool_out([PHs[c].t[:, s, :] for c in range(8)], pools_d[s * 15:(s + 1) * 15, :])
        for q in ("sp", "act"):
            for i, v in enumerate(S.dcount[q]):
                if v > 0:
                    S._wait("sp", ("d" + q, i), v)
        build.stats = dict(S.count, waits=S.nwaits)
    return nc


def run(inputs, n_cores, SEQ, NS, LS, PAST):
    f = lambda k: np.asarray(inputs[k], dtype=np.float32)
    xp, xs = f("x_prompt"), f("x_sample")
    ck, cv = f("cache_sb_k")[0], f("cache_sb_v")[0]
    h0, conv0, pool0 = f("state_lru_h")[0], f("state_lru_conv")[0], f("state_pool")[0]
    wsrc = host_chunks(f("hyb_w_in")[0], f("hyb_w_out")[0], f("ffn_gate"), f("ffn_up"), f("ffn_down"), f("pool_w")[0])
    vecs = np.zeros((128, NV), np.float32)
    nm, nf = f("norm_mix"), f("norm_ffn")
    vecs[:, 0:8] = fm(nm[0]); vecs[:, 8:16] = fm(nm[1]); vecs[:, 16:24] = fm(nf[0]); vecs[:, 24:32] = fm(nf[1])
    vecs[:, 32:40] = fm(f("norm_final")); vecs[:, 40:48] = fm(f("pool_scale")[0])
    cw = f("hyb_conv_w")[0]
    for i in range(4):
        vecs[:, 48 + 4 * i:52 + 4 * i] = fm(cw[i])
    vecs[:, 64:68] = fm(f("hyb_conv_b")[0]); vecs[:, 68:72] = fm(f("hyb_rg_b")[0]); vecs[:, 72:76] = fm(f("hyb_ig_b")[0]); vecs[:, 76:80] = fm(f("hyb_lambda")[0])
    gatew = np.zeros((128, 8, 128), np.float32)
    rg, ig = f("hyb_rg_w")[0], f("hyb_ig_w")[0]
    for c in range(4):
        for kk, wsel in enumerate((rg, ig)):
            gatew[0:64, 4 * kk + c, 0:64] = wsel[2 * c]
            gatew[64:128, 4 * kk + c, 64:128] = wsel[2 * c + 1]
    gatew = gatew.reshape(128, 1024)
    cst = host_consts()
    in_maps = []
    for r in range(n_cores):
        sl = slice(r * NS, (r + 1) * NS)
        in_maps.append({
            "xp": np.ascontiguousarray(xp[r]),
            "xs": np.ascontiguousarray(xs[sl].reshape(NS * LS, D)),
            "ck": np.ascontiguousarray(ck[sl].reshape(NS, PAST, 512)),
            "cv": np.ascontiguousarray(cv[sl].reshape(NS, PAST, 512)),
            "h0": np.ascontiguousarray(h0[sl].reshape(NS, 4, 128).transpose(2, 1, 0)),
            "conv0": np.ascontiguousarray(conv0[sl].reshape(NS, 3, 4, 128).transpose(3, 2, 0, 1)),
            "pool0": np.ascontiguousarray(pool0[sl].reshape(NS, 15, 8, 128).transpose(3, 2, 0, 1)),
            "wsrc": wsrc, "vecs": vecs, "gatew": gatew, "cst": cst,
        })
    nc = build(SEQ, NS, LS, PAST)
    res = run_bass_kernel_spmd(nc, in_maps, core_ids=list(range(n_cores)))
    R = res.results
    cat = lambda k: np.stack([np.asarray(R[r][k], dtype=np.float32) for r in range(n_cores)])
    y_p = cat("yp")
    y_s = cat("ys").reshape(n_cores * NS, LS, D)
    k_p = cat("kp").reshape(1, n_cores, SEQ, 8, 64)
    v_p = cat("vp").reshape(1, n_cores, SEQ, 8, 64)
    h_p = cat("hp").reshape(1, n_cores, 512)
    conv_p = cat("convp").reshape(1, n_cores, 3, 512)
    pool_p = cat("poolp").reshape(1, n_cores, 15, D)
    k_s = cat("ks").reshape(1, n_cores * NS, LS, 8, 64)
    v_s = cat("vs").reshape(1, n_cores * NS, LS, 8, 64)
    h_s = cat("hs").reshape(1, n_cores * NS, 512)
    conv_s = cat("convs").reshape(1, n_cores * NS, 3, 512)
    pool_s = cat("pools").reshape(1, n_cores * NS, 15, D)
    if DEBUG:
        run.dbg = {k: np.asarray(v) for k, v in R[0].items() if k.startswith('dbg_')}
    return (y_p, y_s, k_p, v_p, h_p, conv_p, pool_p, k_s, v_s, h_s, conv_s, pool_s)


def kernel(**inputs):
    return run(inputs, 8, 4096, 4, 32, 4096)
```

```python
import numpy as np
from contextlib import ExitStack
import concourse.bass as bass
import concourse.mybir as mybir
from concourse.bass_utils import run_bass_kernel_spmd

F32 = mybir.dt.float32
BF16 = mybir.dt.bfloat16
AF = mybir.ActivationFunctionType
ALU = mybir.AluOpType

D = 1024
DFF = 2816
NFC = 22
POOL_WINDOWS = (2, 4, 8, 16)
EPS = 1e-6
EPOCH = 30000
DEBUG = False
STOP = None


class StopBuild(Exception):
    pass


def ckpt(k):
    if STOP is not None and STOP == k:
        raise StopBuild()
WSLOT = 4096

def chunk_list():
    L = []
    for s in (3, 4, 0, 1, 2):
        L.append(("KN", s, 0, 4096))
    for s in range(2):
        L.append(("KO", s, 0, 4096))

    def ffn(layer):
        for half in range(2):
            f0 = half * 11
            for i in range(5):
                L.append(("GU", layer, f0 + 2 * i, 4096))
            L.append(("GU1", layer, f0 + 10, 2048))
            for q in range(4):
                L.append(("DN", layer, half * 4 + q, 2816))
    ffn(0)
    L.append(("PW", 0, 0, 2048))
    ffn(1)
    return L


CHUNKS = chunk_list()
NCH = len(CHUNKS)


def host_chunks(w_in, w_out, gate, up, down, pool_w):
    out = np.zeros((NCH, 128, WSLOT), np.float32)
    for i, (k, a, b, nel) in enumerate(CHUNKS):
        if k == "KN":
            out[i] = w_in[:, a * 512:(a + 1) * 512].reshape(8, 128, 512).transpose(1, 0, 2).reshape(128, 4096)
        elif k == "KO":
            out[i] = w_out[:, a * 512:(a + 1) * 512].reshape(8, 128, 512).transpose(1, 0, 2).reshape(128, 4096)
        elif k == "GU":
            g = gate[a][:, b * 128:(b + 2) * 128].reshape(8, 128, 256)
            u = up[a][:, b * 128:(b + 2) * 128].reshape(8, 128, 256)
            out[i] = np.concatenate([g, u], axis=2).transpose(1, 0, 2).reshape(128, 4096)
        elif k == "GU1":
            g = gate[a][:, b * 128:(b + 1) * 128].reshape(8, 128, 128)
            u = up[a][:, b * 128:(b + 1) * 128].reshape(8, 128, 128)
            out[i, :, :2048] = np.concatenate([g, u], axis=2).transpose(1, 0, 2).reshape(128, 2048)
        elif k == "DN":
            half, q = b // 4, b % 4
            blk = down[a][half * 1408:(half + 1) * 1408, q * 256:(q + 1) * 256]
            out[i, :, :2816] = blk.reshape(11, 128, 256).transpose(1, 0, 2).reshape(128, 2816)
        elif k == "PW":
            out[i, :, :2048] = pool_w.reshape(4, 2, 128, 256).transpose(2, 0, 1, 3).reshape(128, 2048)
    return out


V_NM = (0, 8)
V_NF = (16, 24)
V_NFIN = 32
V_PS = 40
V_CW = 48
V_CB = 64
V_RGB = 68
V_IGB = 72
V_LAM = 76
NV = 80
C_ID = 0
C_TRI = 128
C_MASK = 384
C_ONES = 512
C_INV = 640
C_M32 = 704
C_MASK2 = 960
NCST = 1216


def host_consts():
    c = np.zeros((128, NCST), np.float32)
    c[:, C_ID:C_ID + 128] = np.eye(128, dtype=np.float32)
    j = np.arange(128)[:, None]
    k = np.arange(128)[None, :]
    c[:, C_TRI:C_TRI + 128] = -1.0 * (j >= k)
    c[:, C_TRI + 128:C_TRI + 256] = -1.0 * (j < k)
    c[:, C_MASK:C_MASK + 128] = (j < k)
    c[:, C_ONES:C_ONES + 128] = 1.0 / D
    c[:, C_MASK2:C_MASK2 + 128] = (j < k)
    c[:, C_MASK2 + 128:C_MASK2 + 256] = (j < k)
    for g, w in enumerate(POOL_WINDOWS):
        c[:, C_INV + 16 * g:C_INV + 16 * (g + 1)] = 1.0 / np.minimum(w, np.arange(16) + 1.0)
    m32 = (np.arange(32)[:, None] < np.arange(32)[None, :]).astype(np.float32)
    c[:32, C_M32:C_M32 + 256] = np.tile(m32, (1, 8))
    return c


def fm(v):
    return np.ascontiguousarray(v.reshape(-1, 128).T)


class Buf:
    __slots__ = ("t", "writer", "readers", "name")

    def __init__(self, t, name=""):
        self.t = t
        self.writer = None
        self.readers = {}
        self.name = name

    def __getitem__(self, idx):
        return self.t[idx]


class Sched:
    def __init__(self, nc, es, n_epochs=6, n_dma_sems=12):
        self.nc = nc
        self.eng = {"pe": nc.tensor, "act": nc.scalar, "dve": nc.vector, "pool": nc.gpsimd, "sp": nc.sync}
        self.sems = {}
        self.count = {}
        self.waited = {k: {} for k in self.eng}
        for k in self.eng:
            self.count[k] = 0
            if k == "sp":
                continue
            self.sems[k] = [es.enter_context(nc.semaphore(name=f"s_{k}{i}")) for i in range(n_epochs)]
        self.dsems = {}
        self.dcount = {}
        self.dnext = {}
        for q in ("sp", "act"):
            self.dsems[q] = [es.enter_context(nc.semaphore(name=f"d_{q}{i}")) for i in range(n_dma_sems)]
            self.dcount[q] = [0] * n_dma_sems
            self.dnext[q] = 0
        self.semobj = {}
        for k, lst in self.sems.items():
            for i, s in enumerate(lst):
                self.semobj[(k, i)] = s
        for q, lst in self.dsems.items():
            for i, s in enumerate(lst):
                self.semobj[("d" + q, i)] = s
        self.nwaits = 0

    def _wait(self, engname, key, val):
        w = self.waited[engname]
        if w.get(key, 0) >= val:
            return
        self.eng[engname].wait_ge(self.semobj[key], val)
        w[key] = val
        self.nwaits += 1

    def _deps(self, engname, reads, writes):
        need = {}
        for b in reads:
            if b.writer is not None:
                k, v = b.writer
                if need.get(k, 0) < v:
                    need[k] = v
        for b in writes:
            if b.writer is not None:
                k, v = b.writer
                if need.get(k, 0) < v:
                    need[k] = v
            for k, v in b.readers.items():
                if need.get(k, 0) < v:
                    need[k] = v
        for k, v in need.items():
            if engname == "pe" and k[0] == "pe":
                continue
            self._wait(engname, k, v)

    def _mark(self, tok, reads, writes):
        k, v = tok
        for b in reads:
            if b.readers.get(k, 0) < v:
                b.readers[k] = v
        for b in writes:
            b.writer = tok
            b.readers = {}

    def op(self, engname, fn, reads=(), writes=()):
        self._deps(engname, reads, writes)
        inst = fn(self.eng[engname])
        c = self.count[engname]
        ep, v = c // EPOCH, c % EPOCH + 1
        inst.then_inc(self.sems[engname][ep], 1)
        self.count[engname] = c + 1
        tok = ((engname, ep), v)
        self._mark(tok, reads, writes)
        return tok

    def dma(self, q, out, in_, reads=(), writes=(), **kw):
        self._deps(q, reads, writes)
        i = self.dnext[q]
        self.dnext[q] = (i + 1) % len(self.dsems[q])
        key = ("d" + q, i)
        prev = self.dcount[q][i]
        if prev > 0:
            self._wait(q, key, prev)
        inst = self.eng[q].dma_start(out=out, in_=in_, **kw)
        val = prev + 16
        inst.then_inc(self.dsems[q][i], 16)
        self.dcount[q][i] = val
        tok = (key, val)
        self._mark(tok, reads, writes)
        return tok

    def barrier(self):
        toks = {}
        for k in ("pe", "act", "dve", "pool"):
            c = self.count[k]
            if c > 0:
                toks[(k, (c - 1) // EPOCH)] = (c - 1) % EPOCH + 1
        for q in ("sp", "act"):
            for i, v in enumerate(self.dcount[q]):
                if v > 0:
                    toks[("d" + q, i)] = v
        for e in ("pe", "act", "dve", "pool", "sp"):
            for key, v in toks.items():
                if key[0] == e:
                    continue
                self._wait(e, key, v)

    def finish(self, bufs):
        for b in bufs:
            if b.writer is not None:
                self._wait("sp", b.writer[0], b.writer[1])


def build(SEQ, NS, LS, PAST):
    T = 512
    NT = SEQ // T
    TS = NS * LS
    NPB = PAST // 128
    nc = bass.Bass("TRN2", target_bir_lowering=False)

    def din(name, shape, dt=F32):
        return nc.dram_tensor(name, list(shape), dt, kind="ExternalInput").ap()

    def dout(name, shape):
        return nc.dram_tensor(name, list(shape), F32, kind="ExternalOutput").ap()

    xp_d = din("xp", (SEQ, D))
    xs_d = din("xs", (TS, D))
    ck_d = din("ck", (NS, PAST, 512))
    cv_d = din("cv", (NS, PAST, 512))
    h0_d = din("h0", (128, 4, NS))
    conv0_d = din("conv0", (128, 4, NS, 3))
    pool0_d = din("pool0", (128, 8, NS, 15))
    wsrc_d = din("wsrc", (NCH, 128, WSLOT))
    vecs_d = din("vecs", (128, NV))
    gatew_d = din("gatew", (128, 8 * 128))
    cst_d = din("cst", (128, NCST))
    yp_d = dout("yp", (SEQ, D))
    ys_d = dout("ys", (TS, D))
    kp_d = dout("kp", (SEQ, 512))
    vp_d = dout("vp", (SEQ, 512))
    hp_d = dout("hp", (1, 512))
    convp_d = dout("convp", (3, 512))
    poolp_d = dout("poolp", (15, D))
    ks_d = dout("ks", (TS, 512))
    vs_d = dout("vs", (TS, 512))
    hs_d = dout("hs", (NS, 512))
    convs_d = dout("convs", (NS * 3, 512))
    pools_d = dout("pools", (NS * 15, D))
    wscr_d = nc.dram_tensor("wscr", [NCH, 128, WSLOT], BF16, kind="Internal").ap()

    es = ExitStack()
    with es:
        S = Sched(nc, es)
        cnt = [0]
        dbg_list = []

        def dbg(name, buf, ap, shape, dt=F32):
            if not DEBUG:
                return
            d = nc.dram_tensor('dbg_' + name, list(shape), dt, kind='ExternalOutput').ap()
            S.dma('sp', d, ap, reads=[buf])
            dbg_list.append(name)

        def sb(shape, dt, name=None):
            cnt[0] += 1
            nm = "sb_" + (name or f"t{cnt[0]}")
            return Buf(es.enter_context(nc.sbuf_tensor(nm, list(shape), dt)), nm)

        P2 = [es.enter_context(nc.psum_tensor(f"pbank{i}", [128, 1024], F32)) for i in range(4)]
        banks = [Buf(P2[i // 2][:, (i % 2) * 512:(i % 2 + 1) * 512], f"bank{i}") for i in range(8)]
        bank_rr = [0]

        def bank():
            b = banks[bank_rr[0] % 8]
            bank_rr[0] += 1
            return b

        KT = [[sb([128, T], BF16) for _ in range(NT)] for _ in range(4)]
        VV = [sb([128, 4, 512], BF16) for _ in range(max(NT, 8))]
        XT = sb([128, 8, T], F32, "XT")
        xstage = [sb([128, D], F32) for _ in range(2)]
        XN = sb([128, 8, T], BF16, "XN")
        XNB = [Buf(XN.t[:, c_, :], f"xn{c_}") for c_ in range(8)]
        XTF = [Buf(XT.t[:, c_, :], f"xtf{c_}") for c_ in range(8)]
        HH = sb([128, 12 * T], BF16, "HH")
        hh3 = HH.t[:].rearrange("p (f t) -> p f t", t=T)
        HHQ, HHM, HHA = Buf(HH.t[:, 0:4 * T], "hhq"), Buf(HH.t[:, 4 * T:8 * T], "hhm"), Buf(HH.t[:, 8 * T:12 * T], "hha")
        HHALL = [HHQ, HHM, HHA]

        def QT(j, cols):
            return HH.t[:, j * T + cols.start:j * T + cols.stop]

        def MIX(c, cols):
            return HH.t[:, (4 + c) * T + cols.start:(4 + c) * T + cols.stop]

        UE = [sb([128, 3 + T], F32) for _ in range(4)]
        PH = [sb([128, 15], F32) for _ in range(8)]
        HC = sb([128, 4], F32, "HC")
        HCs = sb([128, 4, NS], F32, "HCs")
        NTMP = 12
        TB = [es.enter_context(nc.sbuf_tensor(f"sb_tb{k_}", [128, 1088], F32)) for k_ in range(NTMP // 2)]
        tmps = [Buf(TB[k_ // 2][:, (k_ % 2) * 544:(k_ % 2 + 1) * 544], f"tmp{k_}") for k_ in range(NTMP)]

        def tmp2():
            if tmp_rr[0] % 2 == 1:
                tmp_rr[0] += 1
            k_ = (tmp_rr[0] % NTMP) // 2
            a_, b_ = tmps[2 * k_], tmps[2 * k_ + 1]
            tmp_rr[0] += 2
            return a_, b_, TB[k_][:, :].rearrange("p (s l) -> p s l", s=2)[:, :, 0:T]
        tmp_rr = [0]

        def tmp():
            b = tmps[tmp_rr[0] % NTMP]
            tmp_rr[0] += 1
            return b

        LB = [sb([128, 2, T], BF16) for _ in range(3)]
        lbuf = [Buf(LB[k_].t[:, 0, :], f"lb{k_}") for k_ in range(3)]
        WB = [sb([128, 2, T], BF16) for _ in range(2)]
        wbuf = [Buf(WB[k_ % 2].t[:, k_ // 2, :], f"wb{k_}") for k_ in range(3)]
        ucb = [sb([128, T], BF16) for _ in range(2)]
        dB = ucb
        kvst = [sb([128, 512], F32) for _ in range(2)]
        wring = [sb([128, WSLOT], BF16) for _ in range(3)]
        wstg = [sb([128, 1024], F32) for _ in range(2)]
        KTb = [Buf(VV[5].t[:, k_, :].rearrange("p (j t) -> p j t", j=4), f"ktb{k_}") for k_ in range(2)]
        Vb = [Buf(VV[6].t[:, k_, :], f"vb{k_}") for k_ in range(3)]
        Vnew = [Buf(VV[s_].t[:, 0, :], f"vnew{s_}") for s_ in range(NS)]
        QTZ = Buf(VV[4].t[:, 0:2, :].rearrange("p a n -> p (a n)")[:, 0:8 * TS].rearrange("p (j i t) -> p j i t", j=4, i=2), "qtz")
        UEs = [Buf(VV[c_].t[:, 1, :].bitcast(F32)[:, 0:NS * (3 + LS)].rearrange("p (s l) -> p s l", s=NS), f"ues{c_}") for c_ in range(4)]
        PHs = [Buf(VV[c_].t[:, 3, :].bitcast(F32)[:, 128:128 + NS * 15].rearrange("p (s l) -> p s l", s=NS), f"phs{c_}") for c_ in range(8)]
        KTs = [Buf(VV[7].t[:, j_, 0:TS], f"kts{j_}") for j_ in range(4)]
        cst = sb([128, NCST], F32, "cst")
        vecs = sb([128, NV], F32, "vecs")
        cvec = sb([128, 16], F32, "cvec")
        gwf = wstg[0]
        gwb = sb([128, 1024], BF16, "gwb")
        trib = sb([128, 256], BF16, "trib")
        onesb = sb([128, 128], BF16, "onesb")
        sq = [sb([128, T], BF16) for _ in range(2)]
        rstd = sb([128, T], F32, "rstd")
        epsv = sb([128, 1], F32, "epsv")
        ost = xstage[0]
        WS = [Buf(None, f"ws{i}") for i in range(NCH)]

        ident = cst.t[:, C_ID:C_ID + 128]
        mask = cst.t[:, C_MASK:C_MASK + 128]
        m32 = cst.t[0:32, C_M32:C_M32 + 256]
        mask2 = cst.t[:, C_MASK2:C_MASK2 + 256].rearrange("p (s k) -> p s k", s=2)

        S.op("pool", lambda e: e.memset(epsv[:], EPS), writes=[epsv])
        S.dma("sp", cst[:], cst_d, writes=[cst])
        S.dma("sp", vecs[:], vecs_d, writes=[vecs])
        S.dma("sp", gwf[:], gatew_d, writes=[gwf])
        S.op("dve", lambda e: e.tensor_copy(out=trib[:], in_=cst.t[:, C_TRI:C_TRI + 256]), reads=[cst], writes=[trib])
        S.op("dve", lambda e: e.tensor_copy(out=onesb[:], in_=cst.t[:, C_ONES:C_ONES + 128]), reads=[cst], writes=[onesb])
        S.op("pool", lambda e: e.tensor_copy(out=gwb[:], in_=gwf[:]), reads=[gwf], writes=[gwb])
        S.op("act", lambda e: e.activation(out=cvec.t[:, 8:12], in_=vecs.t[:, V_LAM:V_LAM + 4], func=AF.Exp, scale=-1.0), reads=[vecs], writes=[cvec])
        S.op("act", lambda e: e.activation(out=cvec.t[:, 12:16], in_=cvec.t[:, 8:12], func=AF.Ln, bias=1.0), reads=[cvec], writes=[cvec])
        S.op("dve", lambda e: e.tensor_scalar(out=cvec.t[:, 0:4], in0=cvec.t[:, 12:16], scalar1=-8.0, scalar2=0.0, op0=ALU.mult, op1=ALU.add), reads=[cvec], writes=[cvec])
        S.op("dve", lambda e: e.tensor_scalar(out=cvec.t[:, 4:8], in0=cvec.t[:, 12:16], scalar1=-16.0, scalar2=0.0, op0=ALU.mult, op1=ALU.add), reads=[cvec], writes=[cvec])
        for c in range(4):
            S.op("pool", lambda e, c=c: e.memset(UE[c].t[:, 0:3], 0.0), writes=[UE[c]])
        S.op("pool", lambda e: e.memset(HC[:], 0.0), writes=[HC])
        S.dma("sp", HCs[:], h0_d, writes=[HCs])
        for c in range(8):
            S.op("pool", lambda e, c=c: e.memset(PH[c][:], 0.0), writes=[PH[c]])

        order = []
        for ti in range(NT + 1):
            for ci in range(NCH):
                order.append((ti, ci))
        wstate = {"emitted": 0, "next": 0, "stg": 0, "cast": 0}
        LOOK = 2

        stg_pool = list(wstg) + [Buf(VV[k_].t[:, :, :].rearrange("p a n -> p (a n)").bitcast(F32), f"vstg{k_}") for k_ in range(1, len(VV))]
        NSTG = (len(stg_pool) // 4) * 4
        LD = NSTG // 4 - 1
        wload = {"emitted": 0, "map": {}}

        def w_load(i):
            ti, ci = order[i]
            if ti != 0:
                return
            nel = CHUNKS[ci][3]
            q = nel // 4
            lst = []
            for k in range(4):
                st = stg_pool[wstate["stg"] % NSTG]
                wstate["stg"] += 1
                S.dma("sp", st.t[:, 0:q], wsrc_d[ci, :, k * q:(k + 1) * q], writes=[st])
                lst.append(st)
            wload["map"][i] = lst

        def w_emit(i):
            ti, ci = order[i]
            buf = wring[i % 3]
            nel = CHUNKS[ci][3]
            if ti == 0:
                while wload["emitted"] <= min(i + LD, NCH - 1):
                    w_load(wload["emitted"])
                    wload["emitted"] += 1
                q = nel // 4
                eng = "act" if (wstate["cast"] % 2 == 1) else "dve"
                wstate["cast"] += 1
                for k, st in enumerate(wload["map"].pop(i)):
                    if eng == "act":
                        S.op("act", lambda e, k=k, st=st: e.activation(out=buf.t[:, k * q:(k + 1) * q], in_=st.t[:, 0:q], func=AF.Copy), reads=[st], writes=[buf])
                    else:
                        S.op("dve", lambda e, k=k, st=st: e.tensor_copy(out=buf.t[:, k * q:(k + 1) * q], in_=st.t[:, 0:q]), reads=[st], writes=[buf])
                S.dma("sp", wscr_d[ci, :, 0:nel], buf.t[:, 0:nel], reads=[buf], writes=[WS[ci]])
            else:
                S.dma("sp", buf.t[:, 0:nel], wscr_d[ci, :, 0:nel], reads=[WS[ci]], writes=[buf])

        def wnext(kind):
            i = wstate["next"]
            wstate["next"] += 1
            assert CHUNKS[order[i][1]][0] == kind, (CHUNKS[order[i][1]], kind)
            while wstate["emitted"] <= min(i + LOOK, len(order) - 1):
                w_emit(wstate["emitted"])
                wstate["emitted"] += 1
            return wring[i % 3]

        def mm(out, lhsT, rhs, start, stop, R, W):
            S.op("pe", lambda e: e.matmul(out, lhsT=lhsT, rhs=rhs, start=start, stop=stop, skip_group_check=True), reads=R, writes=W)

        def norm_stats(TW):
            bk = bank()
            for c in range(8):
                s = sq[c % 2]
                if c % 2 == 0:
                    S.op("act", lambda e, c=c, s=s: e.activation(out=s.t[:, :TW], in_=XT.t[:, c, :TW], func=AF.Square), reads=[XT], writes=[s])
                else:
                    S.op("dve", lambda e, c=c, s=s: e.tensor_tensor(out=s.t[:, :TW], in0=XT.t[:, c, :TW], in1=XT.t[:, c, :TW], op=ALU.mult), reads=[XT], writes=[s])
                mm(bk.t[:, :TW], onesb[:], s.t[:, :TW], c == 0, c == 7, [onesb, s], [bk])
            t1 = tmp()
            S.op("act", lambda e: e.activation(out=t1.t[:, :TW], in_=bk.t[:, :TW], func=AF.Ln, bias=epsv.t[:, 0:1]), reads=[bk, epsv], writes=[t1])
            S.op("act", lambda e: e.activation(out=rstd.t[:, :TW], in_=t1.t[:, :TW], func=AF.Exp, scale=-0.5), reads=[t1], writes=[rstd])

        def norm_to_xn(TW, gcol):
            norm_stats(TW)
            for c in range(8):
                norm_apply(c, gcol, TW, XN.t[:, c, :TW], XNB[c])

        def norm_apply(c, gcol, TW, out_ap, out_buf, in_ap=None, rs_ap=None, force_dve=False):
            in_ap = XT.t[:, c, :TW] if in_ap is None else in_ap
            rs_ap = rstd.t[:, :TW] if rs_ap is None else rs_ap
            gsc = vecs.t[:, gcol + c:gcol + c + 1]
            obl = out_buf if isinstance(out_buf, list) else [out_buf]
            if c % 4 != 3 or force_dve:
                S.op("dve", lambda e: e.scalar_tensor_tensor(out=out_ap, in0=in_ap, scalar=gsc, in1=rs_ap, op0=ALU.mult, op1=ALU.mult), reads=[XT, vecs, rstd], writes=obl)
            else:
                t_ = tmp()
                tv = t_.t[:, :TW] if len(in_ap.shape) == 2 else t_.t[:, :TW].rearrange("p (s l) -> p s l", s=in_ap.shape[1])
                S.op("act", lambda e: e.activation(out=tv, in_=in_ap, func=AF.Copy, scale=gsc), reads=[XT, vecs], writes=[t_])
                S.op("pool", lambda e: e.tensor_tensor(out=out_ap, in0=tv, in1=rs_ap, op=ALU.mult), reads=[t_, rstd], writes=obl)

        def proj_fm(wb, j, TW, rhs_fn, R):
            bk = bank()
            w3 = wb.t[:].rearrange("p (c n) -> p c n", c=8)
            for c in range(8):
                mm(bk.t[:, :TW], w3[:, c, j * 128:(j + 1) * 128], rhs_fn(c), c == 0, c == 7, [wb] + (R(c) if callable(R) else R), [bk])
            return bk

        xpref = set()

        def process_tile(ti):
            sample = ti == NT
            TW = TS if sample else T
            nseg = NS if sample else 1
            L = LS if sample else T
            first = ti == 0
            t0 = 0 if sample else ti * T
            x_d = xs_d if sample else xp_d
            ntb = TW // 128
            cols = slice(0, TW)

            if sample:
                S.barrier()
                for s_ in range(NS):
                    S.op("pool", lambda e, s_=s_: e.memset(Vnew[s_].t[:, :], 0.0), writes=[Vnew[s_]])
                S.op("pool", lambda e: e.memset(QTZ.t[:, :, :, :], 0.0), writes=[QTZ])
                for c_ in range(4):
                    S.dma("sp", UEs[c_].t[:, :, 0:3], conv0_d[:, c_], writes=[UEs[c_]])
                for c_ in range(8):
                    S.dma("sp", PHs[c_].t[:, :, :], pool0_d[:, c_], writes=[PHs[c_]])
            for tb in range(ntb):
                xs_ = xstage[tb % 2]
                if (ti, tb) not in xpref:
                    S.dma("sp", xs_[:], x_d[t0 + tb * 128:t0 + (tb + 1) * 128, :], writes=[xs_])
                for half in range(2):
                    bk = bank()
                    for jj in range(4):
                        c = half * 4 + jj
                        S.op("pe", lambda e, c=c, jj=jj, bk=bk, xs_=xs_: e.transpose(out=bk.t[:, jj * 128:(jj + 1) * 128], in_=xs_.t[:, c * 128:(c + 1) * 128], identity=ident), reads=[xs_, cst], writes=[bk])
                    eng = "act" if half == 0 else "dve"
                    S.op(eng, lambda e, half=half, bk=bk, tb=tb: (e.activation(out=XT.t[:, half * 4:half * 4 + 4, tb * 128:(tb + 1) * 128], in_=bk.t[:].rearrange("p (j t) -> p j t", j=4), func=AF.Copy) if half == 0 else
                                                                   e.tensor_copy(out=XT.t[:, half * 4:half * 4 + 4, tb * 128:(tb + 1) * 128], in_=bk.t[:].rearrange("p (j t) -> p j t", j=4))), reads=[bk], writes=[XT] + XTF[half * 4:half * 4 + 4])

            ckpt(100 * ti + 1)
            norm_to_xn(TW, V_NM[0])
            xn_rhs = lambda c: XN.t[:, c, :TW]
            if ti == 0:
                dbg('xt0', XT, XT.t[:, :, :], [128, 8, T])
                pass
                dbg('rstd0', rstd, rstd.t[:, :], [128, T])

            ckpt(100 * ti + 2)
            if sample:
                tblocks = [(s_ * LS, LS, s_ * LS) for s_ in range(NS)]
            else:
                tblocks = [(tb * 128, 128, t0 + tb * 128) for tb in range(ntb)]
            k_out = ks_d if sample else kp_d
            v_out = vs_d if sample else vp_d
            hc = HCs if sample else HC
            gw3 = gwb.t[:].rearrange("p (k n) -> p k n", k=8)

            def v3(ap2):
                return ap2.rearrange("p (s l) -> p s l", s=nseg)

            def ue_of(c):
                ue = UEs[c] if sample else UE[c]
                return ue, (ue.t[:] if sample else ue.t[:].rearrange("p (s l) -> p s l", s=1))

            wu = wnext("KN")
            for c in range(4):
                ue, ue3 = ue_of(c)
                bku = proj_fm(wu, c, TW, xn_rhs, lambda cc: [XNB[cc]])
                S.op("act", lambda e, bku=bku, ue3=ue3: e.activation(out=ue3[:, :, 3:3 + L], in_=v3(bku.t[:, :TW]), func=AF.Copy), reads=[bku], writes=[ue])
            wg = wnext("KN")
            gts, g2s = [], []
            for c in range(4):
                bkg = proj_fm(wg, c, TW, xn_rhs, lambda cc: [XNB[cc]])
                gt = tmp()
                S.op("act", lambda e, bkg=bkg, gt=gt: e.activation(out=gt.t[:, :TW], in_=bkg.t[:, :TW], func=AF.Copy), reads=[bkg], writes=[gt])
                gts.append(gt)
            for c in range(4):
                gt = gts[c]
                g2 = tmp()
                g2s.append(g2)
                S.op("pool", lambda e, gt=gt, g2=g2: e.tensor_tensor(out=g2.t[:, :TW], in0=gt.t[:, :TW], in1=gt.t[:, :TW], op=ALU.mult), reads=[gt], writes=[g2])
                S.op("dve", lambda e, g2=g2: e.tensor_scalar(out=g2.t[:, :TW], in0=g2.t[:, :TW], scalar1=0.044715, scalar2=1.0, op0=ALU.mult, op1=ALU.add), reads=[g2], writes=[g2])
                S.op("pool", lambda e, gt=gt, g2=g2: e.tensor_tensor(out=g2.t[:, :TW], in0=g2.t[:, :TW], in1=gt.t[:, :TW], op=ALU.mult), reads=[g2, gt], writes=[g2])
            for c in range(4):
                g2 = g2s[c]
                S.op("act", lambda e, g2=g2: e.activation(out=g2.t[:, :TW], in_=g2.t[:, :TW], func=AF.Sigmoid, scale=1.5957691216057308), reads=[g2], writes=[g2])
            for c in range(4):
                gt, g2 = gts[c], g2s[c]
                S.op("dve", lambda e, gt=gt, g2=g2, c=c: e.tensor_tensor(out=ATT(c, cols), in0=gt.t[:, :TW], in1=g2.t[:, :TW], op=ALU.mult), reads=[gt, g2], writes=[HHA])

            def piece_q():
                wq = wnext("KN")
                for j in range(4):
                    bk = proj_fm(wq, j, TW, xn_rhs, lambda c: [XNB[c]])
                    if sample:
                        S.op("act", lambda e, j=j, bk=bk: e.activation(out=QTZ.t[0:64, j, 0, :], in_=bk.t[0:64, :TW], func=AF.Copy, scale=0.125), reads=[bk], writes=[QTZ])
                        S.op("dve", lambda e, j=j, bk=bk: e.tensor_scalar(out=QTZ.t[64:128, j, 1, :], in0=bk.t[64:128, :TW], scalar1=0.125, scalar2=0.0, op0=ALU.mult, op1=ALU.add), reads=[bk], writes=[QTZ])
                    else:
                        S.op("act", lambda e, j=j, bk=bk: e.activation(out=QT(j, cols), in_=bk.t[:, :TW], func=AF.Copy, scale=0.125), reads=[bk], writes=[HHQ])

            kstate = {}

            def piece_kfm():
                wk = wnext("KN")
                kstate["wk"] = wk
                for j in range(4):
                    bk = proj_fm(wk, j, TW, xn_rhs, lambda c: [XNB[c]])
                    dst = KTs[j] if sample else KT[j][ti]
                    S.op("dve", lambda e, bk=bk, dst=dst: e.tensor_copy(out=dst.t[:, :TW], in_=bk.t[:, :TW]), reads=[bk], writes=[dst])

            def piece_ktm():
                wk = kstate["wk"]
                wk3 = wk.t[:].rearrange("p (c n) -> p c n", c=8)
                for bi, (c0, n, r0) in enumerate(tblocks):
                    bk = bank()
                    for c in range(8):
                        mm(bk.t[0:n, :], XN.t[:, c, c0:c0 + n], wk3[:, c, :], c == 0, c == 7, [XNB[c], wk], [bk])
                    st = kvst[bi % 2]
                    S.op("act", lambda e, bk=bk, st=st, n=n: e.activation(out=st.t[0:n, :], in_=bk.t[0:n, :], func=AF.Copy), reads=[bk], writes=[st])
                    S.dma("sp", k_out[r0:r0 + n, :], st.t[0:n, :], reads=[st])

            def piece_v():
                wv = wnext("KN")
                wv3 = wv.t[:].rearrange("p (c n) -> p c n", c=8)
                for bi, (c0, n, r0) in enumerate(tblocks):
                    bk = bank()
                    for c in range(8):
                        mm(bk.t[0:n, :], XN.t[:, c, c0:c0 + n], wv3[:, c, :], c == 0, c == 7, [XNB[c], wv], [bk])
                    st = kvst[(bi + 1) % 2]
                    S.op("act", lambda e, bk=bk, st=st, n=n: e.activation(out=st.t[0:n, :], in_=bk.t[0:n, :], func=AF.Copy), reads=[bk], writes=[st])
                    S.dma("sp", v_out[r0:r0 + n, :], st.t[0:n, :], reads=[st])
                    if sample:
                        S.op("dve", lambda e, bk=bk, bi=bi, n=n: e.tensor_copy(out=Vnew[bi].t[0:n, :], in_=bk.t[0:n, :]), reads=[bk], writes=[Vnew[bi]])
                    else:
                        S.op("dve", lambda e, bk=bk, bi=bi: e.tensor_copy(out=VV[ti].t[:, bi, :], in_=bk.t[:, :]), reads=[bk], writes=[VV[ti]])

            pieces = [piece_q, piece_kfm, piece_ktm, piece_v]

            def lruA(c):
                ue, ue3 = ue_of(c)
                uc = tmp()
                cw = lambda i: vecs.t[:, V_CW + 4 * i + c:V_CW + 4 * i + c + 1]
                S.op("dve", lambda e: e.tensor_scalar(out=v3(uc.t[:, :TW]), in0=ue3[:, :, 3:3 + L], scalar1=cw(3), scalar2=vecs.t[:, V_CB + c:V_CB + c + 1], op0=ALU.mult, op1=ALU.add), reads=[ue, vecs], writes=[uc])
                for i in (2, 1, 0):
                    S.op("dve", lambda e, i=i: e.scalar_tensor_tensor(out=v3(uc.t[:, :TW]), in0=ue3[:, :, i:i + L], scalar=cw(i), in1=v3(uc.t[:, :TW]), op0=ALU.mult, op1=ALU.add), reads=[ue, vecs, uc], writes=[uc])
                S.op("pool", lambda e: e.tensor_copy(out=ue3[:, :, 0:3], in_=ue3[:, :, L:L + 3]), reads=[ue], writes=[ue])
                ub = ucb[c % 2]
                S.op("act", lambda e: e.activation(out=ub.t[:, :TW], in_=uc.t[:, :TW], func=AF.Copy), reads=[uc], writes=[ub])
                return uc, ub

            def lruB(c, uc, ub):
                bkr = bank()
                mm(bkr.t[:, :TW], gw3[:, c, :], ub.t[:, :TW], True, True, [gwb, ub], [bkr])
                bki = bank()
                mm(bki.t[:, :TW], gw3[:, 4 + c, :], ub.t[:, :TW], True, True, [gwb, ub], [bki])
                rr = tmp()
                ii = tmp()
                S.op("act", lambda e: e.activation(out=rr.t[:, :TW], in_=bkr.t[:, :TW], func=AF.Sigmoid, bias=vecs.t[:, V_RGB + c:V_RGB + c + 1]), reads=[bkr, vecs], writes=[rr])
                S.op("act", lambda e: e.activation(out=ii.t[:, :TW], in_=bki.t[:, :TW], func=AF.Sigmoid, bias=vecs.t[:, V_IGB + c:V_IGB + c + 1]), reads=[bki, vecs], writes=[ii])
                a2 = tmp()
                aa = rr
                S.op("act", lambda e: e.activation(out=a2.t[:, :TW], in_=rr.t[:, :TW], func=AF.Exp, scale=cvec.t[:, 4 + c:5 + c]), reads=[rr, cvec], writes=[a2])
                S.op("act", lambda e: e.activation(out=aa.t[:, :TW], in_=rr.t[:, :TW], func=AF.Exp, scale=cvec.t[:, c:c + 1]), reads=[rr, cvec], writes=[aa])
                S.op("act", lambda e: e.activation(out=a2.t[:, :TW], in_=a2.t[:, :TW], func=AF.Ln, scale=-1.0, bias=1.0), reads=[a2], writes=[a2])
                S.op("act", lambda e: e.activation(out=a2.t[:, :TW], in_=a2.t[:, :TW], func=AF.Exp, scale=0.5), reads=[a2], writes=[a2])
                S.op("pool", lambda e: e.tensor_tensor(out=ii.t[:, :TW], in0=ii.t[:, :TW], in1=uc.t[:, :TW], op=ALU.mult), reads=[ii, uc], writes=[ii])
                S.op("dve", lambda e: e.tensor_tensor(out=ii.t[:, :TW], in0=ii.t[:, :TW], in1=a2.t[:, :TW], op=ALU.mult), reads=[ii, a2], writes=[ii])
                hh_ = a2
                for s_ in range(nseg):
                    init = hc.t[:, c, s_:s_ + 1] if sample else hc.t[:, c:c + 1]
                    S.op("dve", lambda e, s_=s_, init=init: e.tensor_tensor_scan(out=hh_.t[:, s_ * L:(s_ + 1) * L], data0=aa.t[:, s_ * L:(s_ + 1) * L], data1=ii.t[:, s_ * L:(s_ + 1) * L], initial=init, op0=ALU.mult, op1=ALU.add), reads=[aa, ii, hc], writes=[hh_])
                hdst = hc.t[:, c, :] if sample else hc.t[:, c:c + 1]
                S.op("pool", lambda e: e.tensor_copy(out=hdst, in_=v3(hh_.t[:, :TW])[:, :, L - 1]), reads=[hh_], writes=[hc])
                S.op("dve", lambda e: e.tensor_tensor(out=MIX(c, cols), in0=hh_.t[:, :TW], in1=ATT(c, cols), op=ALU.mult), reads=[hh_, HHA], writes=[HHM])

            for c in range(4):
                uc, ub = lruA(c)
                pieces[c]()
                lruB(c, uc, ub)

            ckpt(100 * ti + 4)
            if sample:
                attn_sample()
            else:
                attn_prompt(ti)

            ckpt(100 * ti + 5)
            for s_ in range(2):
                wo = wnext("KO")
                for j in range(4):
                    n = s_ * 4 + j
                    bk = proj_fm(wo, j, TW, lambda c: MIXALL(c, TW), lambda c: [HHA] if c < 4 else [HHM])
                    S.op("dve", lambda e, bk=bk, n=n: e.tensor_tensor(out=XT.t[:, n, :TW], in0=bk.t[:, :TW], in1=XT.t[:, n, :TW], op=ALU.add), reads=[bk, XT], writes=[XT])

            ckpt(100 * ti + 6)
            ffn(0, TW)
            ckpt(100 * ti + 7)
            pool_mixer(ti, TW, nseg, L, sample, first)
            ckpt(100 * ti + 8)
            if ti + 1 <= NT:
                nsample = (ti + 1 == NT)
                nx_d = xs_d if nsample else xp_d
                nt0 = 0 if nsample else (ti + 1) * T
                for tb in range(1 if nsample else 2):
                    S.dma("sp", xstage[tb][:], nx_d[nt0 + tb * 128:nt0 + (tb + 1) * 128, :], writes=[xstage[tb]])
                    xpref.add((ti + 1, tb))
            ffn(1, TW)
            ckpt(100 * ti + 9)
            norm_stats(TW)
            for c in range(8):
                norm_apply(c, V_NFIN, TW, XT.t[:, c, :TW], [XT, XTF[c]])
            y_d = ys_d if sample else yp_d
            for tb in range(ntb):
                ya, yb, y3 = tmp2()
                for half in range(2):
                    bk = bank()
                    for jj in range(4):
                        c = half * 4 + jj
                        S.op("pe", lambda e, c=c, jj=jj, bk=bk, tb=tb: e.transpose(out=bk.t[:, jj * 128:(jj + 1) * 128], in_=XT.t[:, c, tb * 128:(tb + 1) * 128], identity=ident), reads=[XTF[c], cst], writes=[bk])
                    if half == 0:
                        S.op("act", lambda e, bk=bk, y3=y3: e.activation(out=y3[:, 0, :], in_=bk.t[:, :], func=AF.Copy), reads=[bk], writes=[ya])
                    else:
                        S.op("dve", lambda e, bk=bk, y3=y3: e.tensor_copy(out=y3[:, 1, :], in_=bk.t[:, :]), reads=[bk], writes=[yb])
                S.dma("sp", y_d[t0 + tb * 128:t0 + (tb + 1) * 128, :].rearrange("t (h m) -> t h m", h=2), y3, reads=[ya, yb])

        def MIXALL(c, TW):
            if c < 4:
                return HH.t[:, (8 + c) * T:(8 + c) * T + TW]
            return HH.t[:, c * T:c * T + TW]

        def ATT(j, cols):
            return HH.t[:, (8 + j) * T + cols.start:(8 + j) * T + cols.stop]

        def ffn(layer, TW):
            norm_to_xn(TW, V_NF[0] + 8 * layer)
            for half in range(2):
                slot = 0
                for i in range(6):
                    nf = 2 if i < 5 else 1
                    wb = wnext("GU" if nf == 2 else "GU1")
                    w3 = wb.t[:, 0:8 * 2 * nf * 128].rearrange("p (c n) -> p c n", c=8)
                    for f in range(nf):
                        bg = bank()
                        for c in range(8):
                            mm(bg.t[:, :TW], w3[:, c, f * 128:(f + 1) * 128], XN.t[:, c, :TW], c == 0, c == 7, [wb, XNB[c]], [bg])
                        bu = bank()
                        for c in range(8):
                            mm(bu.t[:, :TW], w3[:, c, (nf + f) * 128:(nf + f + 1) * 128], XN.t[:, c, :TW], c == 0, c == 7, [wb, XNB[c]], [bu])
                        sg = tmp()
                        S.op("act", lambda e, bg=bg, sg=sg: e.activation(out=sg.t[:, :TW], in_=bg.t[:, :TW], func=AF.Silu), reads=[bg], writes=[sg])
                        S.op("dve", lambda e, bu=bu, sg=sg, slot=slot: e.tensor_tensor(out=hh3[:, slot, :TW], in0=bu.t[:, :TW], in1=sg.t[:, :TW], op=ALU.mult), reads=[bu, sg], writes=HHALL)
                        slot += 1
                for q in range(4):
                    wb = wnext("DN")
                    w3 = wb.t[:, 0:2816].rearrange("p (f n) -> p f n", f=11)
                    for o in range(2):
                        n = q * 2 + o
                        bk = bank()
                        for f in range(11):
                            mm(bk.t[:, :TW], w3[:, f, o * 128:(o + 1) * 128], hh3[:, f, :TW], f == 0, f == 10, [wb] + HHALL, [bk])
                        S.op("dve", lambda e, bk=bk, n=n: e.tensor_tensor(out=XT.t[:, n, :TW], in0=bk.t[:, :TW], in1=XT.t[:, n, :TW], op=ALU.add), reads=[bk, XT], writes=[XT])

        def pool_mixer(ti, TW, nseg, L, sample, first):
            norm_stats(TW)
            wb = wnext("PW")
            w4 = wb.t[:, 0:2048].rearrange("p (g c n) -> p g c n", g=4, c=2)
            W_ = 15 + L
            dB4 = [ucb[0], ucb[1], sq[0], sq[1]]

            def v3w(b):
                return b.t[:, 0:nseg * W_].rearrange("p (s l) -> p s l", s=nseg)

            for half in range(2):
                chunks = [4 * half + k_ for k_ in range(4)]
                st = {}
                for c in chunks:
                    ext, tA, tB = tmp(), tmp(), tmp()
                    st[c] = dict(ext=ext, ext3=v3w(ext), tmps=[tA, tB], cur=ext, cur3=v3w(ext), win=POOL_WINDOWS[c // 2])
                for c in chunks:
                    d_ = st[c]
                    norm_apply(c, V_NM[1], TW, d_["ext3"][:, :, 15:15 + L], d_["ext"], in_ap=XT.t[:, c, :TW].rearrange("p (s l) -> p s l", s=nseg),
                               rs_ap=rstd.t[:, :TW].rearrange("p (s l) -> p s l", s=nseg), force_dve=True)
                for c in chunks:
                    d_ = st[c]
                    ph = PHs[c] if sample else PH[c]
                    ph3 = ph.t[:] if sample else ph.t[:].rearrange("p (s l) -> p s l", s=1)
                    S.op("pool", lambda e, d_=d_, ph3=ph3: e.tensor_copy(out=d_["ext3"][:, :, 0:15], in_=ph3), reads=[ph], writes=[d_["ext"]])
                    S.op("pool", lambda e, d_=d_, ph3=ph3: e.tensor_copy(out=ph3, in_=d_["ext3"][:, :, L:L + 15]), reads=[d_["ext"]], writes=[ph])
                sh, lvl = 1, 0
                while sh < 16:
                    for c in chunks:
                        d_ = st[c]
                        if sh >= d_["win"]:
                            continue
                        nxt = d_["tmps"][lvl % 2]
                        nxt3 = v3w(nxt)
                        lo = 2 * sh - 1
                        eng = "pool" if sh in (1, 4) else "dve"
                        S.op(eng, lambda e, cur3=d_["cur3"], nxt3=nxt3, lo=lo, sh=sh: e.tensor_tensor(out=nxt3[:, :, lo:W_], in0=cur3[:, :, lo:W_], in1=cur3[:, :, lo - sh:W_ - sh], op=ALU.add), reads=[d_["cur"]], writes=[nxt])
                        d_["cur"], d_["cur3"] = nxt, nxt3
                    sh *= 2
                    lvl += 1
                for k_, c in enumerate(chunks):
                    d_ = st[c]
                    win = d_["win"]
                    db = dB4[k_]
                    d_["db"] = db
                    S.op("dve", lambda e, d_=d_, db=db, win=win: e.scalar_tensor_tensor(out=db.t[:, :TW].rearrange("p (s l) -> p s l", s=nseg), in0=d_["cur3"][:, :, 15:15 + L], scalar=1.0 / win, in1=d_["ext3"][:, :, 15:15 + L], op0=ALU.mult, op1=ALU.subtract),
                         reads=[d_["cur"], d_["ext"]], writes=[db])
                    if first:
                        t16 = d_["tmps"][0] if d_["cur"] is d_["tmps"][1] else d_["tmps"][1]
                        g = c // 2
                        S.op("dve", lambda e, d_=d_, t16=t16, g=g: e.tensor_tensor(out=t16.t[:, 0:16], in0=d_["cur"].t[:, 15:31], in1=cst.t[:, C_INV + 16 * g:C_INV + 16 * g + 16], op=ALU.mult), reads=[d_["cur"], cst], writes=[t16])
                        S.op("dve", lambda e, d_=d_, t16=t16, db=db: e.tensor_tensor(out=db.t[:, 0:16], in0=t16.t[:, 0:16], in1=d_["ext"].t[:, 15:31], op=ALU.subtract), reads=[t16, d_["ext"], db], writes=[db])
                for g in (2 * half, 2 * half + 1):
                    for o in range(2):
                        n = 2 * g + o
                        bk = bank()
                        for ci in range(2):
                            db = st[2 * g + ci]["db"]
                            mm(bk.t[:, :TW], w4[:, g, ci, o * 128:(o + 1) * 128], db.t[:, :TW], ci == 0, ci == 1, [wb, db], [bk])
                        S.op("dve", lambda e, bk=bk, n=n: e.scalar_tensor_tensor(out=XT.t[:, n, :TW], in0=bk.t[:, :TW], scalar=vecs.t[:, V_PS + n:V_PS + n + 1], in1=XT.t[:, n, :TW], op0=ALU.mult, op1=ALU.add),
                             reads=[bk, vecs, XT], writes=[XT])

        rot = {"l": 0, "w": 0}

        def attn_prompt(g):
            A2 = P2[0][:, :].rearrange("p (s t) -> p s t", s=2)
            C2 = P2[1][:, :].rearrange("p (s t) -> p s t", s=2)
            Ab, Cb = [banks[0], banks[1]], [banks[2], banks[3]]
            Oall = [[banks[4], banks[5]], [banks[6], banks[7]]]
            nst = 4 * g + 4
            seq = [(hp, st) for hp in range(4) for st in range(nst)]
            info = {}

            def geom(st):
                kb = nst - 1 - st
                jj = kb - 4 * g
                c0 = 128 * jj if jj > 0 else 0
                return kb, jj, c0, kb // 4, kb % 4

            def emitA(x):
                hp, st = seq[x]
                kb, jj, c0, tt, bi = geom(st)
                kt = KT[hp][tt]
                for s_ in range(2):
                    mm(A2[:, s_, c0:T], kt.t[64 * s_:64 * s_ + 64, bi * 128:(bi + 1) * 128], HH.t[64 * s_:64 * s_ + 64, hp * T + c0:hp * T + T], True, True, [kt, HHQ], [Ab[s_]])

            def S1(x):
                hp, st = seq[x]
                kb, jj, c0, tt, bi = geom(st)
                ea, eb, e3 = tmp2()
                S.op("act", lambda e: e.activation(out=e3[:, :, c0:T], in_=A2[:, :, c0:T], func=AF.Exp), reads=Ab, writes=[ea, eb])
                if jj >= 0:
                    S.op("pool", lambda e: e.tensor_tensor(out=e3[:, :, c0:c0 + 128], in0=e3[:, :, c0:c0 + 128], in1=mask2, op=ALU.mult), reads=[ea, eb, cst], writes=[ea, eb])
                k_ = rot["l"] % 3
                rot["l"] += 1
                S.op("act", lambda e: e.activation(out=LB[k_].t[:, :, c0:T], in_=e3[:, :, c0:T], func=AF.Ln, bias=1.0), reads=[ea, eb], writes=[LB[k_]])
                info[x] = (hp, st, c0, tt, bi, ea, eb, e3, k_)

            def S2(x):
                hp, st, c0, tt, bi, ea, eb, e3, k_ = info[x]
                for s_ in range(2):
                    mm(C2[:, s_, c0:T], trib.t[:, 0:128], LB[k_].t[:, s_, c0:T], st == 0, False, [trib, LB[k_]], [Cb[s_]])

            def S2p(x):
                hp, st, c0, tt, bi, ea, eb, e3, k_ = info[x]
                pa, pb, p3 = tmp2()
                S.op("act", lambda e: e.activation(out=p3[:, :, c0:T], in_=C2[:, :, c0:T], func=AF.Exp), reads=Cb, writes=[pa, pb])
                info[x] = info[x] + (pa, pb, p3)

            def S3a(x):
                hp, st, c0, tt, bi, ea, eb, e3, k_, pa, pb, p3 = info[x]
                if st < nst - 1:
                    for s_ in range(2):
                        mm(C2[:, s_, c0:T], trib.t[:, 128:256], LB[k_].t[:, s_, c0:T], False, False, [trib, LB[k_]], [Cb[s_]])

            def S3b(x):
                hp, st, c0, tt, bi, ea, eb, e3, k_, pa, pb, p3 = info[x]
                O = Oall[hp % 2]
                w_ = WB[rot["w"] % 2]
                rot["w"] += 1
                S.op("dve", lambda e: e.tensor_tensor(out=w_.t[:, :, c0:T], in0=e3[:, :, c0:T], in1=p3[:, :, c0:T], op=ALU.mult), reads=[ea, eb, pa, pb], writes=[w_])
                for s_ in range(2):
                    mm(O[s_].t[:, c0:T], VV[tt].t[:, bi, hp * 128:(hp + 1) * 128], w_.t[:, s_, c0:T], st == 0, st == nst - 1, [VV[tt], w_], [O[s_]])
                if st == nst - 1:
                    S.op("act", lambda e: e.activation(out=ATT(hp, slice(0, T))[0:64, :], in_=O[0].t[0:64, :], func=AF.Copy), reads=[O[0]], writes=[HHA])
                    S.op("dve", lambda e: e.tensor_copy(out=ATT(hp, slice(0, T))[64:128, :], in_=O[1].t[64:128, :]), reads=[O[1]], writes=[HHA])
                del info[x]

            N = len(seq)
            emitA(0)
            for n in range(N + 2):
                if n < N:
                    S1(n)
                if n >= 2:
                    S3a(n - 2)
                if 1 <= n < N + 1:
                    S2(n - 1)
                if n + 1 < N:
                    emitA(n + 1)
                if 1 <= n < N + 1:
                    S2p(n - 1)
                if n >= 2:
                    S3b(n - 2)

        def attn_sample():
            NQ = 8 * LS
            nst = NPB + 1
            rotc = {"k": 0, "v": 0}
            for sp_ in range(NS // 2):
                slots = []
                for st in range(nst):
                    for s in range(2):
                        slots.append((s, st))
                info = {}
                A = [banks[0], banks[1]]
                ACC = [banks[2], banks[3]]
                O = [banks[4], banks[5]]
                TR = [banks[6], banks[7]]
                pre = {}

                def load(n):
                    s, st = slots[n]
                    kb = NPB - st
                    if kb == NPB:
                        return
                    seq = 2 * sp_ + s
                    kf = tmp()
                    vf = tmp()
                    S.dma("sp", kf.t[:, 0:512], ck_d[seq, kb * 128:(kb + 1) * 128, :], writes=[kf])
                    S.dma("sp", vf.t[:, 0:512], cv_d[seq, kb * 128:(kb + 1) * 128, :], writes=[vf])
                    pre[n] = (kf, vf)

                ktmap = {}

                def S0(n):
                    s, st = slots[n]
                    if NPB - st == NPB:
                        return
                    kf, vf = pre.pop(n)
                    for j in range(4):
                        S.op("pe", lambda e, j=j: e.transpose(out=TR[s].t[:, j * 128:(j + 1) * 128], in_=kf.t[:, j * 128:(j + 1) * 128], identity=ident), reads=[kf, cst], writes=[TR[s]])
                    ktb = KTb[rotc["k"] % 2]
                    rotc["k"] += 1
                    S.op("dve", lambda e: e.tensor_copy(out=ktb.t[:, :, :], in_=TR[s].t[:, :].rearrange("p (j k) -> p j k", j=4)), reads=[TR[s]], writes=[ktb])
                    ktmap[n] = (ktb, vf)

                def S1(n):
                    s, st = slots[n]
                    seq = 2 * sp_ + s
                    kb = NPB - st
                    new = kb == NPB
                    rows = 128
                    qc = slice(seq * LS, (seq + 1) * LS)
                    if new:
                        vsrc = Vnew[seq]
                        for j in range(4):
                            mm(A[s].t[0:LS, j * 2 * LS:(j + 1) * 2 * LS].rearrange("p (i q) -> p i q", i=2), KTs[j].t[:, qc], QTZ.t[:, j, :, qc], True, True, [KTs[j], QTZ], [A[s]])
                    else:
                        ktb, vf = ktmap.pop(n)
                        vsrc = Vb[rotc["v"] % 3]
                        rotc["v"] += 1
                        if s == 0:
                            S.op("dve", lambda e: e.tensor_copy(out=vsrc.t[:, :], in_=vf.t[:, 0:512]), reads=[vf], writes=[vsrc])
                        else:
                            S.op("act", lambda e: e.activation(out=vsrc.t[:, :], in_=vf.t[:, 0:512], func=AF.Copy), reads=[vf], writes=[vsrc])
                        for j in range(4):
                            mm(A[s].t[:, j * 2 * LS:(j + 1) * 2 * LS].rearrange("p (i q) -> p i q", i=2), ktb.t[:, j, :], QTZ.t[:, j, :, qc], True, True, [ktb, QTZ], [A[s]])
                    e_ = tmp()
                    if new:
                        S.op("pool", lambda e: e.memset(e_.t[:, 0:NQ], 0.0), writes=[e_])
                        S.op("act", lambda e: e.activation(out=e_.t[0:LS, 0:NQ], in_=A[s].t[0:LS, 0:NQ], func=AF.Exp), reads=[A[s]], writes=[e_])
                        S.op("pool", lambda e: e.tensor_tensor(out=e_.t[0:LS, 0:NQ], in0=e_.t[0:LS, 0:NQ], in1=m32, op=ALU.mult), reads=[e_, cst], writes=[e_])
                    else:
                        S.op("act", lambda e: e.activation(out=e_.t[0:rows, 0:NQ], in_=A[s].t[0:rows, 0:NQ], func=AF.Exp), reads=[A[s]], writes=[e_])
                    l_ = lbuf[rot["l"] % 3]
                    rot["l"] += 1
                    S.op("act", lambda e: e.activation(out=l_.t[0:rows, 0:NQ], in_=e_.t[0:rows, 0:NQ], func=AF.Ln, bias=1.0), reads=[e_], writes=[l_])
                    info[n] = (s, st, seq, rows, e_, l_, vsrc)

                def S2(n):
                    s, st, seq, rows, e_, l_, vsrc = info[n]
                    mm(ACC[s].t[:, 0:NQ], trib.t[0:rows, 0:128], l_.t[0:rows, 0:NQ], st == 0, False, [trib, l_], [ACC[s]])
                    p_ = tmp()
                    S.op("act", lambda e: e.activation(out=p_.t[0:rows, 0:NQ], in_=ACC[s].t[0:rows, 0:NQ], func=AF.Exp), reads=[ACC[s]], writes=[p_])
                    info[n] = info[n] + (p_,)

                def S3(n):
                    s, st, seq, rows, e_, l_, vsrc, p_ = info[n]
                    if st < nst - 1:
                        mm(ACC[s].t[:, 0:NQ], trib.t[0:rows, 128:256], l_.t[0:rows, 0:NQ], False, False, [trib, l_], [ACC[s]])
                    w_ = wbuf[rot["w"] % 3]
                    rot["w"] += 1
                    S.op("dve", lambda e: e.tensor_tensor(out=w_.t[0:rows, 0:NQ], in0=e_.t[0:rows, 0:NQ], in1=p_.t[0:rows, 0:NQ], op=ALU.mult), reads=[e_, p_], writes=[w_])
                    for j in range(4):
                        mm(O[s].t[:, j * 2 * LS:(j + 1) * 2 * LS], vsrc.t[0:rows, j * 128:(j + 1) * 128], w_.t[0:rows, j * 2 * LS:(j + 1) * 2 * LS], st == 0 and j == 0, st == nst - 1, [vsrc, w_], [O[s]])
                    if st == nst - 1:
                        qc = slice(seq * LS, (seq + 1) * LS)
                        for h in range(8):
                            j, i = h // 2, h % 2
                            eng = "act" if h % 2 == 0 else "dve"
                            if eng == "act":
                                S.op("act", lambda e, h=h, j=j, i=i: e.activation(out=ATT(j, qc)[64 * i:64 * i + 64, :], in_=O[s].t[64 * i:64 * i + 64, h * LS:(h + 1) * LS], func=AF.Copy), reads=[O[s]], writes=[HHA])
                            else:
                                S.op("dve", lambda e, h=h, j=j, i=i: e.tensor_copy(out=ATT(j, qc)[64 * i:64 * i + 64, :], in_=O[s].t[64 * i:64 * i + 64, h * LS:(h + 1) * LS]), reads=[O[s]], writes=[HHA])
                    del info[n]

                N = len(slots)
                for n in range(min(3, N)):
                    load(n)
                S0(0)
                for n in range(N + 2):
                    if n + 1 < N:
                        S0(n + 1)
                    if n < N:
                        S1(n)
                    if n + 3 < N:
                        load(n + 3)
                    if 1 <= n < N + 1:
                        S2(n - 1)
                    if n >= 2:
                        S3(n - 2)

        try:
            ckpt(0)
            for ti in range(NT + 1):
                process_tile(ti)
                if ti == 0:
                    S.barrier()
                ckpt(10 + ti)
        except StopBuild:
            pass

        with nc.allow_non_contiguous_dma(reason="small state outputs"):
            S.dma("sp", hp_d.rearrange("o (c p) -> p (o c)", p=128), HC[:], reads=[HC])
            for s in range(NS):
                S.dma("sp", hs_d[s:s + 1, :].rearrange("o (c p) -> p (o c)", p=128), HCs.t[:, :, s], reads=[HCs])
            for c in range(4):
                S.dma("sp", convp_d[:, c * 128:(c + 1) * 128].rearrange("r p -> p r"), UE[c].t[:, 0:3], reads=[UE[c]])
                for s in range(NS):
                    S.dma("sp", convs_d[s * 3:(s + 1) * 3, c * 128:(c + 1) * 128].rearrange("r p -> p r"), UEs[c].t[:, s, 0:3], reads=[UEs[c]])
        def pool_out(srcs, dst):
            bks = [bank(), bank()]
            for c in range(8):
                bk = bks[c // 4]
                S.op("pe", lambda e, c=c, bk=bk: e.transpose(out=bk.t[0:15, (c % 4) * 128:(c % 4 + 1) * 128], in_=srcs[c], identity=ident), reads=[cst] + list(PH) + list(PHs), writes=[bk])
            S.op("act", lambda e: e.activation(out=ost.t[0:15, 0:512], in_=bks[0].t[0:15, :], func=AF.Copy), reads=[bks[0]], writes=[ost])
            S.op("dve", lambda e: e.tensor_copy(out=ost.t[0:15, 512:1024], in_=bks[1].t[0:15, :]), reads=[bks[1]], writes=[ost])
            S.dma("sp", dst, ost.t[0:15, :], reads=[ost])

        pool_out([PH[c].t[:, :] for c in range(8)], poolp_d[:, :])
        for s in range(NS):
            pool_out([PHs[c].t[:, s, :] for c in range(8)], pools_d[s * 15:(s + 1) * 15, :])
        for q in ("sp", "act"):
            for i, v in enumerate(S.dcount[q]):
                if v > 0:
                    S._wait("sp", ("d" + q, i), v)
        build.stats = dict(S.count, waits=S.nwaits)
    return nc


def run(inputs, n_cores, SEQ, NS, LS, PAST):
    f = lambda k: np.asarray(inputs[k], dtype=np.float32)
    xp, xs = f("x_prompt"), f("x_sample")
    ck, cv = f("cache_sb_k")[0], f("cache_sb_v")[0]
    h0, conv0, pool0 = f("state_lru_h")[0], f("state_lru_conv")[0], f("state_pool")[0]
    wsrc = host_chunks(f("hyb_w_in")[0], f("hyb_w_out")[0], f("ffn_gate"), f("ffn_up"), f("ffn_down"), f("pool_w")[0])
    vecs = np.zeros((128, NV), np.float32)
    nm, nf = f("norm_mix"), f("norm_ffn")
    vecs[:, 0:8] = fm(nm[0]); vecs[:, 8:16] = fm(nm[1]); vecs[:, 16:24] = fm(nf[0]); vecs[:, 24:32] = fm(nf[1])
    vecs[:, 32:40] = fm(f("norm_final")); vecs[:, 40:48] = fm(f("pool_scale")[0])
    cw = f("hyb_conv_w")[0]
    for i in range(4):
        vecs[:, 48 + 4 * i:52 + 4 * i] = fm(cw[i])
    vecs[:, 64:68] = fm(f("hyb_conv_b")[0]); vecs[:, 68:72] = fm(f("hyb_rg_b")[0]); vecs[:, 72:76] = fm(f("hyb_ig_b")[0]); vecs[:, 76:80] = fm(f("hyb_lambda")[0])
    gatew = np.zeros((128, 8, 128), np.float32)
    rg, ig = f("hyb_rg_w")[0], f("hyb_ig_w")[0]
    for c in range(4):
        for kk, wsel in enumerate((rg, ig)):
            gatew[0:64, 4 * kk + c, 0:64] = wsel[2 * c]
            gatew[64:128, 4 * kk + c, 64:128] = wsel[2 * c + 1]
    gatew = gatew.reshape(128, 1024)
    cst = host_consts()
    in_maps = []
    for r in range(n_cores):
        sl = slice(r * NS, (r + 1) * NS)
        in_maps.append({
            "xp": np.ascontiguousarray(xp[r]),
            "xs": np.ascontiguousarray(xs[sl].reshape(NS * LS, D)),
            "ck": np.ascontiguousarray(ck[sl].reshape(NS, PAST, 512)),
            "cv": np.ascontiguousarray(cv[sl].reshape(NS, PAST, 512)),
            "h0": np.ascontiguousarray(h0[sl].reshape(NS, 4, 128).transpose(2, 1, 0)),
            "conv0": np.ascontiguousarray(conv0[sl].reshape(NS, 3, 4, 128).transpose(3, 2, 0, 1)),
            "pool0": np.ascontiguousarray(pool0[sl].reshape(NS, 15, 8, 128).transpose(3, 2, 0, 1)),
            "wsrc": wsrc, "vecs": vecs, "gatew": gatew, "cst": cst,
        })
    nc = build(SEQ, NS, LS, PAST)
    res = run_bass_kernel_spmd(nc, in_maps, core_ids=list(range(n_cores)))
    R = res.results
    cat = lambda k: np.stack([np.asarray(R[r][k], dtype=np.float32) for r in range(n_cores)])
    y_p = cat("yp")
    y_s = cat("ys").reshape(n_cores * NS, LS, D)
    k_p = cat("kp").reshape(1, n_cores, SEQ, 8, 64)
    v_p = cat("vp").reshape(1, n_cores, SEQ, 8, 64)
    h_p = cat("hp").reshape(1, n_cores, 512)
    conv_p = cat("convp").reshape(1, n_cores, 3, 512)
    pool_p = cat("poolp").reshape(1, n_cores, 15, D)
    k_s = cat("ks").reshape(1, n_cores * NS, LS, 8, 64)
    v_s = cat("vs").reshape(1, n_cores * NS, LS, 8, 64)
    h_s = cat("hs").reshape(1, n_cores * NS, 512)
    conv_s = cat("convs").reshape(1, n_cores * NS, 3, 512)
    pool_s = cat("pools").reshape(1, n_cores * NS, 15, D)
    if DEBUG:
        run.dbg = {k: np.asarray(v) for k, v in R[0].items() if k.startswith('dbg_')}
    return (y_p, y_s, k_p, v_p, h_p, conv_p, pool_p, k_s, v_s, h_s, conv_s, pool_s)


def kernel(**inputs):
    return run(inputs, 8, 4096, 4, 32, 4096)
```

```python
import numpy as np
from contextlib import ExitStack
import concourse.bass as bass
import concourse.mybir as mybir
from concourse.bass_utils import run_bass_kernel_spmd

F32 = mybir.dt.float32
BF16 = mybir.dt.bfloat16
AF = mybir.ActivationFunctionType
ALU = mybir.AluOpType

D = 1024
DFF = 2816
NFC = 22
POOL_WINDOWS = (2, 4, 8, 16)
EPS = 1e-6
EPOCH = 30000
DEBUG = False
STOP = None


class StopBuild(Exception):
    pass


def ckpt(k):
    if STOP is not None and STOP == k:
        raise StopBuild()
WSLOT = 4096

def chunk_list():
    L = []
    for s in (3, 4, 0, 1, 2):
        L.append(("KN", s, 0, 4096))
    for s in range(2):
        L.append(("KO", s, 0, 4096))

    def ffn(layer):
        for half in range(2):
            f0 = half * 11
            for i in range(5):
                L.append(("GU", layer, f0 + 2 * i, 4096))
            L.append(("GU1", layer, f0 + 10, 2048))
            for q in range(4):
                L.append(("DN", layer, half * 4 + q, 2816))
    ffn(0)
    L.append(("PW", 0, 0, 2048))
    ffn(1)
    return L


CHUNKS = chunk_list()
NCH = len(CHUNKS)


def host_chunks(w_in, w_out, gate, up, down, pool_w):
    out = np.zeros((NCH, 128, WSLOT), np.float32)
    for i, (k, a, b, nel) in enumerate(CHUNKS):
        if k == "KN":
            out[i] = w_in[:, a * 512:(a + 1) * 512].reshape(8, 128, 512).transpose(1, 0, 2).reshape(128, 4096)
        elif k == "KO":
            out[i] = w_out[:, a * 512:(a + 1) * 512].reshape(8, 128, 512).transpose(1, 0, 2).reshape(128, 4096)
        elif k == "GU":
            g = gate[a][:, b * 128:(b + 2) * 128].reshape(8, 128, 256)
            u = up[a][:, b * 128:(b + 2) * 128].reshape(8, 128, 256)
            out[i] = np.concatenate([g, u], axis=2).transpose(1, 0, 2).reshape(128, 4096)
        elif k == "GU1":
            g = gate[a][:, b * 128:(b + 1) * 128].reshape(8, 128, 128)
            u = up[a][:, b * 128:(b + 1) * 128].reshape(8, 128, 128)
            out[i, :, :2048] = np.concatenate([g, u], axis=2).transpose(1, 0, 2).reshape(128, 2048)
        elif k == "DN":
            half, q = b // 4, b % 4
            blk = down[a][half * 1408:(half + 1) * 1408, q * 256:(q + 1) * 256]
            out[i, :, :2816] = blk.reshape(11, 128, 256).transpose(1, 0, 2).reshape(128, 2816)
        elif k == "PW":
            out[i, :, :2048] = pool_w.reshape(4, 2, 128, 256).transpose(2, 0, 1, 3).reshape(128, 2048)
    return out


V_NM = (0, 8)
V_NF = (16, 24)
V_NFIN = 32
V_PS = 40
V_CW = 48
V_CB = 64
V_RGB = 68
V_IGB = 72
V_LAM = 76
NV = 80
C_ID = 0
C_TRI = 128
C_MASK = 384
C_ONES = 512
C_INV = 640
C_M32 = 704
C_MASK2 = 960
NCST = 1216


def host_consts():
    c = np.zeros((128, NCST), np.float32)
    c[:, C_ID:C_ID + 128] = np.eye(128, dtype=np.float32)
    j = np.arange(128)[:, None]
    k = np.arange(128)[None, :]
    c[:, C_TRI:C_TRI + 128] = -1.0 * (j >= k)
    c[:, C_TRI + 128:C_TRI + 256] = -1.0 * (j < k)
    c[:, C_MASK:C_MASK + 128] = (j < k)
    c[:, C_ONES:C_ONES + 128] = 1.0 / D
    c[:, C_MASK2:C_MASK2 + 128] = (j < k)
    c[:, C_MASK2 + 128:C_MASK2 + 256] = (j < k)
    for g, w in enumerate(POOL_WINDOWS):
        c[:, C_INV + 16 * g:C_INV + 16 * (g + 1)] = 1.0 / np.minimum(w, np.arange(16) + 1.0)
    m32 = (np.arange(32)[:, None] < np.arange(32)[None, :]).astype(np.float32)
    c[:32, C_M32:C_M32 + 256] = np.tile(m32, (1, 8))
    return c


def fm(v):
    return np.ascontiguousarray(v.reshape(-1, 128).T)


class Buf:
    __slots__ = ("t", "writer", "readers", "name")

    def __init__(self, t, name=""):
        self.t = t
        self.writer = None
        self.readers = {}
        self.name = name

    def __getitem__(self, idx):
        return self.t[idx]


class Sched:
    def __init__(self, nc, es, n_epochs=6, n_dma_sems=12):
        self.nc = nc
        self.eng = {"pe": nc.tensor, "act": nc.scalar, "dve": nc.vector, "pool": nc.gpsimd, "sp": nc.sync}
        self.sems = {}
        self.count = {}
        self.waited = {k: {} for k in self.eng}
        for k in self.eng:
            self.count[k] = 0
            if k == "sp":
                continue
            self.sems[k] = [es.enter_context(nc.semaphore(name=f"s_{k}{i}")) for i in range(n_epochs)]
        self.dsems = {}
        self.dcount = {}
        self.dnext = {}
        for q in ("sp", "act"):
            self.dsems[q] = [es.enter_context(nc.semaphore(name=f"d_{q}{i}")) for i in range(n_dma_sems)]
            self.dcount[q] = [0] * n_dma_sems
            self.dnext[q] = 0
        self.semobj = {}
        for k, lst in self.sems.items():
            for i, s in enumerate(lst):
                self.semobj[(k, i)] = s
        for q, lst in self.dsems.items():
            for i, s in enumerate(lst):
                self.semobj[("d" + q, i)] = s
        self.nwaits = 0

    def _wait(self, engname, key, val):
        w = self.waited[engname]
        if w.get(key, 0) >= val:
            return
        self.eng[engname].wait_ge(self.semobj[key], val)
        w[key] = val
        self.nwaits += 1

    def _deps(self, engname, reads, writes):
        need = {}
        for b in reads:
            if b.writer is not None:
                k, v = b.writer
                if need.get(k, 0) < v:
                    need[k] = v
        for b in writes:
            if b.writer is not None:
                k, v = b.writer
                if need.get(k, 0) < v:
                    need[k] = v
            for k, v in b.readers.items():
                if need.get(k, 0) < v:
                    need[k] = v
        for k, v in need.items():
            if engname == "pe" and k[0] == "pe":
                continue
            self._wait(engname, k, v)

    def _mark(self, tok, reads, writes):
        k, v = tok
        for b in reads:
            if b.readers.get(k, 0) < v:
                b.readers[k] = v
        for b in writes:
            b.writer = tok
            b.readers = {}

    def op(self, engname, fn, reads=(), writes=()):
        self._deps(engname, reads, writes)
        inst = fn(self.eng[engname])
        c = self.count[engname]
        ep, v = c // EPOCH, c % EPOCH + 1
        inst.then_inc(self.sems[engname][ep], 1)
        self.count[engname] = c + 1
        tok = ((engname, ep), v)
        self._mark(tok, reads, writes)
        return tok

    def dma(self, q, out, in_, reads=(), writes=(), **kw):
        self._deps(q, reads, writes)
        i = self.dnext[q]
        self.dnext[q] = (i + 1) % len(self.dsems[q])
        key = ("d" + q, i)
        prev = self.dcount[q][i]
        if prev > 0:
            self._wait(q, key, prev)
        inst = self.eng[q].dma_start(out=out, in_=in_, **kw)
        val = prev + 16
        inst.then_inc(self.dsems[q][i], 16)
        self.dcount[q][i] = val
        tok = (key, val)
        self._mark(tok, reads, writes)
        return tok

    def barrier(self):
        toks = {}
        for k in ("pe", "act", "dve", "pool"):
            c = self.count[k]
            if c > 0:
                toks[(k, (c - 1) // EPOCH)] = (c - 1) % EPOCH + 1
        for q in ("sp", "act"):
            for i, v in enumerate(self.dcount[q]):
                if v > 0:
                    toks[("d" + q, i)] = v
        for e in ("pe", "act", "dve", "pool", "sp"):
            for key, v in toks.items():
                if key[0] == e:
                    continue
                self._wait(e, key, v)

    def finish(self, bufs):
        for b in bufs:
            if b.writer is not None:
                self._wait("sp", b.writer[0], b.writer[1])


def build(SEQ, NS, LS, PAST):
    T = 512
    NT = SEQ // T
    TS = NS * LS
    NPB = PAST // 128
    nc = bass.Bass("TRN2", target_bir_lowering=False)

    def din(name, shape, dt=F32):
        return nc.dram_tensor(name, list(shape), dt, kind="ExternalInput").ap()

    def dout(name, shape):
        return nc.dram_tensor(name, list(shape), F32, kind="ExternalOutput").ap()

    xp_d = din("xp", (SEQ, D))
    xs_d = din("xs", (TS, D))
    ck_d = din("ck", (NS, PAST, 512))
    cv_d = din("cv", (NS, PAST, 512))
    h0_d = din("h0", (128, 4, NS))
    conv0_d = din("conv0", (128, 4, NS, 3))
    pool0_d = din("pool0", (128, 8, NS, 15))
    wsrc_d = din("wsrc", (NCH, 128, WSLOT))
    vecs_d = din("vecs", (128, NV))
    gatew_d = din("gatew", (128, 8 * 128))
    cst_d = din("cst", (128, NCST))
    yp_d = dout("yp", (SEQ, D))
    ys_d = dout("ys", (TS, D))
    kp_d = dout("kp", (SEQ, 512))
    vp_d = dout("vp", (SEQ, 512))
    hp_d = dout("hp", (1, 512))
    convp_d = dout("convp", (3, 512))
    poolp_d = dout("poolp", (15, D))
    ks_d = dout("ks", (TS, 512))
    vs_d = dout("vs", (TS, 512))
    hs_d = dout("hs", (NS, 512))
    convs_d = dout("convs", (NS * 3, 512))
    pools_d = dout("pools", (NS * 15, D))
    wscr_d = nc.dram_tensor("wscr", [NCH, 128, WSLOT], BF16, kind="Internal").ap()

    es = ExitStack()
    with es:
        S = Sched(nc, es)
        cnt = [0]
        dbg_list = []

        def dbg(name, buf, ap, shape, dt=F32):
            if not DEBUG:
                return
            d = nc.dram_tensor('dbg_' + name, list(shape), dt, kind='ExternalOutput').ap()
            S.dma('sp', d, ap, reads=[buf])
            dbg_list.append(name)

        def sb(shape, dt, name=None):
            cnt[0] += 1
            nm = "sb_" + (name or f"t{cnt[0]}")
            return Buf(es.enter_context(nc.sbuf_tensor(nm, list(shape), dt)), nm)

        P2 = [es.enter_context(nc.psum_tensor(f"pbank{i}", [128, 1024], F32)) for i in range(4)]
        banks = [Buf(P2[i // 2][:, (i % 2) * 512:(i % 2 + 1) * 512], f"bank{i}") for i in range(8)]
        bank_rr = [0]

        def bank():
            b = banks[bank_rr[0] % 8]
            bank_rr[0] += 1
            return b

        KT = [[sb([128, T], BF16) for _ in range(NT)] for _ in range(4)]
        VV = [sb([128, 4, 512], BF16) for _ in range(max(NT, 8))]
        XT = sb([128, 8, T], F32, "XT")
        xstage = [sb([128, D], F32) for _ in range(2)]
        XN = sb([128, 8, T], BF16, "XN")
        XNB = [Buf(XN.t[:, c_, :], f"xn{c_}") for c_ in range(8)]
        HH = sb([128, 12 * T], BF16, "HH")
        hh3 = HH.t[:].rearrange("p (f t) -> p f t", t=T)
        HHQ, HHM, HHA = Buf(HH.t[:, 0:4 * T], "hhq"), Buf(HH.t[:, 4 * T:8 * T], "hhm"), Buf(HH.t[:, 8 * T:12 * T], "hha")
        HHALL = [HHQ, HHM, HHA]

        def QT(j, cols):
            return HH.t[:, j * T + cols.start:j * T + cols.stop]

        def MIX(c, cols):
            return HH.t[:, (4 + c) * T + cols.start:(4 + c) * T + cols.stop]

        UE = [sb([128, 3 + T], F32) for _ in range(4)]
        PH = [sb([128, 15], F32) for _ in range(8)]
        HC = sb([128, 4], F32, "HC")
        HCs = sb([128, 4, NS], F32, "HCs")
        NTMP = 12
        TB = [es.enter_context(nc.sbuf_tensor(f"sb_tb{k_}", [128, 1088], F32)) for k_ in range(NTMP // 2)]
        tmps = [Buf(TB[k_ // 2][:, (k_ % 2) * 544:(k_ % 2 + 1) * 544], f"tmp{k_}") for k_ in range(NTMP)]

        def tmp2():
            if tmp_rr[0] % 2 == 1:
                tmp_rr[0] += 1
            k_ = (tmp_rr[0] % NTMP) // 2
            a_, b_ = tmps[2 * k_], tmps[2 * k_ + 1]
            tmp_rr[0] += 2
            return a_, b_, TB[k_][:, :].rearrange("p (s l) -> p s l", s=2)[:, :, 0:T]
        tmp_rr = [0]

        def tmp():
            b = tmps[tmp_rr[0] % NTMP]
            tmp_rr[0] += 1
            return b

        LB = [sb([128, 2, T], BF16) for _ in range(3)]
        lbuf = [Buf(LB[k_].t[:, 0, :], f"lb{k_}") for k_ in range(3)]
        WB = [sb([128, 2, T], BF16) for _ in range(2)]
        wbuf = [Buf(WB[k_ % 2].t[:, k_ // 2, :], f"wb{k_}") for k_ in range(3)]
        ucb = [sb([128, T], BF16) for _ in range(2)]
        dB = ucb
        kvst = [sb([128, 512], F32) for _ in range(2)]
        wring = [sb([128, WSLOT], BF16) for _ in range(3)]
        wstg = [sb([128, 1024], F32) for _ in range(2)]
        KTb = [Buf(VV[5].t[:, k_, :].rearrange("p (j t) -> p j t", j=4), f"ktb{k_}") for k_ in range(2)]
        Vb = [Buf(VV[6].t[:, k_, :], f"vb{k_}") for k_ in range(3)]
        Vnew = [Buf(VV[s_].t[:, 0, :], f"vnew{s_}") for s_ in range(NS)]
        QTZ = Buf(VV[4].t[:, 0:2, :].rearrange("p a n -> p (a n)")[:, 0:8 * TS].rearrange("p (j i t) -> p j i t", j=4, i=2), "qtz")
        UEs = [Buf(VV[c_].t[:, 1, :].bitcast(F32)[:, 0:NS * (3 + LS)].rearrange("p (s l) -> p s l", s=NS), f"ues{c_}") for c_ in range(4)]
        PHs = [Buf(VV[c_].t[:, 3, :].bitcast(F32)[:, 128:128 + NS * 15].rearrange("p (s l) -> p s l", s=NS), f"phs{c_}") for c_ in range(8)]
        KTs = [Buf(VV[7].t[:, j_, 0:TS], f"kts{j_}") for j_ in range(4)]
        cst = sb([128, NCST], F32, "cst")
        vecs = sb([128, NV], F32, "vecs")
        cvec = sb([128, 16], F32, "cvec")
        gwf = wstg[0]
        gwb = sb([128, 1024], BF16, "gwb")
        trib = sb([128, 256], BF16, "trib")
        onesb = sb([128, 128], BF16, "onesb")
        sq = [sb([128, T], BF16) for _ in range(2)]
        rstd = sb([128, T], F32, "rstd")
        epsv = sb([128, 1], F32, "epsv")
        ost = xstage[0]
        WS = [Buf(None, f"ws{i}") for i in range(NCH)]

        ident = cst.t[:, C_ID:C_ID + 128]
        mask = cst.t[:, C_MASK:C_MASK + 128]
        m32 = cst.t[0:32, C_M32:C_M32 + 256]
        mask2 = cst.t[:, C_MASK2:C_MASK2 + 256].rearrange("p (s k) -> p s k", s=2)

        S.op("pool", lambda e: e.memset(epsv[:], EPS), writes=[epsv])
        S.dma("sp", cst[:], cst_d, writes=[cst])
        S.dma("sp", vecs[:], vecs_d, writes=[vecs])
        S.dma("sp", gwf[:], gatew_d, writes=[gwf])
        S.op("dve", lambda e: e.tensor_copy(out=trib[:], in_=cst.t[:, C_TRI:C_TRI + 256]), reads=[cst], writes=[trib])
        S.op("dve", lambda e: e.tensor_copy(out=onesb[:], in_=cst.t[:, C_ONES:C_ONES + 128]), reads=[cst], writes=[onesb])
        S.op("pool", lambda e: e.tensor_copy(out=gwb[:], in_=gwf[:]), reads=[gwf], writes=[gwb])
        S.op("act", lambda e: e.activation(out=cvec.t[:, 8:12], in_=vecs.t[:, V_LAM:V_LAM + 4], func=AF.Exp, scale=-1.0), reads=[vecs], writes=[cvec])
        S.op("act", lambda e: e.activation(out=cvec.t[:, 12:16], in_=cvec.t[:, 8:12], func=AF.Ln, bias=1.0), reads=[cvec], writes=[cvec])
        S.op("dve", lambda e: e.tensor_scalar(out=cvec.t[:, 0:4], in0=cvec.t[:, 12:16], scalar1=-8.0, scalar2=0.0, op0=ALU.mult, op1=ALU.add), reads=[cvec], writes=[cvec])
        S.op("dve", lambda e: e.tensor_scalar(out=cvec.t[:, 4:8], in0=cvec.t[:, 12:16], scalar1=-16.0, scalar2=0.0, op0=ALU.mult, op1=ALU.add), reads=[cvec], writes=[cvec])
        for c in range(4):
            S.op("pool", lambda e, c=c: e.memset(UE[c].t[:, 0:3], 0.0), writes=[UE[c]])
        S.op("pool", lambda e: e.memset(HC[:], 0.0), writes=[HC])
        S.dma("sp", HCs[:], h0_d, writes=[HCs])
        for c in range(8):
            S.op("pool", lambda e, c=c: e.memset(PH[c][:], 0.0), writes=[PH[c]])

        order = []
        for ti in range(NT + 1):
            for ci in range(NCH):
                order.append((ti, ci))
        wstate = {"emitted": 0, "next": 0, "stg": 0, "cast": 0}
        LOOK = 2

        stg_pool = list(wstg) + [Buf(VV[k_].t[:, :, :].rearrange("p a n -> p (a n)").bitcast(F32), f"vstg{k_}") for k_ in range(1, len(VV))]
        NSTG = (len(stg_pool) // 4) * 4
        LD = NSTG // 4 - 1
        wload = {"emitted": 0, "map": {}}

        def w_load(i):
            ti, ci = order[i]
            if ti != 0:
                return
            nel = CHUNKS[ci][3]
            q = nel // 4
            lst = []
            for k in range(4):
                st = stg_pool[wstate["stg"] % NSTG]
                wstate["stg"] += 1
                S.dma("sp", st.t[:, 0:q], wsrc_d[ci, :, k * q:(k + 1) * q], writes=[st])
                lst.append(st)
            wload["map"][i] = lst

        def w_emit(i):
            ti, ci = order[i]
            buf = wring[i % 3]
            nel = CHUNKS[ci][3]
            if ti == 0:
                while wload["emitted"] <= min(i + LD, NCH - 1):
                    w_load(wload["emitted"])
                    wload["emitted"] += 1
                q = nel // 4
                eng = "act" if (wstate["cast"] % 2 == 1) else "dve"
                wstate["cast"] += 1
                for k, st in enumerate(wload["map"].pop(i)):
                    if eng == "act":
                        S.op("act", lambda e, k=k, st=st: e.activation(out=buf.t[:, k * q:(k + 1) * q], in_=st.t[:, 0:q], func=AF.Copy), reads=[st], writes=[buf])
                    else:
                        S.op("dve", lambda e, k=k, st=st: e.tensor_copy(out=buf.t[:, k * q:(k + 1) * q], in_=st.t[:, 0:q]), reads=[st], writes=[buf])
                S.dma("sp", wscr_d[ci, :, 0:nel], buf.t[:, 0:nel], reads=[buf], writes=[WS[ci]])
            else:
                S.dma("sp", buf.t[:, 0:nel], wscr_d[ci, :, 0:nel], reads=[WS[ci]], writes=[buf])

        def wnext(kind):
            i = wstate["next"]
            wstate["next"] += 1
            assert CHUNKS[order[i][1]][0] == kind, (CHUNKS[order[i][1]], kind)
            while wstate["emitted"] <= min(i + LOOK, len(order) - 1):
                w_emit(wstate["emitted"])
                wstate["emitted"] += 1
            return wring[i % 3]

        def mm(out, lhsT, rhs, start, stop, R, W):
            S.op("pe", lambda e: e.matmul(out, lhsT=lhsT, rhs=rhs, start=start, stop=stop, skip_group_check=True), reads=R, writes=W)

        def norm_stats(TW):
            bk = bank()
            for c in range(8):
                s = (sq[0], sq[1], ucb[0], ucb[1])[c % 4]
                if c % 2 == 0:
                    S.op("act", lambda e, c=c, s=s: e.activation(out=s.t[:, :TW], in_=XT.t[:, c, :TW], func=AF.Square), reads=[XT], writes=[s])
                else:
                    S.op("dve", lambda e, c=c, s=s: e.tensor_tensor(out=s.t[:, :TW], in0=XT.t[:, c, :TW], in1=XT.t[:, c, :TW], op=ALU.mult), reads=[XT], writes=[s])
                mm(bk.t[:, :TW], onesb[:], s.t[:, :TW], c == 0, c == 7, [onesb, s], [bk])
            t1 = tmp()
            S.op("act", lambda e: e.activation(out=t1.t[:, :TW], in_=bk.t[:, :TW], func=AF.Ln, bias=epsv.t[:, 0:1]), reads=[bk, epsv], writes=[t1])
            S.op("act", lambda e: e.activation(out=rstd.t[:, :TW], in_=t1.t[:, :TW], func=AF.Exp, scale=-0.5), reads=[t1], writes=[rstd])

        def norm_to_xn(TW, gcol):
            norm_stats(TW)
            for c in range(8):
                norm_apply(c, gcol, TW, XN.t[:, c, :TW], XNB[c])

        def norm_apply(c, gcol, TW, out_ap, out_buf, in_ap=None, rs_ap=None, force_dve=False):
            in_ap = XT.t[:, c, :TW] if in_ap is None else in_ap
            rs_ap = rstd.t[:, :TW] if rs_ap is None else rs_ap
            gsc = vecs.t[:, gcol + c:gcol + c + 1]
            if c % 4 != 3 or force_dve:
                S.op("dve", lambda e: e.scalar_tensor_tensor(out=out_ap, in0=in_ap, scalar=gsc, in1=rs_ap, op0=ALU.mult, op1=ALU.mult), reads=[XT, vecs, rstd], writes=[out_buf])
            else:
                t_ = tmp()
                tv = t_.t[:, :TW] if len(in_ap.shape) == 2 else t_.t[:, :TW].rearrange("p (s l) -> p s l", s=in_ap.shape[1])
                S.op("act", lambda e: e.activation(out=tv, in_=in_ap, func=AF.Copy, scale=gsc), reads=[XT, vecs], writes=[t_])
                S.op("pool", lambda e: e.tensor_tensor(out=out_ap, in0=tv, in1=rs_ap, op=ALU.mult), reads=[t_, rstd], writes=[out_buf])

        def proj_fm(wb, j, TW, rhs_fn, R):
            bk = bank()
            w3 = wb.t[:].rearrange("p (c n) -> p c n", c=8)
            for c in range(8):
                mm(bk.t[:, :TW], w3[:, c, j * 128:(j + 1) * 128], rhs_fn(c), c == 0, c == 7, [wb] + (R(c) if callable(R) else R), [bk])
            return bk

        xpref = set()

        def process_tile(ti):
            sample = ti == NT
            TW = TS if sample else T
            nseg = NS if sample else 1
            L = LS if sample else T
            first = ti == 0
            t0 = 0 if sample else ti * T
            x_d = xs_d if sample else xp_d
            ntb = TW // 128
            cols = slice(0, TW)

            if sample:
                S.barrier()
                for s_ in range(NS):
                    S.op("pool", lambda e, s_=s_: e.memset(Vnew[s_].t[:, :], 0.0), writes=[Vnew[s_]])
                S.op("pool", lambda e: e.memset(QTZ.t[:, :, :, :], 0.0), writes=[QTZ])
                for c_ in range(4):
                    S.dma("sp", UEs[c_].t[:, :, 0:3], conv0_d[:, c_], writes=[UEs[c_]])
                for c_ in range(8):
                    S.dma("sp", PHs[c_].t[:, :, :], pool0_d[:, c_], writes=[PHs[c_]])
            for tb in range(ntb):
                xs_ = xstage[tb % 2]
                if (ti, tb) not in xpref:
                    S.dma("sp", xs_[:], x_d[t0 + tb * 128:t0 + (tb + 1) * 128, :], writes=[xs_])
                for half in range(2):
                    bk = bank()
                    for jj in range(4):
                        c = half * 4 + jj
                        S.op("pe", lambda e, c=c, jj=jj, bk=bk, xs_=xs_: e.transpose(out=bk.t[:, jj * 128:(jj + 1) * 128], in_=xs_.t[:, c * 128:(c + 1) * 128], identity=ident), reads=[xs_, cst], writes=[bk])
                    eng = "act" if half == 0 else "dve"
                    S.op(eng, lambda e, half=half, bk=bk, tb=tb: (e.activation(out=XT.t[:, half * 4:half * 4 + 4, tb * 128:(tb + 1) * 128], in_=bk.t[:].rearrange("p (j t) -> p j t", j=4), func=AF.Copy) if half == 0 else
                                                                   e.tensor_copy(out=XT.t[:, half * 4:half * 4 + 4, tb * 128:(tb + 1) * 128], in_=bk.t[:].rearrange("p (j t) -> p j t", j=4))), reads=[bk], writes=[XT])

            ckpt(100 * ti + 1)
            norm_to_xn(TW, V_NM[0])
            xn_rhs = lambda c: XN.t[:, c, :TW]
            if ti == 0:
                dbg('xt0', XT, XT.t[:, :, :], [128, 8, T])
                pass
                dbg('rstd0', rstd, rstd.t[:, :], [128, T])

            ckpt(100 * ti + 2)
            if sample:
                tblocks = [(s_ * LS, LS, s_ * LS) for s_ in range(NS)]
            else:
                tblocks = [(tb * 128, 128, t0 + tb * 128) for tb in range(ntb)]
            k_out = ks_d if sample else kp_d
            v_out = vs_d if sample else vp_d
            hc = HCs if sample else HC
            gw3 = gwb.t[:].rearrange("p (k n) -> p k n", k=8)

            def v3(ap2):
                return ap2.rearrange("p (s l) -> p s l", s=nseg)

            def ue_of(c):
                ue = UEs[c] if sample else UE[c]
                return ue, (ue.t[:] if sample else ue.t[:].rearrange("p (s l) -> p s l", s=1))

            wu = wnext("KN")
            for c in range(4):
                ue, ue3 = ue_of(c)
                bku = proj_fm(wu, c, TW, xn_rhs, lambda cc: [XNB[cc]])
                S.op("act", lambda e, bku=bku, ue3=ue3: e.activation(out=ue3[:, :, 3:3 + L], in_=v3(bku.t[:, :TW]), func=AF.Copy), reads=[bku], writes=[ue])
            wg = wnext("KN")
            gts, g2s = [], []
            for c in range(4):
                bkg = proj_fm(wg, c, TW, xn_rhs, lambda cc: [XNB[cc]])
                gt = tmp()
                S.op("act", lambda e, bkg=bkg, gt=gt: e.activation(out=gt.t[:, :TW], in_=bkg.t[:, :TW], func=AF.Copy), reads=[bkg], writes=[gt])
                gts.append(gt)
            for c in range(4):
                gt = gts[c]
                g2 = tmp()
                g2s.append(g2)
                S.op("pool", lambda e, gt=gt, g2=g2: e.tensor_tensor(out=g2.t[:, :TW], in0=gt.t[:, :TW], in1=gt.t[:, :TW], op=ALU.mult), reads=[gt], writes=[g2])
                S.op("dve", lambda e, g2=g2: e.tensor_scalar(out=g2.t[:, :TW], in0=g2.t[:, :TW], scalar1=0.044715, scalar2=1.0, op0=ALU.mult, op1=ALU.add), reads=[g2], writes=[g2])
                S.op("pool", lambda e, gt=gt, g2=g2: e.tensor_tensor(out=g2.t[:, :TW], in0=g2.t[:, :TW], in1=gt.t[:, :TW], op=ALU.mult), reads=[g2, gt], writes=[g2])
            for c in range(4):
                g2 = g2s[c]
                S.op("act", lambda e, g2=g2: e.activation(out=g2.t[:, :TW], in_=g2.t[:, :TW], func=AF.Sigmoid, scale=1.5957691216057308), reads=[g2], writes=[g2])
            for c in range(4):
                gt, g2 = gts[c], g2s[c]
                S.op("dve", lambda e, gt=gt, g2=g2, c=c: e.tensor_tensor(out=ATT(c, cols), in0=gt.t[:, :TW], in1=g2.t[:, :TW], op=ALU.mult), reads=[gt, g2], writes=[HHA])

            def piece_q():
                wq = wnext("KN")
                for j in range(4):
                    bk = proj_fm(wq, j, TW, xn_rhs, lambda c: [XNB[c]])
                    if sample:
                        S.op("act", lambda e, j=j, bk=bk: e.activation(out=QTZ.t[0:64, j, 0, :], in_=bk.t[0:64, :TW], func=AF.Copy, scale=0.125), reads=[bk], writes=[QTZ])
                        S.op("dve", lambda e, j=j, bk=bk: e.tensor_scalar(out=QTZ.t[64:128, j, 1, :], in0=bk.t[64:128, :TW], scalar1=0.125, scalar2=0.0, op0=ALU.mult, op1=ALU.add), reads=[bk], writes=[QTZ])
                    else:
                        S.op("act", lambda e, j=j, bk=bk: e.activation(out=QT(j, cols), in_=bk.t[:, :TW], func=AF.Copy, scale=0.125), reads=[bk], writes=[HHQ])

            kstate = {}

            def piece_kfm():
                wk = wnext("KN")
                kstate["wk"] = wk
                for j in range(4):
                    bk = proj_fm(wk, j, TW, xn_rhs, lambda c: [XNB[c]])
                    dst = KTs[j] if sample else KT[j][ti]
                    S.op("dve", lambda e, bk=bk, dst=dst: e.tensor_copy(out=dst.t[:, :TW], in_=bk.t[:, :TW]), reads=[bk], writes=[dst])

            def piece_ktm():
                wk = kstate["wk"]
                wk3 = wk.t[:].rearrange("p (c n) -> p c n", c=8)
                for bi, (c0, n, r0) in enumerate(tblocks):
                    bk = bank()
                    for c in range(8):
                        mm(bk.t[0:n, :], XN.t[:, c, c0:c0 + n], wk3[:, c, :], c == 0, c == 7, [XNB[c], wk], [bk])
                    st = kvst[bi % 2]
                    S.op("act", lambda e, bk=bk, st=st, n=n: e.activation(out=st.t[0:n, :], in_=bk.t[0:n, :], func=AF.Copy), reads=[bk], writes=[st])
                    S.dma("sp", k_out[r0:r0 + n, :], st.t[0:n, :], reads=[st])

            def piece_v():
                wv = wnext("KN")
                wv3 = wv.t[:].rearrange("p (c n) -> p c n", c=8)
                for bi, (c0, n, r0) in enumerate(tblocks):
                    bk = bank()
                    for c in range(8):
                        mm(bk.t[0:n, :], XN.t[:, c, c0:c0 + n], wv3[:, c, :], c == 0, c == 7, [XNB[c], wv], [bk])
                    st = kvst[(bi + 1) % 2]
                    S.op("act", lambda e, bk=bk, st=st, n=n: e.activation(out=st.t[0:n, :], in_=bk.t[0:n, :], func=AF.Copy), reads=[bk], writes=[st])
                    S.dma("sp", v_out[r0:r0 + n, :], st.t[0:n, :], reads=[st])
                    if sample:
                        S.op("dve", lambda e, bk=bk, bi=bi, n=n: e.tensor_copy(out=Vnew[bi].t[0:n, :], in_=bk.t[0:n, :]), reads=[bk], writes=[Vnew[bi]])
                    else:
                        S.op("dve", lambda e, bk=bk, bi=bi: e.tensor_copy(out=VV[ti].t[:, bi, :], in_=bk.t[:, :]), reads=[bk], writes=[VV[ti]])

            pieces = [piece_q, piece_kfm, piece_ktm, piece_v]

            def lruA(c):
                ue, ue3 = ue_of(c)
                uc = tmp()
                cw = lambda i: vecs.t[:, V_CW + 4 * i + c:V_CW + 4 * i + c + 1]
                S.op("dve", lambda e: e.tensor_scalar(out=v3(uc.t[:, :TW]), in0=ue3[:, :, 3:3 + L], scalar1=cw(3), scalar2=vecs.t[:, V_CB + c:V_CB + c + 1], op0=ALU.mult, op1=ALU.add), reads=[ue, vecs], writes=[uc])
                for i in (2, 1, 0):
                    S.op("dve", lambda e, i=i: e.scalar_tensor_tensor(out=v3(uc.t[:, :TW]), in0=ue3[:, :, i:i + L], scalar=cw(i), in1=v3(uc.t[:, :TW]), op0=ALU.mult, op1=ALU.add), reads=[ue, vecs, uc], writes=[uc])
                S.op("pool", lambda e: e.tensor_copy(out=ue3[:, :, 0:3], in_=ue3[:, :, L:L + 3]), reads=[ue], writes=[ue])
                ub = ucb[c % 2]
                S.op("act", lambda e: e.activation(out=ub.t[:, :TW], in_=uc.t[:, :TW], func=AF.Copy), reads=[uc], writes=[ub])
                return uc, ub

            def lruB(c, uc, ub):
                bkr = bank()
                mm(bkr.t[:, :TW], gw3[:, c, :], ub.t[:, :TW], True, True, [gwb, ub], [bkr])
                bki = bank()
                mm(bki.t[:, :TW], gw3[:, 4 + c, :], ub.t[:, :TW], True, True, [gwb, ub], [bki])
                rr = tmp()
                ii = tmp()
                S.op("act", lambda e: e.activation(out=rr.t[:, :TW], in_=bkr.t[:, :TW], func=AF.Sigmoid, bias=vecs.t[:, V_RGB + c:V_RGB + c + 1]), reads=[bkr, vecs], writes=[rr])
                S.op("act", lambda e: e.activation(out=ii.t[:, :TW], in_=bki.t[:, :TW], func=AF.Sigmoid, bias=vecs.t[:, V_IGB + c:V_IGB + c + 1]), reads=[bki, vecs], writes=[ii])
                a2 = tmp()
                aa = rr
                S.op("act", lambda e: e.activation(out=a2.t[:, :TW], in_=rr.t[:, :TW], func=AF.Exp, scale=cvec.t[:, 4 + c:5 + c]), reads=[rr, cvec], writes=[a2])
                S.op("act", lambda e: e.activation(out=aa.t[:, :TW], in_=rr.t[:, :TW], func=AF.Exp, scale=cvec.t[:, c:c + 1]), reads=[rr, cvec], writes=[aa])
                S.op("act", lambda e: e.activation(out=a2.t[:, :TW], in_=a2.t[:, :TW], func=AF.Ln, scale=-1.0, bias=1.0), reads=[a2], writes=[a2])
                S.op("act", lambda e: e.activation(out=a2.t[:, :TW], in_=a2.t[:, :TW], func=AF.Exp, scale=0.5), reads=[a2], writes=[a2])
                S.op("pool", lambda e: e.tensor_tensor(out=ii.t[:, :TW], in0=ii.t[:, :TW], in1=uc.t[:, :TW], op=ALU.mult), reads=[ii, uc], writes=[ii])
                S.op("dve", lambda e: e.tensor_tensor(out=ii.t[:, :TW], in0=ii.t[:, :TW], in1=a2.t[:, :TW], op=ALU.mult), reads=[ii, a2], writes=[ii])
                hh_ = a2
                for s_ in range(nseg):
                    init = hc.t[:, c, s_:s_ + 1] if sample else hc.t[:, c:c + 1]
                    S.op("dve", lambda e, s_=s_, init=init: e.tensor_tensor_scan(out=hh_.t[:, s_ * L:(s_ + 1) * L], data0=aa.t[:, s_ * L:(s_ + 1) * L], data1=ii.t[:, s_ * L:(s_ + 1) * L], initial=init, op0=ALU.mult, op1=ALU.add), reads=[aa, ii, hc], writes=[hh_])
                hdst = hc.t[:, c, :] if sample else hc.t[:, c:c + 1]
                S.op("pool", lambda e: e.tensor_copy(out=hdst, in_=v3(hh_.t[:, :TW])[:, :, L - 1]), reads=[hh_], writes=[hc])
                S.op("dve", lambda e: e.tensor_tensor(out=MIX(c, cols), in0=hh_.t[:, :TW], in1=ATT(c, cols), op=ALU.mult), reads=[hh_, HHA], writes=[HHM])

            for c in range(4):
                uc, ub = lruA(c)
                pieces[c]()
                lruB(c, uc, ub)

            ckpt(100 * ti + 4)
            if sample:
                attn_sample()
            else:
                attn_prompt(ti)

            ckpt(100 * ti + 5)
            for s_ in range(2):
                wo = wnext("KO")
                for j in range(4):
                    n = s_ * 4 + j
                    bk = proj_fm(wo, j, TW, lambda c: MIXALL(c, TW), lambda c: [HHA] if c < 4 else [HHM])
                    S.op("dve", lambda e, bk=bk, n=n: e.tensor_tensor(out=XT.t[:, n, :TW], in0=bk.t[:, :TW], in1=XT.t[:, n, :TW], op=ALU.add), reads=[bk, XT], writes=[XT])

            ckpt(100 * ti + 6)
            ffn(0, TW)
            ckpt(100 * ti + 7)
            pool_mixer(ti, TW, nseg, L, sample, first)
            ckpt(100 * ti + 8)
            if ti + 1 <= NT:
                nsample = (ti + 1 == NT)
                nx_d = xs_d if nsample else xp_d
                nt0 = 0 if nsample else (ti + 1) * T
                for tb in range(1 if nsample else 2):
                    S.dma("sp", xstage[tb][:], nx_d[nt0 + tb * 128:nt0 + (tb + 1) * 128, :], writes=[xstage[tb]])
                    xpref.add((ti + 1, tb))
            ffn(1, TW)
            ckpt(100 * ti + 9)
            norm_stats(TW)
            for c in range(8):
                norm_apply(c, V_NFIN, TW, XT.t[:, c, :TW], XT)
            y_d = ys_d if sample else yp_d
            for tb in range(ntb):
                ya, yb, y3 = tmp2()
                for half in range(2):
                    bk = bank()
                    for jj in range(4):
                        c = half * 4 + jj
                        S.op("pe", lambda e, c=c, jj=jj, bk=bk, tb=tb: e.transpose(out=bk.t[:, jj * 128:(jj + 1) * 128], in_=XT.t[:, c, tb * 128:(tb + 1) * 128], identity=ident), reads=[XT, cst], writes=[bk])
                    if half == 0:
                        S.op("act", lambda e, bk=bk, y3=y3: e.activation(out=y3[:, 0, :], in_=bk.t[:, :], func=AF.Copy), reads=[bk], writes=[ya])
                    else:
                        S.op("dve", lambda e, bk=bk, y3=y3: e.tensor_copy(out=y3[:, 1, :], in_=bk.t[:, :]), reads=[bk], writes=[yb])
                S.dma("sp", y_d[t0 + tb * 128:t0 + (tb + 1) * 128, :].rearrange("t (h m) -> t h m", h=2), y3, reads=[ya, yb])

        def MIXALL(c, TW):
            if c < 4:
                return HH.t[:, (8 + c) * T:(8 + c) * T + TW]
            return HH.t[:, c * T:c * T + TW]

        def ATT(j, cols):
            return HH.t[:, (8 + j) * T + cols.start:(8 + j) * T + cols.stop]

        def ffn(layer, TW):
            norm_to_xn(TW, V_NF[0] + 8 * layer)
            for half in range(2):
                slot = 0
                for i in range(6):
                    nf = 2 if i < 5 else 1
                    wb = wnext("GU" if nf == 2 else "GU1")
                    w3 = wb.t[:, 0:8 * 2 * nf * 128].rearrange("p (c n) -> p c n", c=8)
                    for f in range(nf):
                        bg = bank()
                        for c in range(8):
                            mm(bg.t[:, :TW], w3[:, c, f * 128:(f + 1) * 128], XN.t[:, c, :TW], c == 0, c == 7, [wb, XNB[c]], [bg])
                        bu = bank()
                        for c in range(8):
                            mm(bu.t[:, :TW], w3[:, c, (nf + f) * 128:(nf + f + 1) * 128], XN.t[:, c, :TW], c == 0, c == 7, [wb, XNB[c]], [bu])
                        sg = tmp()
                        S.op("act", lambda e, bg=bg, sg=sg: e.activation(out=sg.t[:, :TW], in_=bg.t[:, :TW], func=AF.Silu), reads=[bg], writes=[sg])
                        S.op("dve", lambda e, bu=bu, sg=sg, slot=slot: e.tensor_tensor(out=hh3[:, slot, :TW], in0=bu.t[:, :TW], in1=sg.t[:, :TW], op=ALU.mult), reads=[bu, sg], writes=HHALL)
                        slot += 1
                for q in range(4):
                    wb = wnext("DN")
                    w3 = wb.t[:, 0:2816].rearrange("p (f n) -> p f n", f=11)
                    for o in range(2):
                        n = q * 2 + o
                        bk = bank()
                        for f in range(11):
                            mm(bk.t[:, :TW], w3[:, f, o * 128:(o + 1) * 128], hh3[:, f, :TW], f == 0, f == 10, [wb] + HHALL, [bk])
                        S.op("dve", lambda e, bk=bk, n=n: e.tensor_tensor(out=XT.t[:, n, :TW], in0=bk.t[:, :TW], in1=XT.t[:, n, :TW], op=ALU.add), reads=[bk, XT], writes=[XT])

        def pool_mixer(ti, TW, nseg, L, sample, first):
            norm_stats(TW)
            wb = wnext("PW")
            w4 = wb.t[:, 0:2048].rearrange("p (g c n) -> p g c n", g=4, c=2)
            W_ = 15 + L
            dB4 = [ucb[0], ucb[1], sq[0], sq[1]]

            def v3w(b):
                return b.t[:, 0:nseg * W_].rearrange("p (s l) -> p s l", s=nseg)

            for half in range(2):
                chunks = [4 * half + k_ for k_ in range(4)]
                st = {}
                for c in chunks:
                    ext, tA, tB = tmp(), tmp(), tmp()
                    st[c] = dict(ext=ext, ext3=v3w(ext), tmps=[tA, tB], cur=ext, cur3=v3w(ext), win=POOL_WINDOWS[c // 2])
                for c in chunks:
                    d_ = st[c]
                    norm_apply(c, V_NM[1], TW, d_["ext3"][:, :, 15:15 + L], d_["ext"], in_ap=XT.t[:, c, :TW].rearrange("p (s l) -> p s l", s=nseg),
                               rs_ap=rstd.t[:, :TW].rearrange("p (s l) -> p s l", s=nseg), force_dve=True)
                for c in chunks:
                    d_ = st[c]
                    ph = PHs[c] if sample else PH[c]
                    ph3 = ph.t[:] if sample else ph.t[:].rearrange("p (s l) -> p s l", s=1)
                    S.op("pool", lambda e, d_=d_, ph3=ph3: e.tensor_copy(out=d_["ext3"][:, :, 0:15], in_=ph3), reads=[ph], writes=[d_["ext"]])
                    S.op("pool", lambda e, d_=d_, ph3=ph3: e.tensor_copy(out=ph3, in_=d_["ext3"][:, :, L:L + 15]), reads=[d_["ext"]], writes=[ph])
                sh, lvl = 1, 0
                while sh < 16:
                    for c in chunks:
                        d_ = st[c]
                        if sh >= d_["win"]:
                            continue
                        nxt = d_["tmps"][lvl % 2]
                        nxt3 = v3w(nxt)
                        lo = 2 * sh - 1
                        eng = "pool" if sh in (1, 4) else "dve"
                        S.op(eng, lambda e, cur3=d_["cur3"], nxt3=nxt3, lo=lo, sh=sh: e.tensor_tensor(out=nxt3[:, :, lo:W_], in0=cur3[:, :, lo:W_], in1=cur3[:, :, lo - sh:W_ - sh], op=ALU.add), reads=[d_["cur"]], writes=[nxt])
                        d_["cur"], d_["cur3"] = nxt, nxt3
                    sh *= 2
                    lvl += 1
                for k_, c in enumerate(chunks):
                    d_ = st[c]
                    win = d_["win"]
                    db = dB4[k_]
                    d_["db"] = db
                    S.op("dve", lambda e, d_=d_, db=db, win=win: e.scalar_tensor_tensor(out=db.t[:, :TW].rearrange("p (s l) -> p s l", s=nseg), in0=d_["cur3"][:, :, 15:15 + L], scalar=1.0 / win, in1=d_["ext3"][:, :, 15:15 + L], op0=ALU.mult, op1=ALU.subtract),
                         reads=[d_["cur"], d_["ext"]], writes=[db])
                    if first:
                        t16 = d_["tmps"][0] if d_["cur"] is d_["tmps"][1] else d_["tmps"][1]
                        g = c // 2
                        S.op("dve", lambda e, d_=d_, t16=t16, g=g: e.tensor_tensor(out=t16.t[:, 0:16], in0=d_["cur"].t[:, 15:31], in1=cst.t[:, C_INV + 16 * g:C_INV + 16 * g + 16], op=ALU.mult), reads=[d_["cur"], cst], writes=[t16])
                        S.op("dve", lambda e, d_=d_, t16=t16, db=db: e.tensor_tensor(out=db.t[:, 0:16], in0=t16.t[:, 0:16], in1=d_["ext"].t[:, 15:31], op=ALU.subtract), reads=[t16, d_["ext"], db], writes=[db])
                for g in (2 * half, 2 * half + 1):
                    for o in range(2):
                        n = 2 * g + o
                        bk = bank()
                        for ci in range(2):
                            db = st[2 * g + ci]["db"]
                            mm(bk.t[:, :TW], w4[:, g, ci, o * 128:(o + 1) * 128], db.t[:, :TW], ci == 0, ci == 1, [wb, db], [bk])
                        S.op("dve", lambda e, bk=bk, n=n: e.scalar_tensor_tensor(out=XT.t[:, n, :TW], in0=bk.t[:, :TW], scalar=vecs.t[:, V_PS + n:V_PS + n + 1], in1=XT.t[:, n, :TW], op0=ALU.mult, op1=ALU.add),
                             reads=[bk, vecs, XT], writes=[XT])

        rot = {"l": 0, "w": 0}

        def attn_prompt(g):
            A2 = P2[0][:, :].rearrange("p (s t) -> p s t", s=2)
            C2 = P2[1][:, :].rearrange("p (s t) -> p s t", s=2)
            Ab, Cb = [banks[0], banks[1]], [banks[2], banks[3]]
            Oall = [[banks[4], banks[5]], [banks[6], banks[7]]]
            nst = 4 * g + 4
            seq = [(hp, st) for hp in range(4) for st in range(nst)]
            info = {}

            def geom(st):
                kb = nst - 1 - st
                jj = kb - 4 * g
                c0 = 128 * jj if jj > 0 else 0
                return kb, jj, c0, kb // 4, kb % 4

            def emitA(x):
                hp, st = seq[x]
                kb, jj, c0, tt, bi = geom(st)
                kt = KT[hp][tt]
                for s_ in range(2):
                    mm(A2[:, s_, c0:T], kt.t[64 * s_:64 * s_ + 64, bi * 128:(bi + 1) * 128], HH.t[64 * s_:64 * s_ + 64, hp * T + c0:hp * T + T], True, True, [kt, HHQ], [Ab[s_]])

            def S1(x):
                hp, st = seq[x]
                kb, jj, c0, tt, bi = geom(st)
                ea, eb, e3 = tmp2()
                S.op("act", lambda e: e.activation(out=e3[:, :, c0:T], in_=A2[:, :, c0:T], func=AF.Exp), reads=Ab, writes=[ea, eb])
                if jj >= 0:
                    S.op("pool", lambda e: e.tensor_tensor(out=e3[:, :, c0:c0 + 128], in0=e3[:, :, c0:c0 + 128], in1=mask2, op=ALU.mult), reads=[ea, eb, cst], writes=[ea, eb])
                k_ = rot["l"] % 3
                rot["l"] += 1
                S.op("act", lambda e: e.activation(out=LB[k_].t[:, :, c0:T], in_=e3[:, :, c0:T], func=AF.Ln, bias=1.0), reads=[ea, eb], writes=[LB[k_]])
                info[x] = (hp, st, c0, tt, bi, ea, eb, e3, k_)

            def S2(x):
                hp, st, c0, tt, bi, ea, eb, e3, k_ = info[x]
                for s_ in range(2):
                    mm(C2[:, s_, c0:T], trib.t[:, 0:128], LB[k_].t[:, s_, c0:T], st == 0, False, [trib, LB[k_]], [Cb[s_]])

            def S2p(x):
                hp, st, c0, tt, bi, ea, eb, e3, k_ = info[x]
                pa, pb, p3 = tmp2()
                S.op("act", lambda e: e.activation(out=p3[:, :, c0:T], in_=C2[:, :, c0:T], func=AF.Exp), reads=Cb, writes=[pa, pb])
                info[x] = info[x] + (pa, pb, p3)

            def S3a(x):
                hp, st, c0, tt, bi, ea, eb, e3, k_, pa, pb, p3 = info[x]
                if st < nst - 1:
                    for s_ in range(2):
                        mm(C2[:, s_, c0:T], trib.t[:, 128:256], LB[k_].t[:, s_, c0:T], False, False, [trib, LB[k_]], [Cb[s_]])

            def S3b(x):
                hp, st, c0, tt, bi, ea, eb, e3, k_, pa, pb, p3 = info[x]
                O = Oall[hp % 2]
                w_ = WB[rot["w"] % 2]
                rot["w"] += 1
                S.op("dve", lambda e: e.tensor_tensor(out=w_.t[:, :, c0:T], in0=e3[:, :, c0:T], in1=p3[:, :, c0:T], op=ALU.mult), reads=[ea, eb, pa, pb], writes=[w_])
                for s_ in range(2):
                    mm(O[s_].t[:, c0:T], VV[tt].t[:, bi, hp * 128:(hp + 1) * 128], w_.t[:, s_, c0:T], st == 0, st == nst - 1, [VV[tt], w_], [O[s_]])
                if st == nst - 1:
                    S.op("act", lambda e: e.activation(out=ATT(hp, slice(0, T))[0:64, :], in_=O[0].t[0:64, :], func=AF.Copy), reads=[O[0]], writes=[HHA])
                    S.op("dve", lambda e: e.tensor_copy(out=ATT(hp, slice(0, T))[64:128, :], in_=O[1].t[64:128, :]), reads=[O[1]], writes=[HHA])
                del info[x]

            N = len(seq)
            emitA(0)
            for n in range(N + 2):
                if n < N:
                    S1(n)
                if n >= 2:
                    S3a(n - 2)
                if 1 <= n < N + 1:
                    S2(n - 1)
                if n + 1 < N:
                    emitA(n + 1)
                if 1 <= n < N + 1:
                    S2p(n - 1)
                if n >= 2:
                    S3b(n - 2)

        def attn_sample():
            NQ = 8 * LS
            nst = NPB + 1
            rotc = {"k": 0, "v": 0}
            for sp_ in range(NS // 2):
                slots = []
                for st in range(nst):
                    for s in range(2):
                        slots.append((s, st))
                info = {}
                A = [banks[0], banks[1]]
                ACC = [banks[2], banks[3]]
                O = [banks[4], banks[5]]
                TR = [banks[6], banks[7]]
                pre = {}

                def load(n):
                    s, st = slots[n]
                    kb = NPB - st
                    if kb == NPB:
                        return
                    seq = 2 * sp_ + s
                    kf = tmp()
                    vf = tmp()
                    S.dma("sp", kf.t[:, 0:512], ck_d[seq, kb * 128:(kb + 1) * 128, :], writes=[kf])
                    S.dma("sp", vf.t[:, 0:512], cv_d[seq, kb * 128:(kb + 1) * 128, :], writes=[vf])
                    pre[n] = (kf, vf)

                ktmap = {}

                def S0(n):
                    s, st = slots[n]
                    if NPB - st == NPB:
                        return
                    kf, vf = pre.pop(n)
                    for j in range(4):
                        S.op("pe", lambda e, j=j: e.transpose(out=TR[s].t[:, j * 128:(j + 1) * 128], in_=kf.t[:, j * 128:(j + 1) * 128], identity=ident), reads=[kf, cst], writes=[TR[s]])
                    ktb = KTb[rotc["k"] % 2]
                    rotc["k"] += 1
                    S.op("dve", lambda e: e.tensor_copy(out=ktb.t[:, :, :], in_=TR[s].t[:, :].rearrange("p (j k) -> p j k", j=4)), reads=[TR[s]], writes=[ktb])
                    ktmap[n] = (ktb, vf)

                def S1(n):
                    s, st = slots[n]
                    seq = 2 * sp_ + s
                    kb = NPB - st
                    new = kb == NPB
                    rows = 128
                    qc = slice(seq * LS, (seq + 1) * LS)
                    if new:
                        vsrc = Vnew[seq]
                        for j in range(4):
                            mm(A[s].t[0:LS, j * 2 * LS:(j + 1) * 2 * LS].rearrange("p (i q) -> p i q", i=2), KTs[j].t[:, qc], QTZ.t[:, j, :, qc], True, True, [KTs[j], QTZ], [A[s]])
                    else:
                        ktb, vf = ktmap.pop(n)
                        vsrc = Vb[rotc["v"] % 3]
                        rotc["v"] += 1
                        if s == 0:
                            S.op("dve", lambda e: e.tensor_copy(out=vsrc.t[:, :], in_=vf.t[:, 0:512]), reads=[vf], writes=[vsrc])
                        else:
                            S.op("act", lambda e: e.activation(out=vsrc.t[:, :], in_=vf.t[:, 0:512], func=AF.Copy), reads=[vf], writes=[vsrc])
                        for j in range(4):
                            mm(A[s].t[:, j * 2 * LS:(j + 1) * 2 * LS].rearrange("p (i q) -> p i q", i=2), ktb.t[:, j, :], QTZ.t[:, j, :, qc], True, True, [ktb, QTZ], [A[s]])
                    e_ = tmp()
                    if new:
                        S.op("pool", lambda e: e.memset(e_.t[:, 0:NQ], 0.0), writes=[e_])
                        S.op("act", lambda e: e.activation(out=e_.t[0:LS, 0:NQ], in_=A[s].t[0:LS, 0:NQ], func=AF.Exp), reads=[A[s]], writes=[e_])
                        S.op("pool", lambda e: e.tensor_tensor(out=e_.t[0:LS, 0:NQ], in0=e_.t[0:LS, 0:NQ], in1=m32, op=ALU.mult), reads=[e_, cst], writes=[e_])
                    else:
                        S.op("act", lambda e: e.activation(out=e_.t[0:rows, 0:NQ], in_=A[s].t[0:rows, 0:NQ], func=AF.Exp), reads=[A[s]], writes=[e_])
                    l_ = lbuf[rot["l"] % 3]
                    rot["l"] += 1
                    S.op("act", lambda e: e.activation(out=l_.t[0:rows, 0:NQ], in_=e_.t[0:rows, 0:NQ], func=AF.Ln, bias=1.0), reads=[e_], writes=[l_])
                    info[n] = (s, st, seq, rows, e_, l_, vsrc)

                def S2(n):
                    s, st, seq, rows, e_, l_, vsrc = info[n]
                    mm(ACC[s].t[:, 0:NQ], trib.t[0:rows, 0:128], l_.t[0:rows, 0:NQ], st == 0, False, [trib, l_], [ACC[s]])
                    p_ = tmp()
                    S.op("act", lambda e: e.activation(out=p_.t[0:rows, 0:NQ], in_=ACC[s].t[0:rows, 0:NQ], func=AF.Exp), reads=[ACC[s]], writes=[p_])
                    info[n] = info[n] + (p_,)

                def S3(n):
                    s, st, seq, rows, e_, l_, vsrc, p_ = info[n]
                    if st < nst - 1:
                        mm(ACC[s].t[:, 0:NQ], trib.t[0:rows, 128:256], l_.t[0:rows, 0:NQ], False, False, [trib, l_], [ACC[s]])
                    w_ = wbuf[rot["w"] % 3]
                    rot["w"] += 1
                    S.op("dve", lambda e: e.tensor_tensor(out=w_.t[0:rows, 0:NQ], in0=e_.t[0:rows, 0:NQ], in1=p_.t[0:rows, 0:NQ], op=ALU.mult), reads=[e_, p_], writes=[w_])
                    for j in range(4):
                        mm(O[s].t[:, j * 2 * LS:(j + 1) * 2 * LS], vsrc.t[0:rows, j * 128:(j + 1) * 128], w_.t[0:rows, j * 2 * LS:(j + 1) * 2 * LS], st == 0 and j == 0, st == nst - 1, [vsrc, w_], [O[s]])
                    if st == nst - 1:
                        qc = slice(seq * LS, (seq + 1) * LS)
                        for h in range(8):
                            j, i = h // 2, h % 2
                            eng = "act" if h % 2 == 0 else "dve"
                            if eng == "act":
                                S.op("act", lambda e, h=h, j=j, i=i: e.activation(out=ATT(j, qc)[64 * i:64 * i + 64, :], in_=O[s].t[64 * i:64 * i + 64, h * LS:(h + 1) * LS], func=AF.Copy), reads=[O[s]], writes=[HHA])
                            else:
                                S.op("dve", lambda e, h=h, j=j, i=i: e.tensor_copy(out=ATT(j, qc)[64 * i:64 * i + 64, :], in_=O[s].t[64 * i:64 * i + 64, h * LS:(h + 1) * LS]), reads=[O[s]], writes=[HHA])
                    del info[n]

                N = len(slots)
                for n in range(min(3, N)):
                    load(n)
                S0(0)
                for n in range(N + 2):
                    if n + 1 < N:
                        S0(n + 1)
                    if n < N:
                        S1(n)
                    if n + 3 < N:
                        load(n + 3)
                    if 1 <= n < N + 1:
                        S2(n - 1)
                    if n >= 2:
                        S3(n - 2)

        try:
            ckpt(0)
            for ti in range(NT + 1):
                process_tile(ti)
                if ti == 0:
                    S.barrier()
                ckpt(10 + ti)
        except StopBuild:
            pass

        with nc.allow_non_contiguous_dma(reason="small state outputs"):
            S.dma("sp", hp_d.rearrange("o (c p) -> p (o c)", p=128), HC[:], reads=[HC])
            for s in range(NS):
                S.dma("sp", hs_d[s:s + 1, :].rearrange("o (c p) -> p (o c)", p=128), HCs.t[:, :, s], reads=[HCs])
            for c in range(4):
                S.dma("sp", convp_d[:, c * 128:(c + 1) * 128].rearrange("r p -> p r"), UE[c].t[:, 0:3], reads=[UE[c]])
                for s in range(NS):
                    S.dma("sp", convs_d[s * 3:(s + 1) * 3, c * 128:(c + 1) * 128].rearrange("r p -> p r"), UEs[c].t[:, s, 0:3], reads=[UEs[c]])
        def pool_out(srcs, dst):
            bks = [bank(), bank()]
            for c in range(8):
                bk = bks[c // 4]
                S.op("pe", lambda e, c=c, bk=bk: e.transpose(out=bk.t[0:15, (c % 4) * 128:(c % 4 + 1) * 128], in_=srcs[c], identity=ident), reads=[cst] + list(PH) + list(PHs), writes=[bk])
            S.op("act", lambda e: e.activation(out=ost.t[0:15, 0:512], in_=bks[0].t[0:15, :], func=AF.Copy), reads=[bks[0]], writes=[ost])
            S.op("dve", lambda e: e.tensor_copy(out=ost.t[0:15, 512:1024], in_=bks[1].t[0:15, :]), reads=[bks[1]], writes=[ost])
            S.dma("sp", dst, ost.t[0:15, :], reads=[ost])

        pool_out([PH[c].t[:, :] for c in range(8)], poolp_d[:, :])
        for s in range(NS):
            pool_out([PHs[c].t[:, s, :] for c in range(8)], pools_d[s * 15:(s + 1) * 15, :])
        for q in ("sp", "act"):
            for i, v in enumerate(S.dcount[q]):
                if v > 0:
                    S._wait("sp", ("d" + q, i), v)
        build.stats = dict(S.count, waits=S.nwaits)
    return nc


def run(inputs, n_cores, SEQ, NS, LS, PAST):
    f = lambda k: np.asarray(inputs[k], dtype=np.float32)
    xp, xs = f("x_prompt"), f("x_sample")
    ck, cv = f("cache_sb_k")[0], f("cache_sb_v")[0]
    h0, conv0, pool0 = f("state_lru_h")[0], f("state_lru_conv")[0], f("state_pool")[0]
    wsrc = host_chunks(f("hyb_w_in")[0], f("hyb_w_out")[0], f("ffn_gate"), f("ffn_up"), f("ffn_down"), f("pool_w")[0])
    vecs = np.zeros((128, NV), np.float32)
    nm, nf = f("norm_mix"), f("norm_ffn")
    vecs[:, 0:8] = fm(nm[0]); vecs[:, 8:16] = fm(nm[1]); vecs[:, 16:24] = fm(nf[0]); vecs[:, 24:32] = fm(nf[1])
    vecs[:, 32:40] = fm(f("norm_final")); vecs[:, 40:48] = fm(f("pool_scale")[0])
    cw = f("hyb_conv_w")[0]
    for i in range(4):
        vecs[:, 48 + 4 * i:52 + 4 * i] = fm(cw[i])
    vecs[:, 64:68] = fm(f("hyb_conv_b")[0]); vecs[:, 68:72] = fm(f("hyb_rg_b")[0]); vecs[:, 72:76] = fm(f("hyb_ig_b")[0]); vecs[:, 76:80] = fm(f("hyb_lambda")[0])
    gatew = np.zeros((128, 8, 128), np.float32)
    rg, ig = f("hyb_rg_w")[0], f("hyb_ig_w")[0]
    for c in range(4):
        for kk, wsel in enumerate((rg, ig)):
            gatew[0:64, 4 * kk + c, 0:64] = wsel[2 * c]
            gatew[64:128, 4 * kk + c, 64:128] = wsel[2 * c + 1]
    gatew = gatew.reshape(128, 1024)
    cst = host_consts()
    in_maps = []
    for r in range(n_cores):
        sl = slice(r * NS, (r + 1) * NS)
        in_maps.append({
            "xp": np.ascontiguousarray(xp[r]),
            "xs": np.ascontiguousarray(xs[sl].reshape(NS * LS, D)),
            "ck": np.ascontiguousarray(ck[sl].reshape(NS, PAST, 512)),
            "cv": np.ascontiguousarray(cv[sl].reshape(NS, PAST, 512)),
            "h0": np.ascontiguousarray(h0[sl].reshape(NS, 4, 128).transpose(2, 1, 0)),
            "conv0": np.ascontiguousarray(conv0[sl].reshape(NS, 3, 4, 128).transpose(3, 2, 0, 1)),
            "pool0": np.ascontiguousarray(pool0[sl].reshape(NS, 15, 8, 128).transpose(3, 2, 0, 1)),
            "wsrc": wsrc, "vecs": vecs, "gatew": gatew, "cst": cst,
        })
    nc = build(SEQ, NS, LS, PAST)
    res = run_bass_kernel_spmd(nc, in_maps, core_ids=list(range(n_cores)))
    R = res.results
    cat = lambda k: np.stack([np.asarray(R[r][k], dtype=np.float32) for r in range(n_cores)])
    y_p = cat("yp")
    y_s = cat("ys").reshape(n_cores * NS, LS, D)
    k_p = cat("kp").reshape(1, n_cores, SEQ, 8, 64)
    v_p = cat("vp").reshape(1, n_cores, SEQ, 8, 64)
    h_p = cat("hp").reshape(1, n_cores, 512)
    conv_p = cat("convp").reshape(1, n_cores, 3, 512)
    pool_p = cat("poolp").reshape(1, n_cores, 15, D)
    k_s = cat("ks").reshape(1, n_cores * NS, LS, 8, 64)
    v_s = cat("vs").reshape(1, n_cores * NS, LS, 8, 64)
    h_s = cat("hs").reshape(1, n_cores * NS, 512)
    conv_s = cat("convs").reshape(1, n_cores * NS, 3, 512)
    pool_s = cat("pools").reshape(1, n_cores * NS, 15, D)
    if DEBUG:
        run.dbg = {k: np.asarray(v) for k, v in R[0].items() if k.startswith('dbg_')}
    return (y_p, y_s, k_p, v_p, h_p, conv_p, pool_p, k_s, v_s, h_s, conv_s, pool_s)


def kernel(**inputs):
    return run(inputs, 8, 4096, 4, 32, 4096)
```
